# Optimizing a Trainium2 kernel written in Bass

```python
import math
import jax, jax.numpy as jnp
from jax import lax
import numpy as np

D_MODEL = 1024
BATCH = 16
SEQ = 2048
DEPTH = 1

HEAD_DIM = 64
BLOCK = 128
N_META = 16
META_PAD = BLOCK - N_META
WINDOW = 128
ROPE_THETA = 10000.0
RMS_EPS = 1e-6
NEG_INF = -1e30

SWA_Q_HEADS = 16
SWA_KV_HEADS = 4
SWA_GROUP = SWA_Q_HEADS // SWA_KV_HEADS
SWA_WIDTH = SWA_Q_HEADS * HEAD_DIM

DIFF_HEADS = 8
DIFF_V_DIM = 2 * HEAD_DIM
DIFF_WIDTH = DIFF_HEADS * DIFF_V_DIM

D_FF = ((8 * D_MODEL // 3 + 255) // 256) * 256

QA_COLS = SWA_Q_HEADS * HEAD_DIM
KA_COLS = SWA_KV_HEADS * HEAD_DIM
VA_COLS = SWA_KV_HEADS * HEAD_DIM
QB_COLS = DIFF_HEADS * 2 * HEAD_DIM
KB_COLS = DIFF_HEADS * 2 * HEAD_DIM
VB_COLS = DIFF_WIDTH
GATE_COLS = 2 * D_MODEL
IN_COLS = QA_COLS + KA_COLS + VA_COLS + QB_COLS + KB_COLS + VB_COLS + GATE_COLS
SPLITS = (
    QA_COLS,
    QA_COLS + KA_COLS,
    QA_COLS + KA_COLS + VA_COLS,
    QA_COLS + KA_COLS + VA_COLS + QB_COLS,
    QA_COLS + KA_COLS + VA_COLS + QB_COLS + KB_COLS,
    QA_COLS + KA_COLS + VA_COLS + QB_COLS + KB_COLS + VB_COLS,
)

kernel_name = "hybrid_swa_sink_diffattn_gated_encoder"


def rms_norm(x, g):
    xf = x.astype(jnp.float32)
    y = xf * lax.rsqrt(jnp.mean(xf * xf, axis=-1, keepdims=True) + RMS_EPS)
    return (y * g.astype(jnp.float32)).astype(x.dtype)


def rope(x, pos):
    d = x.shape[-1]
    half = d // 2
    inv_freq = ROPE_THETA ** (-jnp.arange(0, d, 2, dtype=jnp.float32) / d)
    ang = pos.astype(jnp.float32)[:, None] * inv_freq[None, :]
    shape = (1, x.shape[1]) + (1,) * (x.ndim - 3) + (half,)
    cos = jnp.cos(ang).reshape(shape)
    sin = jnp.sin(ang).reshape(shape)
    xf = x.astype(jnp.float32)
    x1, x2 = xf[..., :half], xf[..., half:]
    return jnp.concatenate([x1 * cos - x2 * sin, x2 * cos + x1 * sin], axis=-1).astype(x.dtype)


def windowed_gqa_sink_attention(q, k, v, sink, pos, is_real):
    B, Lp = q.shape[0], q.shape[1]
    nb = Lp // BLOCK
    scale = HEAD_DIM ** -0.5
    k_meta = k[:, META_PAD:BLOCK]
    v_meta = v[:, META_PAD:BLOCK]
    padw = ((0, 0), (BLOCK, BLOCK), (0, 0), (0, 0))
    k_p = jnp.pad(k, padw)
    v_p = jnp.pad(v, padw)
    pos_p = jnp.pad(pos, (BLOCK, BLOCK))
    real_p = jnp.pad(is_real, (BLOCK, BLOCK))
    qb = q.reshape(B, nb, BLOCK, SWA_KV_HEADS, SWA_GROUP, HEAD_DIM).transpose(1, 0, 2, 3, 4, 5)
    sink_g = sink.astype(jnp.float32).reshape(SWA_KV_HEADS, SWA_GROUP)

    def one_block(args):
        i, qi = args
        start = i * BLOCK
        kb = lax.dynamic_slice_in_dim(k_p, start, 3 * BLOCK, axis=1)
        vb = lax.dynamic_slice_in_dim(v_p, start, 3 * BLOCK, axis=1)
        kpos = lax.dynamic_slice_in_dim(pos_p, start, 3 * BLOCK)
        kreal = lax.dynamic_slice_in_dim(real_p, start, 3 * BLOCK)
        qpos = lax.dynamic_slice_in_dim(pos, start, BLOCK)
        mask = kreal[None, :] & (jnp.abs(qpos[:, None] - kpos[None, :]) <= WINDOW)
        s_band = jnp.einsum('bqhgd,bkhd->bhgqk', qi, kb).astype(jnp.float32) * scale
        s_band = jnp.where(mask, s_band, NEG_INF)
        s_meta = jnp.einsum('bqhgd,bmhd->bhgqm', qi, k_meta).astype(jnp.float32) * scale
        s_sink = jnp.broadcast_to(sink_g[None, :, :, None, None], s_meta.shape[:-1] + (1,))
        p = jax.nn.softmax(jnp.concatenate([s_band, s_meta, s_sink], axis=-1), axis=-1)
        p_band = p[..., :3 * BLOCK].astype(v.dtype)
        p_meta = p[..., 3 * BLOCK:3 * BLOCK + N_META].astype(v.dtype)
        return (jnp.einsum('bhgqk,bkhd->bqhgd', p_band, vb)
                + jnp.einsum('bhgqm,bmhd->bqhgd', p_meta, v_meta))

    o = lax.map(one_block, (jnp.arange(nb), qb))
    return o.transpose(1, 0, 2, 3, 4, 5).reshape(B, Lp, SWA_WIDTH)


def differential_attention(q, k, v, lam, lambda_init, subln_gain, is_key):
    B, Lp = q.shape[0], q.shape[1]
    nb = Lp // BLOCK
    scale = HEAD_DIM ** -0.5
    key_bias = jnp.where(is_key, 0.0, NEG_INF).astype(jnp.float32)
    qb = q.reshape(B, nb, BLOCK, DIFF_HEADS, 2, HEAD_DIM).transpose(1, 0, 2, 3, 4, 5)

    def one_block(qi):
        s = jnp.einsum('bqhcd,bkhcd->bhcqk', qi, k).astype(jnp.float32) * scale + key_bias
        p = jax.nn.softmax(s, axis=-1)
        a = (p[:, :, 0] - lam * p[:, :, 1]).astype(v.dtype)
        return jnp.einsum('bhqk,bkhe->bqhe', a, v)

    o = lax.map(one_block, qb)
    o = o.transpose(1, 0, 2, 3, 4).reshape(B, Lp, DIFF_HEADS, DIFF_V_DIM)
    o = rms_norm(o, subln_gain) * (1.0 - lambda_init)
    return o.reshape(B, Lp, DIFF_WIDTH).astype(v.dtype)


def setup_inputs(seed: int = 0) -> dict:
    key = jax.random.key(seed)
    ks = jax.random.split(key, 20)
    f32 = jnp.float32

    def nrm(k, shape, scale):
        return jax.random.normal(k, shape, f32) * scale

    def gain(k, shape):
        return 1.0 + 0.02 * jax.random.normal(k, shape, f32)

    return {
        "x": nrm(ks[0], (BATCH, SEQ, D_MODEL), 1.0),
        "meta_tokens": nrm(ks[1], (N_META, D_MODEL), 1.0),
        "pre_mix_gain": gain(ks[2], (DEPTH, D_MODEL)),
        "w_in": nrm(ks[3], (DEPTH, D_MODEL, IN_COLS), D_MODEL ** -0.5),
        "b_gate": nrm(ks[4], (DEPTH, GATE_COLS), 0.1),
        "attn_sink": nrm(ks[5], (DEPTH, SWA_Q_HEADS), 0.5),
        "lambda_q1": nrm(ks[6], (DEPTH, HEAD_DIM), 0.1),
        "lambda_k1": nrm(ks[7], (DEPTH, HEAD_DIM), 0.1),
        "lambda_q2": nrm(ks[8], (DEPTH, HEAD_DIM), 0.1),
        "lambda_k2": nrm(ks[9], (DEPTH, HEAD_DIM), 0.1),
        "diff_subln_gain": gain(ks[10], (DEPTH, DIFF_V_DIM)),
        "w_branch_swa": nrm(ks[11], (DEPTH, SWA_WIDTH, D_MODEL), SWA_WIDTH ** -0.5),
        "w_branch_diff": nrm(ks[12], (DEPTH, DIFF_WIDTH, D_MODEL), DIFF_WIDTH ** -0.5),
        "w_out": nrm(ks[13], (DEPTH, D_MODEL, D_MODEL), D_MODEL ** -0.5),
        "post_mix_gain": gain(ks[14], (DEPTH, D_MODEL)),
        "pre_ffn_gain": gain(ks[15], (DEPTH, D_MODEL)),
        "w_ffn_in": nrm(ks[16], (DEPTH, D_MODEL, 2 * D_FF), D_MODEL ** -0.5),
        "w_ffn_out": nrm(ks[17], (DEPTH, D_FF, D_MODEL), D_FF ** -0.5),
        "post_ffn_gain": gain(ks[18], (DEPTH, D_MODEL)),
    }


def reference(x, meta_tokens, pre_mix_gain, w_in, b_gate, attn_sink, lambda_q1, lambda_k1,
              lambda_q2, lambda_k2, diff_subln_gain, w_branch_swa, w_branch_diff, w_out,
              post_mix_gain, pre_ffn_gain, w_ffn_in, w_ffn_out, post_ffn_gain):
    B = x.shape[0]
    filler = jnp.zeros((B, META_PAD, D_MODEL), x.dtype)
    meta = jnp.broadcast_to(meta_tokens[None].astype(x.dtype), (B, N_META, D_MODEL))
    h = jnp.concatenate([filler, meta, x], axis=1)
    Lp = h.shape[1]
    pos = jnp.arange(Lp, dtype=jnp.int32) - META_PAD
    is_real = pos >= N_META
    is_key = pos >= 0

    for l in range(DEPTH):
        lambda_init = 0.8 - 0.6 * math.exp(-0.3 * l)
        u = rms_norm(h, pre_mix_gain[l])
        proj = u @ w_in[l]
        qa, ka, va, qb, kb, vb, g = jnp.split(proj, SPLITS, axis=-1)
        qa = rope(qa.reshape(B, Lp, SWA_Q_HEADS, HEAD_DIM), pos)
        ka = rope(ka.reshape(B, Lp, SWA_KV_HEADS, HEAD_DIM), pos)
        va = va.reshape(B, Lp, SWA_KV_HEADS, HEAD_DIM)
        qb = rope(qb.reshape(B, Lp, DIFF_HEADS, 2, HEAD_DIM), pos)
        kb = rope(kb.reshape(B, Lp, DIFF_HEADS, 2, HEAD_DIM), pos)
        vb = vb.reshape(B, Lp, DIFF_HEADS, DIFF_V_DIM)

        o_swa = windowed_gqa_sink_attention(qa, ka, va, attn_sink[l], pos, is_real)
        lam = (jnp.exp(jnp.sum(lambda_q1[l].astype(jnp.float32) * lambda_k1[l].astype(jnp.float32)))
               - jnp.exp(jnp.sum(lambda_q2[l].astype(jnp.float32) * lambda_k2[l].astype(jnp.float32)))
               + lambda_init)
        o_diff = differential_attention(qb, kb, vb, lam, lambda_init, diff_subln_gain[l], is_key)

        gates = jax.nn.sigmoid(g + b_gate[l])
        g_swa, g_diff = gates[..., :D_MODEL], gates[..., D_MODEL:]
        merged = g_swa * (o_swa @ w_branch_swa[l]) + g_diff * (o_diff @ w_branch_diff[l])
        h = h + rms_norm(merged @ w_out[l], post_mix_gain[l])

        u = rms_norm(h, pre_ffn_gain[l])
        gate_up = u @ w_ffn_in[l]
        f = (jax.nn.silu(gate_up[..., :D_FF]) * gate_up[..., D_FF:]) @ w_ffn_out[l]
        h = h + rms_norm(f, post_ffn_gain[l])

    return h[:, BLOCK:]
```

```python
import numpy as np
import ml_dtypes
from contextlib import ExitStack
import concourse.bass as bass
import concourse.mybir as mybir
from concourse.bass_utils import run_bass_kernel_spmd

F32 = mybir.dt.float32
BF16 = mybir.dt.bfloat16
AF = mybir.ActivationFunctionType
ALU = mybir.AluOpType
AX = mybir.AxisListType

EPOCH = 16000
N_DMA_SEMS = 14


class Buf:
    __slots__ = ("name", "writer", "readers", "dma_readers")

    def __init__(self, name):
        self.name = name
        self.writer = None
        self.readers = {}
        self.dma_readers = []


class Instr:
    __slots__ = ("eng", "fn", "deps", "is_dma", "seq", "needs_inc", "sem", "val", "dma_slot")

    def __init__(self, eng, fn, is_dma):
        self.eng = eng
        self.fn = fn
        self.deps = []
        self.is_dma = is_dma
        self.seq = -1
        self.needs_inc = False
        self.sem = None
        self.val = 0
        self.dma_slot = -1


ENGS = ("pe", "act", "dve", "pool", "sp")


def CALL(method, *args, **kwargs):
    return (method, args, kwargs)


class Rec:
    def __init__(self, nc):
        self.nc = nc
        self.streams = {e: [] for e in ENGS}
        self.dma_rr = {e: 0 for e in ENGS}
        self.dma_last = {e: [None] * N_DMA_SEMS for e in ENGS}
        self.n_instr = 0

    def _add(self, eng, fn, reads, writes, is_dma):
        ins = Instr(eng, fn, is_dma)
        deps = {}
        for b in reads:
            w = b.writer
            if w is not None:
                deps[id(w)] = w
        for b in writes:
            w = b.writer
            if w is not None:
                deps[id(w)] = w
            for r in b.readers.values():
                deps[id(r)] = r
            for r in b.dma_readers:
                deps[id(r)] = r
        if is_dma:
            slot = self.dma_rr[eng]
            self.dma_rr[eng] = (slot + 1) % N_DMA_SEMS
            prev = self.dma_last[eng][slot]
            if prev is not None:
                deps[id(prev)] = prev
            self.dma_last[eng][slot] = ins
            ins.dma_slot = slot
        for d in deps.values():
            if d is ins:
                continue
            if not is_dma and not d.is_dma and d.eng == eng:
                if eng == "pe":
                    continue
                wrote = False
                for b in reads:
                    if b.writer is d:
                        wrote = True
                for b in writes:
                    if b.writer is d:
                        wrote = True
                if not wrote:
                    continue
            ins.deps.append(d)
            d.needs_inc = True
        for b in reads:
            if is_dma:
                b.dma_readers.append(ins)
            else:
                b.readers[eng] = ins
        for b in writes:
            b.writer = ins
            b.readers = {}
            b.dma_readers = []
        ins.seq = len(self.streams[eng])
        self.streams[eng].append(ins)
        self.n_instr += 1
        return ins

    def op(self, eng, fn, reads=(), writes=()):
        return self._add(eng, fn, reads, writes, False)

    def dma(self, eng, fn, reads=(), writes=()):
        ins = self._add(eng, fn, reads, writes, True)
        ins.needs_inc = True
        return ins

    def finalize_and_emit(self):
        nc = self.nc
        self.sems = {}
        for e in ENGS:
            cnt = 0
            cur = None
            for ins in self.streams[e]:
                if ins.is_dma or not ins.needs_inc:
                    continue
                if cnt % EPOCH == 0:
                    cur = nc.alloc_semaphore(f"s_{e}_{cnt // EPOCH}")
                ins.sem = cur
                ins.val = cnt % EPOCH + 1
                cnt += 1
        for e in ENGS:
            if self.dma_rr[e] == 0 and self.dma_last[e][0] is None:
                continue
            sems = [nc.alloc_semaphore(f"d_{e}_{i}") for i in range(N_DMA_SEMS)]
            counts = [0] * N_DMA_SEMS
            for ins in self.streams[e]:
                if ins.is_dma:
                    counts[ins.dma_slot] += 16
                    ins.sem = sems[ins.dma_slot]
                    ins.val = counts[ins.dma_slot]
        self.n_waits = 0

        def replay(ename, eobj):
            waited_seq = {}
            waited_dma = {}
            for ins in self.streams[ename]:
                waits = []
                for d in ins.deps:
                    if d.is_dma:
                        k = id(d.sem)
                        if waited_dma.get(k, 0) >= d.val:
                            continue
                        waited_dma[k] = d.val
                    else:
                        if waited_seq.get(d.eng, -1) >= d.seq:
                            continue
                        waited_seq[d.eng] = d.seq
                    waits.append((d.sem, d.val))
                best = {}
                for (sm, v) in waits:
                    k = id(sm)
                    if k not in best or best[k][1] < v:
                        best[k] = (sm, v)
                waits = list(best.values())
                if ins.fn is None:
                    for (sm, v) in waits:
                        eobj.wait_ge(sm, v)
                        self.n_waits += 1
                    continue
                for (sm, v) in waits[1:]:
                    eobj.wait_ge(sm, v)
                    self.n_waits += 1
                m, a, kw = ins.fn
                bi = getattr(eobj, m)(*a, **kw)
                if waits:
                    bi._wait_ge(waits[0][0], waits[0][1])
                if ins.is_dma:
                    bi.then_inc(ins.sem, 16)
                elif ins.needs_inc:
                    bi.then_inc(ins.sem, 1)

        with nc.Block() as block:
            @block.tensor
            def _(t):
                replay("pe", t)

            @block.scalar
            def _(a):
                replay("act", a)

            @block.vector
            def _(v):
                replay("dve", v)

            @block.gpsimd
            def _(g):
                replay("pool", g)

            @block.sync
            def _(s):
                replay("sp", s)


D = 1024
SEQ = 2048
NMETA = 16
TOK = SEQ + NMETA
DFF = 2816
NJ = DFF // 128
IN_COLS = 6656
QA0, KA0, VA0, QB0, KB0, VB0, G0 = 0, 1024, 1280, 1536, 2560, 3584, 4608
RMS_EPS = 1e-6
A0 = 0
B0 = A0 + 8 * TOK
C0 = B0 + 8 * SEQ
D0 = C0 + 14336
ARENA = D0 + 8 * SEQ
GRAN = 128
MASKV = -30000.0


def host_constants():
    pos = np.concatenate([np.arange(SEQ) + NMETA, np.arange(NMETA)]).astype(np.float64)
    inv_freq = 10000.0 ** (-(np.arange(0, 64, 2, dtype=np.float64)) / 64.0)
    p = np.arange(128)
    ang = inv_freq[p % 32][:, None] * pos[None, :]
    cosT = np.cos(ang)
    sgn = np.where((p % 64) < 32, 1.0, -1.0)[:, None]
    sinP = np.sin(ang) * sgn
    b = np.arange(128)[:, None]
    a = np.arange(128)[None, :]
    lo = np.where(a <= b, 0.0, MASKV)
    hi = np.where(b <= a, 0.0, MASKV)
    maskb = np.concatenate([np.tile(lo, (1, 4)), np.tile(hi, (1, 4))], axis=1)
    ident = np.eye(128)
    prot = np.zeros((128, 128))
    prot[p ^ 32, p] = 1.0
    ones = np.ones((128, 128))
    onesd = np.full((128, 128), 1.0 / 128.0)
    mats = np.concatenate([ident, prot, ones, onesd], axis=1)
    bf = ml_dtypes.bfloat16
    return {"c_cos": cosT.astype(np.float32).astype(bf), "c_sin": sinP.astype(np.float32).astype(bf),
            "c_mask": maskb.astype(np.float32).astype(bf), "c_mats": mats.astype(np.float32).astype(bf)}


def build_program(nseq=2, stop_after=99, debug=False):
    nc = bass.Bass("TRN2", target_bir_lowering=False)
    R = Rec(nc)

    def din(name, shape, dt=F32):
        return nc.dram_tensor(name, list(shape), dt, kind="ExternalInput").ap()

    x_d = din("x", [nseq, SEQ, D])
    meta_d = din("meta", [NMETA, D])
    w_in_d = din("w_in", [D, IN_COLS]).rearrange("(kc p) n -> p kc n", p=128)
    w_bs_d = din("w_bs", [D, D]).rearrange("(kc p) n -> p kc n", p=128)
    w_bd_d = din("w_bd", [D, D]).rearrange("(kc p) n -> p kc n", p=128)
    w_out_d = din("w_out", [D, D]).rearrange("(kc p) n -> p kc n", p=128)
    w_fi_d = din("w_fi", [D, 2 * DFF]).rearrange("(kc p) n -> p kc n", p=128)
    w_fo_d = din("w_fo", [DFF, D]).rearrange("(j p) n -> p j n", p=128)
    g1_d = din("g1_fm", [128, 8])
    g3_d = din("g3_fm", [128, 8])
    g2_d = din("g_post", [1, D])
    g4_d = din("g_postffn", [1, D])
    bg_d = din("bgate_fm", [128, 16])
    sink_d = din("sink_fm", [128, 8])
    lamp_d = din("lam_p", [4, 64])
    subln_d = din("subln", [128, 1])
    cos_d = din("c_cos", [128, TOK], BF16)
    sin_d = din("c_sin", [128, TOK], BF16)
    mask_d = din("c_mask", [128, 1024], BF16)
    mats_d = din("c_mats", [128, 512], BF16)
    out_d = nc.dram_tensor("out", [nseq, SEQ, D], F32, kind="ExternalOutput").ap()

    arena = nc.alloc_sbuf_tensor("arena", [128, ARENA], BF16)
    agran = [Buf(f"ar{i}") for i in range((ARENA + GRAN - 1) // GRAN)]

    class AT:
        def __init__(self, base, Rr, C, dt=BF16):
            self.base, self.R, self.C, self.dt = base, Rr, C, dt
            self.es = 2 if dt == F32 else 1
            assert base + Rr * C * self.es <= ARENA
            v = arena[:, base:base + Rr * C * self.es]
            if dt == F32:
                v = v.bitcast(F32)
            self.v = v.rearrange("p (r c) -> p r c", r=Rr)

        def bufs(self, r0, r1, c0, c1):
            out = []
            for r in range(r0, r1):
                s = self.base + (r * self.C + c0) * self.es
                e = self.base + (r * self.C + c1) * self.es
                out.extend(agran[s // GRAN:(e - 1) // GRAN + 1])
            return out

    def sb(name, shape, dt):
        return nc.alloc_sbuf_tensor(name, list(shape), dt)

    cos_t = sb("cos_t", [128, TOK], BF16)
    sin_t = sb("sin_t", [128, TOK], BF16)
    mask_t = sb("mask_t", [128, 1024], BF16)
    mats_t = sb("mats_t", [128, 512], BF16)
    ident = mats_t[:, 0:128]
    prot = mats_t[:, 128:256]
    ones = mats_t[:, 256:384]
    onesd = mats_t[:, 384:512]
    g1_t = sb("g1_t", [128, 8], F32)
    g3_t = sb("g3_t", [128, 8], F32)
    g2_t = sb("g2_t", [128, D], F32)
    g4_t = sb("g4_t", [128, D], F32)
    bg_t = sb("bg_t", [128, 16], F32)
    es_t = sb("es_t", [128, 8], F32)
    lamd = sb("lamd", [128, 4], F32)
    nlam = sb("nlam", [128, 1], F32)
    g08 = sb("g08", [128, 1], F32)
    cB = Buf("consts")
    msffn_t = sb("msffn", [128, 8, 4], F32)
    msffn_b = Buf("msffn")

    class TPool:
        def __init__(self, name, n, shape, dt):
            self.t = [sb(f"{name}{i}", shape, dt) for i in range(n)]
            self.b = [Buf(f"{name}{i}") for i in range(n)]
            self.i = 0

        def next(self):
            i = self.i
            self.i = (i + 1) % len(self.t)
            return self.t[i], self.b[i]

    xs_p = TPool("xs", 2, [128, D], F32)
    xn_p = TPool("xn", 2, [128, D], BF16)
    junk_p = TPool("junk", 1, [128, D], BF16)
    st_p = TPool("st", 8, [128, 4], F32)
    e_p = TPool("E", 7, [128, 512], BF16)
    es_p = TPool("Esum", 3, [128, 512], BF16)
    import os as _os
    PAIR_S = int(_os.environ.get("PAIR_S", "1"))
    ra_p = e_p
    rb_p = e_p
    f_p = TPool("ft", 3, [128, 512], F32)
    w_p = TPool("wsl", 4, [128, 4096], BF16)
    psum = [nc.alloc_psum_tensor(f"ps{i}", [128, 512], F32) for i in range(8)]
    psb = [Buf(f"ps{i}") for i in range(8)]

    class Rot:
        def __init__(self, idxs):
            self.idxs, self.i = list(idxs), 0

        def next(self):
            k = self.idxs[self.i]
            self.i = (self.i + 1) % len(self.idxs)
            return psum[k], psb[k]

    dbg_out = {}

    def dbg_dump(name, ap, bufs, shape, dt=BF16):
        if not debug:
            return
        d = nc.dram_tensor("dbg_" + name, list(shape), dt, kind="ExternalOutput").ap()
        dbg_out[name] = R.dma("sp", CALL("dma_start", out=d, in_=ap), reads=bufs, writes=[Buf("dbgsink")])

    _lt, _lb = f_p.next()
    lamp_t = _lt[:, 0:256].rearrange("p (a b) -> p a b", a=4)
    lamtmp = _lt[:, 256:384].rearrange("p (a b) -> p a b", a=2)
    def cload(dst, src):
        R.dma("sp", CALL("dma_start", out=dst, in_=src), writes=[cB, _lb])

    cload(cos_t[:], cos_d)
    cload(sin_t[:], sin_d)
    cload(mask_t[:], mask_d)
    cload(mats_t[:], mats_d)
    cload(g1_t[:], g1_d)
    cload(g3_t[:], g3_d)
    cload(g2_t[:], g2_d.partition_broadcast(128))
    cload(g4_t[:], g4_d.partition_broadcast(128))
    cload(bg_t[:], bg_d)
    cload(es_t[:], sink_d)
    for i in range(4):
        cload(lamp_t[:, i, :], lamp_d[i:i + 1, :].partition_broadcast(128))
    cload(g08[:], subln_d)
    R.op("act", CALL("activation", out=es_t[:], in_=es_t[:], func=AF.Exp), reads=[cB], writes=[cB])
    R.op("dve", CALL("tensor_tensor", out=lamtmp[:, 0, :], in0=lamp_t[:, 0, :], in1=lamp_t[:, 1, :], op=ALU.mult),
         reads=[cB, _lb], writes=[cB, _lb])
    R.op("dve", CALL("tensor_tensor", out=lamtmp[:, 1, :], in0=lamp_t[:, 2, :], in1=lamp_t[:, 3, :], op=ALU.mult),
         reads=[cB, _lb], writes=[cB, _lb])
    R.op("dve", CALL("reduce_sum", out=lamd[:, 0:2], in_=lamtmp, axis=AX.X), reads=[cB, _lb], writes=[cB, _lb])
    R.op("act", CALL("activation", out=lamd[:, 2:4], in_=lamd[:, 0:2], func=AF.Exp), reads=[cB], writes=[cB])
    R.op("dve", CALL("tensor_tensor", out=nlam[:], in0=lamd[:, 3:4], in1=lamd[:, 2:3], op=ALU.subtract),
         reads=[cB], writes=[cB])
    R.op("dve", CALL("tensor_scalar", out=nlam[:], in0=nlam[:], scalar1=-0.2, scalar2=None, op0=ALU.add),
         reads=[cB], writes=[cB])
    R.op("dve", CALL("tensor_scalar", out=g08[:], in0=g08[:], scalar1=0.8, scalar2=None, op0=ALU.mult),
         reads=[cB], writes=[cB])

    def rstd_from_ms(ms_ap, st_t, st_b, extra_reads=()):
        R.op("act", CALL("activation", out=st_t[:, 1:2], in_=ms_ap, func=AF.Ln, bias=RMS_EPS, scale=1.0),
             reads=[st_b] + list(extra_reads), writes=[st_b])
        R.op("act", CALL("activation", out=st_t[:, 2:3], in_=st_t[:, 1:2], func=AF.Exp, scale=-0.5),
             reads=[st_b], writes=[st_b])
        return st_t[:, 2:3]

    def load_w(src3, c0, ncols, wt, wb, dst0=0, kc_n=8, dst_view=None):
        v = dst_view if dst_view is not None else wt[:].rearrange("p (k c) -> p k c", k=8)
        R.dma("pool", CALL("dma_start", out=v[:, 0:kc_n, dst0:dst0 + ncols], in_=src3[:, 0:kc_n, c0:c0 + ncols]),
              writes=[wb])

    def norm_transpose(src_tile, src_b, rows, gain_t, dst, c0, rot):
        jt, jb = junk_p.next()
        st, stb = st_p.next()
        R.op("act", CALL("activation", out=jt[0:rows, :], in_=src_tile[0:rows, :], func=AF.Square, scale=1.0 / 32.0,
                                           accum_out=st[0:rows, 0:1]),
             reads=src_b, writes=[jb, stb])
        R.op("act", CALL("activation", out=st[0:rows, 1:2], in_=st[0:rows, 0:1], func=AF.Ln, bias=RMS_EPS, scale=1.0),
             reads=[stb], writes=[stb])
        R.op("act", CALL("activation", out=st[0:rows, 2:3], in_=st[0:rows, 1:2], func=AF.Exp, scale=-0.5),
             reads=[stb], writes=[stb])
        xt, xb = xn_p.next()
        R.op("dve", CALL("tensor_scalar", out=xt[0:rows, :], in0=src_tile[0:rows, :], scalar1=st[0:rows, 2:3],
                                              scalar2=None, op0=ALU.mult),
             reads=list(src_b) + [stb], writes=[xb])
        bank, bb = rot.next()
        pbf = bank[:].bitcast(BF16)
        for k in range(8):
            R.op("pe", CALL("transpose", pbf[:, k * 128:k * 128 + rows], xt[0:rows, k * 128:(k + 1) * 128],
                                                  ident[0:rows, 0:rows]),
                 reads=[xb, cB], writes=[bb])
        pv = pbf.rearrange("p (k t) -> p k t", k=8)
        gb = gain_t[:, 0:8].unsqueeze(2).to_broadcast([128, 8, rows])
        R.op("dve", CALL("tensor_tensor", out=dst.v[:, 0:8, c0:c0 + rows], in0=pv[:, :, 0:rows], in1=gb, op=ALU.mult),
             reads=[bb, cB], writes=dst.bufs(0, 8, c0, c0 + rows))

    def rope_evac(bank, bb, n, tok0, dsts, scale, rot2):
        at, ab = ra_p.next()
        bt, btb = rb_p.next()
        R.op("dve", CALL("tensor_tensor", out=at[:, 0:n], in0=bank[:, 0:n], in1=cos_t[:, tok0:tok0 + n], op=ALU.mult),
             reads=[bb, cB], writes=[ab])
        R.op("dve", CALL("tensor_tensor", out=bt[:, 0:n], in0=bank[:, 0:n], in1=sin_t[:, tok0:tok0 + n], op=ALU.mult),
             reads=[bb, cB], writes=[btb])

        def stage2():
            b2, b2b = rot2.next()
            R.op("pe", CALL("matmul", b2[:, 0:n], lhsT=ident, rhs=at[:, 0:n], start=True, stop=False),
                 reads=[ab, cB], writes=[b2b])
            R.op("pe", CALL("matmul", b2[:, 0:n], lhsT=prot, rhs=bt[:, 0:n], start=False, stop=True),
                 reads=[btb, cB], writes=[b2b])
            for (p0, p1, dst_ap, dst_bufs) in dsts:
                R.op("act", CALL("activation", out=dst_ap, in_=b2[p0:p1, 0:n], func=AF.Copy, scale=scale),
                     reads=[b2b], writes=dst_bufs)
        return stage2

    TT5 = [(0, 512), (512, 512), (1024, 512), (1536, 512), (2048, 16)]

    def proj_fm(wt, wb, wcol, uT, n_tiles, rot, consume):
        wv = wt[:].rearrange("p (k c) -> p k c", k=8)
        for (tok0, n) in TT5[:n_tiles]:
            bank, bb = rot.next()
            for k in range(8):
                R.op("pe", CALL("matmul",
                    bank[:, 0:n], lhsT=wv[:, k, wcol:wcol + 128], rhs=uT.v[:, k, tok0:tok0 + n],
                    start=(k == 0), stop=(k == 7)),
                    reads=[wb] + uT.bufs(k, k + 1, tok0, tok0 + n), writes=[bb])
            consume(bank, bb, tok0, n)

    pending = []

    def flush_pending(keep=0):
        while len(pending) > keep:
            pending.pop(0)()

    for s in range(nseq):
        uT = AT(A0, 8, TOK)
        qaT = AT(B0, 8, SEQ)
        mT = AT(B0, 8, SEQ)
        kaT = AT(C0, 4, TOK)
        va = AT(C0 + 4 * TOK, 17, 256)
        oT = AT(D0, 8, SEQ)

        rot1 = Rot([0, 1])
        for tb in range(17):
            rows = 128 if tb < 16 else NMETA
            xt, xb = xs_p.next()
            src = x_d[s, tb * 128:(tb + 1) * 128, :] if tb < 16 else meta_d
            R.dma("sp", CALL("dma_start", out=xt[0:rows, :], in_=src), writes=[xb])
            norm_transpose(xt, [xb], rows, g1_t, uT, tb * 128, rot1)
        if s == 0:
            dbg_dump("uT", uT.v[:, :, :], uT.bufs(0, 8, 0, TOK), [128, 8, TOK])
        if stop_after <= 1:
            continue

        rotP = Rot([2, 3, 4, 5])
        rotR = Rot([6, 7])
        wq0, wq0b = w_p.next()
        load_w(w_in_d, QA0, 512, wq0, wq0b)
        wq1, wq1b = w_p.next()
        load_w(w_in_d, QA0 + 512, 512, wq1, wq1b)
        wk, wkb = w_p.next()
        wk5 = wk[:].rearrange("p (k g d c) -> p k g d c", k=8, g=4, d=2)
        for kc in range(8):
            for dd in range(2):
                R.dma("pool", CALL("dma_start",
                    out=wk5[:, kc, :, dd, :],
                    in_=w_in_d[:, kc, KA0:KA0 + 256].rearrange("p (g c) -> p g c", g=4)), writes=[wkb])
        wv_, wvb = w_p.next()
        load_w(w_in_d, VA0, 256, wv_, wvb)

        def consume_rope(dst, r, scale):
            def f(bank, bb, tok0, n):
                pending.append(rope_evac(bank, bb, n, tok0, [(0, 128, dst.v[:, r, tok0:tok0 + n], dst.bufs(r, r + 1, tok0, tok0 + n))],
                                         scale, rotR))
                flush_pending(keep=1)
            return f

        def consume_rope_qz(qz):
            def f(bank, bb, tok0, n):
                dsts = [(0, 64, qz.v[0:64, 0, tok0:tok0 + n], qz.bufs(0, 1, tok0, tok0 + n)),
                        (64, 128, qz.v[64:128, 1, tok0:tok0 + n], qz.bufs(1, 2, tok0, tok0 + n))]
                pending.append(rope_evac(bank, bb, n, tok0, dsts, 0.125, rotR))
                flush_pending(keep=1)
            return f

        for c in range(8):
            wt, wb = (wq0, wq0b) if c < 4 else (wq1, wq1b)
            proj_fm(wt, wb, (c % 4) * 128, uT, 4, rotP, consume_rope(qaT, c, 0.125))
        for g in range(4):
            proj_fm(wk, wkb, g * 128, uT, 5, rotP, consume_rope(kaT, g, 1.0))
        flush_pending()
        wvv = wv_[:].rearrange("p (k c) -> p k c", k=8)
        for tb in range(17):
            rows = 128 if tb < 16 else NMETA
            bank, bb = rotP.next()
            for k in range(8):
                R.op("pe", CALL("matmul",
                    bank[0:rows, 0:256], lhsT=uT.v[:, k, tb * 128:tb * 128 + rows], rhs=wvv[:, k, 0:256],
                    start=(k == 0), stop=(k == 7)),
                    reads=[wvb] + uT.bufs(k, k + 1, tb * 128, tb * 128 + rows), writes=[bb])
            R.op("act", CALL("activation", out=va.v[0:rows, tb, :], in_=bank[0:rows, 0:256],
                                                                          func=AF.Copy),
                 reads=[bb], writes=va.bufs(tb, tb + 1, 0, 256))
        if s == 0:
            dbg_dump("qaT", qaT.v[:, :, :], qaT.bufs(0, 8, 0, SEQ), [128, 8, SEQ])
            dbg_dump("kaT", kaT.v[:, :, :], kaT.bufs(0, 4, 0, TOK), [128, 4, TOK])
            dbg_dump("va", va.v[:, 0:16, :], va.bufs(0, 16, 0, 256), [128, 16, 256])
        if stop_after <= 2:
            continue

        rotS = Rot([0, 1, 2, 3, 4, 5])
        rotO = Rot([6, 7])
        items = []
        for g in range(4):
            for i in range(16):
                kbs = []
                if i > 0:
                    kbs.append((i - 1, 0))
                kbs.append((i, None))
                if i < 15:
                    kbs.append((i + 1, 1))
                kbs.append((16, None))
                for n_, (kb, mk) in enumerate(kbs):
                    items.append((g, i, kb, mk, n_ == 0, n_ == len(kbs) - 1))
        state = {}

        def swa_a(it):
            g, i, kb, mk, first, last = it
            nk = 128 if kb < 16 else NMETA
            k0 = kb * 128
            bE, bEb = rotS.next()
            bO, bOb = rotS.next()
            q0, q1 = i * 128, (i + 1) * 128
            R.op("pe", CALL("matmul", bE[0:nk, 0:256], lhsT=kaT.v[0:64, g, k0:k0 + nk],
                                          rhs=qaT.v[0:64, 2 * g:2 * g + 2, q0:q1], start=True, stop=(mk is None)),
                 reads=kaT.bufs(g, g + 1, k0, k0 + nk) + qaT.bufs(2 * g, 2 * g + 2, q0, q1), writes=[bEb])
            R.op("pe", CALL("matmul", bO[0:nk, 0:256], lhsT=kaT.v[64:128, g, k0:k0 + nk],
                                          rhs=qaT.v[64:128, 2 * g:2 * g + 2, q0:q1], start=True, stop=(mk is None)),
                 reads=kaT.bufs(g, g + 1, k0, k0 + nk) + qaT.bufs(2 * g, 2 * g + 2, q0, q1), writes=[bOb])
            if mk is not None:
                R.op("pe", CALL("matmul", bE[:, 0:256], lhsT=ident, rhs=mask_t[:, mk * 512:mk * 512 + 256],
                                              start=False, stop=True), reads=[cB], writes=[bEb])
                R.op("pe", CALL("matmul", bO[:, 0:256], lhsT=ident, rhs=mask_t[:, mk * 512:mk * 512 + 256],
                                              start=False, stop=True), reads=[cB], writes=[bOb])
            et, eb = e_p.next()
            R.op("act", CALL("activation", out=et[0:nk, 0:256], in_=bE[0:nk, 0:256], func=AF.Exp), reads=[bEb], writes=[eb])
            R.op("act", CALL("activation", out=et[0:nk, 256:512], in_=bO[0:nk, 0:256], func=AF.Exp), reads=[bOb], writes=[eb])
            state[it] = (et, eb, nk)

        import os
        SWA_DBG = int(os.environ.get("SWA_DBG", "9"))

        def swa_b(it):
            g, i, kb, mk, first, last = it
            et, eb, nk = state.pop(it)
            if SWA_DBG <= 1:
                return
            if first:
                state["acc"] = rotO.next()
            acc, accb = state["acc"]
            vrd = va.bufs(kb, kb + 1, g * 64, (g + 1) * 64)
            lv = va.v[0:nk, kb, g * 64:(g + 1) * 64]
            R.op("pe", CALL("matmul", acc[0:64, 0:256], lhsT=lv, rhs=et[0:nk, 0:256], start=first, stop=False),
                 reads=vrd + [eb], writes=[accb])
            R.op("pe", CALL("matmul", acc[64:128, 0:256], lhsT=lv, rhs=et[0:nk, 256:512], start=first, stop=False),
                 reads=vrd + [eb], writes=[accb])
            R.op("pe", CALL("matmul", acc[0:64, 256:512], lhsT=ones[0:nk, 0:64], rhs=et[0:nk, 0:256], start=False,
                                          stop=last), reads=[cB, eb], writes=[accb])
            R.op("pe", CALL("matmul", acc[64:128, 256:512], lhsT=ones[0:nk, 0:64], rhs=et[0:nk, 256:512], start=False,
                                          stop=last), reads=[cB, eb], writes=[accb])
            if last and SWA_DBG > 2:
                dt_, db_ = f_p.next()
                esb = es_t[:, 2 * g:2 * g + 2].unsqueeze(2).to_broadcast([128, 2, 128])
                d3 = dt_[:, 0:256].rearrange("p (j q) -> p j q", j=2)
                R.op("dve", CALL("tensor_tensor", out=d3, in0=acc[:, 256:512].rearrange("p (j q) -> p j q", j=2),
                                                      in1=esb, op=ALU.add), reads=[accb, cB], writes=[db_])
                R.op("dve", CALL("reciprocal", out=dt_[:, 256:512], in_=dt_[:, 0:256]), reads=[db_], writes=[db_])
                q0, q1 = i * 128, (i + 1) * 128
                R.op("dve", CALL("tensor_tensor", out=oT.v[:, 2 * g:2 * g + 2, q0:q1],
                                                      in0=acc[:, 0:256].rearrange("p (j q) -> p j q", j=2),
                                                      in1=dt_[:, 256:512].rearrange("p (j q) -> p j q", j=2), op=ALU.mult),
                     reads=[accb, db_], writes=oT.bufs(2 * g, 2 * g + 2, q0, q1))

        DEPTH = 2
        for n_ in range(len(items) + DEPTH):
            if n_ < len(items):
                swa_a(items[n_])
            if n_ - DEPTH >= 0:
                swa_b(items[n_ - DEPTH])
        if s == 0:
            dbg_dump("oswaT", oT.v[:, :, :], oT.bufs(0, 8, 0, SEQ), [128, 8, SEQ])
        if stop_after <= 3:
            continue

        def merge_phase(w_br_d, gcol0, bcol0, accumulate):
            rotM = Rot([0, 1, 2, 3, 4, 5, 6, 7])
            slots = []
            for half in range(2):
                wa, wab = w_p.next()
                load_w(w_br_d, half * 512, 512, wa, wab)
                wg, wgb = w_p.next()
                load_w(w_in_d, gcol0 + half * 512, 512, wg, wgb)
                slots.append((wa, wab, wg, wgb))
            for m in range(8):
                wa, wab, wg, wgb = slots[m // 4]
                wav = wa[:].rearrange("p (k c) -> p k c", k=8)
                wgv = wg[:].rearrange("p (k c) -> p k c", k=8)
                mc = (m % 4) * 128
                for t in range(4):
                    t0 = t * 512
                    bp, bpb = rotM.next()
                    bg, bgb = rotM.next()
                    for k in range(8):
                        R.op("pe", CALL("matmul", bp[:, :], lhsT=wav[:, k, mc:mc + 128],
                                                                    rhs=oT.v[:, k, t0:t0 + 512], start=(k == 0), stop=(k == 7)),
                             reads=[wab] + oT.bufs(k, k + 1, t0, t0 + 512), writes=[bpb])
                    for k in range(8):
                        R.op("pe", CALL("matmul", bg[:, :], lhsT=wgv[:, k, mc:mc + 128],
                                                                    rhs=uT.v[:, k, t0:t0 + 512], start=(k == 0), stop=(k == 7)),
                             reads=[wgb] + uT.bufs(k, k + 1, t0, t0 + 512), writes=[bgb])
                    gt_, gtb = f_p.next()
                    R.op("act", CALL("activation", out=gt_[:, :], in_=bg[:, :], func=AF.Sigmoid,
                                                                      bias=bg_t[:, bcol0 + m:bcol0 + m + 1], scale=1.0),
                         reads=[bgb, cB], writes=[gtb])
                    mb = mT.bufs(m, m + 1, t0, t0 + 512)
                    if not accumulate:
                        R.op("dve", CALL("tensor_tensor", out=mT.v[:, m, t0:t0 + 512], in0=bp[:, :],
                                                                             in1=gt_[:, :], op=ALU.mult),
                             reads=[bpb, gtb], writes=mb)
                    else:
                        R.op("dve", CALL("tensor_tensor", out=gt_[:, :], in0=bp[:, :], in1=gt_[:, :],
                                                                             op=ALU.mult), reads=[bpb, gtb], writes=[gtb])
                        R.op("dve", CALL("tensor_tensor", out=mT.v[:, m, t0:t0 + 512], in0=mT.v[:, m, t0:t0 + 512],
                                                                      in1=gt_[:, :], op=ALU.add), reads=[gtb] + mb, writes=mb)

        merge_phase(w_bs_d, G0, 0, False)
        if s == 0:
            dbg_dump("m1T", mT.v[:, :, :], mT.bufs(0, 8, 0, SEQ), [128, 8, SEQ])
        if stop_after <= 4:
            continue

        rotP = Rot([0, 1, 2])
        rotR = Rot([3])
        for h in range(8):
            base = C0
            qz = AT(base, 2, SEQ)
            kb_ = AT(base + 2 * SEQ, 1, TOK)
            vb = AT(base + 2 * SEQ + TOK, 17, 256)
            hv = (h % 2) * 128
            if h == 0:
                R.op("dve", CALL("memset", qz.v[64:128, 0, :], 0.0), writes=qz.bufs(0, 1, 0, SEQ))
                R.op("dve", CALL("memset", qz.v[0:64, 1, :], 0.0), writes=qz.bufs(1, 2, 0, SEQ))
            wt, wb = w_p.next()
            wv3 = wt[:].rearrange("p (k c) -> p k c", k=8)
            load_w(w_in_d, QB0 + h * 128, 128, wt, wb, dst0=0)
            load_w(w_in_d, KB0 + h * 128, 128, wt, wb, dst0=128)
            if h % 2 == 0:
                load_w(w_in_d, VB0 + h * 128, 256, wt, wb, dst0=256)
            rotP = Rot([0, 1, 2])
            rotR = Rot([7])
            proj_fm(wt, wb, 0, uT, 4, rotP, consume_rope_qz(qz))
            proj_fm(wt, wb, 128, uT, 5, rotP, consume_rope(kb_, 0, 1.0))
            flush_pending()
            for tb in (range(17) if h % 2 == 0 else ()):
                rows = 128 if tb < 16 else NMETA
                bank, bb = rotP.next()
                for k in range(8):
                    R.op("pe", CALL("matmul",
                        bank[0:rows, 0:256], lhsT=uT.v[:, k, tb * 128:tb * 128 + rows], rhs=wv3[:, k, 256:512],
                        start=(k == 0), stop=(k == 7)),
                        reads=[wb] + uT.bufs(k, k + 1, tb * 128, tb * 128 + rows), writes=[bb])
                R.op("dve", CALL("tensor_copy", out=vb.v[0:rows, tb, :], in_=bank[0:rows, 0:256]),
                     reads=[bb], writes=vb.bufs(tb, tb + 1, 0, 256))
            if s == 0 and h == 0:
                dbg_dump("qz0", qz.v[:, :, :], qz.bufs(0, 2, 0, SEQ), [128, 2, SEQ])
                dbg_dump("kb0", kb_.v[:, :, :], kb_.bufs(0, 1, 0, TOK), [128, 1, TOK])
                dbg_dump("vb0", vb.v[:, 0:16, 0:128], vb.bufs(0, 16, 0, 128), [128, 16, 128])
            rotS = Rot([0, 1, 2, 3])
            accs = {0: (4, 5), 1: (6, 7)}
            deferred = []
            ditems = [(qt, c, kbi) for qt in range(4) for c in range(2) for kbi in range(17)]
            dstate = {}

            def dif_a(it):
                qt, c, kbi = it
                nk = 128 if kbi < 16 else NMETA
                k0 = kbi * 128
                bank, bb = rotS.next()
                R.op("pe", CALL("matmul", bank[0:nk, :], lhsT=kb_.v[:, 0, k0:k0 + nk],
                                              rhs=qz.v[:, c, qt * 512:(qt + 1) * 512], start=True, stop=True),
                     reads=kb_.bufs(0, 1, k0, k0 + nk) + qz.bufs(c, c + 1, qt * 512, (qt + 1) * 512), writes=[bb])
                et, eb = e_p.next()
                R.op("act", CALL("activation", out=et[0:nk, :], in_=bank[0:nk, :], func=AF.Exp), reads=[bb], writes=[eb])
                dstate[it] = (et, eb, nk)
                if PAIR_S and kbi < 16 and kbi % 2 == 1:
                    pa, pab, _ = dstate[(qt, c, kbi - 1)]
                    sm, smb = es_p.next()
                    R.op("pool", CALL("tensor_tensor", out=sm[:, :], in0=pa[:, :], in1=et[:, :], op=ALU.add),
                         reads=[pab, eb], writes=[smb])
                    dstate[("sum", qt, c, kbi)] = (sm, smb)

            def dif_b(it):
                qt, c, kbi = it
                et, eb, nk = dstate.pop(it)
                first, last = kbi == 0, kbi == 16
                oi, si = accs[c]
                R.op("pe", CALL("matmul", psum[oi][:, :], lhsT=vb.v[0:nk, kbi, hv:hv + 128], rhs=et[0:nk, :], start=first, stop=last),
                     reads=vb.bufs(kbi, kbi + 1, hv, hv + 128) + [eb], writes=[psb[oi]])
                if not PAIR_S:
                    R.op("pe", CALL("matmul", psum[si][:, :], lhsT=ones[0:nk, :], rhs=et[0:nk, :], start=first, stop=last),
                         reads=[cB, eb], writes=[psb[si]])
                elif kbi == 16:
                    R.op("pe", CALL("matmul", psum[si][:, :], lhsT=ones[0:nk, :], rhs=et[0:nk, :], start=False, stop=True),
                         reads=[cB, eb], writes=[psb[si]])
                elif kbi % 2 == 1:
                    sm, smb = dstate.pop(("sum", qt, c, kbi))
                    R.op("pe", CALL("matmul", psum[si][:, :], lhsT=ones[:, :], rhs=sm[:, :], start=(kbi == 1), stop=False),
                         reads=[cB, smb], writes=[psb[si]])
                if not last:
                    return
                rt, rb = f_p.next()
                R.op("dve", CALL("reciprocal", out=rt[:, :], in_=psum[si][:, :]), reads=[psb[si]], writes=[rb])
                R.op("dve", CALL("tensor_tensor", out=rt[:, :], in0=psum[oi][:, :], in1=rt[:, :], op=ALU.mult),
                     reads=[psb[oi], rb], writes=[rb])
                if c == 0:
                    dstate["t0"] = (rt, rb)
                    return
                t0t, t0b = dstate.pop("t0")
                R.op("dve", CALL("scalar_tensor_tensor", out=rt[:, :], in0=rt[:, :], scalar=nlam[:, 0:1], in1=t0t[:, :],
                                                             op0=ALU.mult, op1=ALU.add), reads=[rb, t0b, cB], writes=[rb])
                sq, sqb = e_p.next()
                R.op("dve", CALL("tensor_tensor", out=sq[:, :], in0=rt[:, :], in1=rt[:, :], op=ALU.mult), reads=[rb], writes=[sqb])

                def tail(qt=qt, rt=rt, rb=rb, t0t=t0t, t0b=t0b, sq=sq, sqb=sqb):
                    ssk, ssb = rotS.next()
                    R.op("pe", CALL("matmul", ssk[:, :], lhsT=onesd, rhs=sq[:, :], start=True, stop=True),
                         reads=[cB, sqb], writes=[ssb])
                    R.op("act", CALL("activation", out=t0t[:, :], in_=ssk[:, :], func=AF.Ln, bias=RMS_EPS, scale=1.0),
                         reads=[ssb], writes=[t0b])
                    R.op("act", CALL("activation", out=t0t[:, :], in_=t0t[:, :], func=AF.Exp, scale=-0.5),
                         reads=[t0b], writes=[t0b])
                    R.op("dve", CALL("scalar_tensor_tensor", out=oT.v[:, h, qt * 512:(qt + 1) * 512], in0=rt[:, :],
                                                                 scalar=g08[:, 0:1], in1=t0t[:, :], op0=ALU.mult, op1=ALU.mult),
                         reads=[rb, t0b, cB], writes=oT.bufs(h, h + 1, qt * 512, (qt + 1) * 512))
                deferred.append([6, tail])

            DD = 3
            for n_ in range(len(ditems) + DD):
                if n_ < len(ditems):
                    dif_a(ditems[n_])
                if n_ - DD >= 0:
                    dif_b(ditems[n_ - DD])
                for dfr in deferred:
                    dfr[0] -= 1
                while deferred and deferred[0][0] <= 0:
                    deferred.pop(0)[1]()
            while deferred:
                deferred.pop(0)[1]()
        if s == 0:
            dbg_dump("odiffT", oT.v[:, :, :], oT.bufs(0, 8, 0, SEQ), [128, 8, SEQ])
        if stop_after <= 5:
            continue
        merge_phase(w_bd_d, G0 + 1024, 8, True)
        if s == 0:
            dbg_dump("mT", mT.v[:, :, :], mT.bufs(0, 8, 0, SEQ), [128, 8, SEQ])
        if stop_after <= 6:
            continue

        for hf in range(2):
            hT = AT(A0, 8, D, F32)
            u2T = AT(C0, 8, 1024)
            aT = AT(C0 + 8192, NJ, 1024)
            fst = AT(C0, 8, 512, F32)
            rotY = Rot([0, 1, 2, 3])
            rotT = Rot([4, 5])
            wo0, wo0b = w_p.next()
            load_w(w_out_d, 0, 512, wo0, wo0b)
            wo1, wo1b = w_p.next()
            load_w(w_out_d, 512, 512, wo1, wo1b)
            wov = [wo0[:].rearrange("p (k c) -> p k c", k=8), wo1[:].rearrange("p (k c) -> p k c", k=8)]
            wob = [wo0b, wo1b]
            nt_q = []
            for tl in range(8):
                tb = hf * 8 + tl
                q0 = tb * 128
                ybank = []
                for ch in range(2):
                    bank, bb = rotY.next()
                    for k in range(8):
                        R.op("pe", CALL("matmul", bank[:, :], lhsT=mT.v[:, k, q0:q0 + 128],
                                                                             rhs=wov[ch][:, k, :], start=(k == 0), stop=(k == 7)),
                             reads=[wob[ch]] + mT.bufs(k, k + 1, q0, q0 + 128), writes=[bb])
                    ybank.append((bank, bb))
                if len(nt_q) >= 2:
                    nt_q.pop(0)()
                st, stb = st_p.next()
                for ch in range(2):
                    jt, jb = junk_p.next()
                    R.op("act", CALL("activation", out=jt[:, 0:512], in_=ybank[ch][0][:, :], func=AF.Square,
                                                                    scale=1.0 / 32.0, accum_out=st[:, ch:ch + 1]),
                         reads=[ybank[ch][1]], writes=[jb, stb])
                R.op("dve", CALL("tensor_tensor", out=st[:, 0:1], in0=st[:, 0:1], in1=st[:, 1:2], op=ALU.add),
                     reads=[stb], writes=[stb])
                R.op("act", CALL("activation", out=st[:, 1:2], in_=st[:, 0:1], func=AF.Ln, bias=RMS_EPS, scale=1.0),
                     reads=[stb], writes=[stb])
                R.op("act", CALL("activation", out=st[:, 2:3], in_=st[:, 1:2], func=AF.Exp, scale=-0.5),
                     reads=[stb], writes=[stb])
                xt, xb = xs_p.next()
                R.dma("sp", CALL("dma_start", out=xt[:, :], in_=x_d[s, q0:q0 + 128, :]), writes=[xb])
                for ch in range(2):
                    c0 = ch * 512
                    tt, ttb = f_p.next()
                    R.op("dve", CALL("scalar_tensor_tensor",
                        out=tt[:, :], in0=ybank[ch][0][:, :], scalar=st[:, 2:3], in1=g2_t[:, c0:c0 + 512],
                        op0=ALU.mult, op1=ALU.mult), reads=[ybank[ch][1], stb, cB], writes=[ttb])
                    R.op("dve", CALL("tensor_tensor", out=hT.v[:, tl, c0:c0 + 512], in0=tt[:, :],
                                                                       in1=xt[:, c0:c0 + 512], op=ALU.add),
                         reads=[ttb, xb], writes=hT.bufs(tl, tl + 1, c0, c0 + 512))
                nt_q.append(lambda tl=tl: norm_transpose(hT.v[:, tl, :], hT.bufs(tl, tl + 1, 0, D), 128, g3_t, u2T, tl * 128, rotT))
            while nt_q:
                nt_q.pop(0)()
            if s == 0 and hf == 0:
                dbg_dump("h0", hT.v[:, :, :], hT.bufs(0, 8, 0, D), [128, 8, D], F32)
                dbg_dump("u2T0", u2T.v[:, :, :], u2T.bufs(0, 8, 0, 1024), [128, 8, 1024])
            if stop_after <= 7:
                continue
            rotF = Rot([0, 1, 2, 3, 4, 5, 6, 7])
            wcur = None
            for j in range(NJ):
                if j % 2 == 0:
                    wcur = w_p.next()
                    for jj in range(2):
                        if j + jj < NJ:
                            load_w(w_fi_d, (j + jj) * 128, 128, wcur[0], wcur[1], dst0=jj * 256)
                            load_w(w_fi_d, DFF + (j + jj) * 128, 128, wcur[0], wcur[1], dst0=jj * 256 + 128)
                wt, wb = wcur
                wv3 = wt[:].rearrange("p (k c) -> p k c", k=8)
                cg = (j % 2) * 256
                for t in range(2):
                    t0 = t * 512
                    bgk, bgb = rotF.next()
                    buk, bub = rotF.next()
                    for k in range(8):
                        R.op("pe", CALL("matmul", bgk[:, :], lhsT=wv3[:, k, cg:cg + 128],
                                                                      rhs=u2T.v[:, k, t0:t0 + 512], start=(k == 0), stop=(k == 7)),
                             reads=[wb] + u2T.bufs(k, k + 1, t0, t0 + 512), writes=[bgb])
                    for k in range(8):
                        R.op("pe", CALL("matmul", buk[:, :], lhsT=wv3[:, k, cg + 128:cg + 256],
                                                                      rhs=u2T.v[:, k, t0:t0 + 512], start=(k == 0), stop=(k == 7)),
                             reads=[wb] + u2T.bufs(k, k + 1, t0, t0 + 512), writes=[bub])
                    sg, sgb = f_p.next()
                    R.op("act", CALL("activation", out=sg[:, :], in_=bgk[:, :], func=AF.Silu),
                         reads=[bgb], writes=[sgb])
                    R.op("dve", CALL("tensor_tensor", out=aT.v[:, j, t0:t0 + 512], in0=buk[:, :],
                                                                                     in1=sg[:, :], op=ALU.mult),
                         reads=[bub, sgb], writes=aT.bufs(j, j + 1, t0, t0 + 512))
            if s == 0 and hf == 0:
                dbg_dump("aT0", aT.v[:, :, :], aT.bufs(0, NJ, 0, 1024), [128, NJ, 1024])
            if stop_after <= 8:
                continue
            rotO2 = Rot([0, 1, 2, 3])
            ms_t, msb = msffn_t, msffn_b
            for ch in range(2):
                wsl = []
                for (j0, j1) in ((0, 8), (8, 16), (16, 22)):
                    wt, wb = w_p.next()
                    wv3 = wt[:].rearrange("p (k c) -> p k c", k=8)
                    R.dma("pool", CALL("dma_start", out=wv3[:, 0:j1 - j0, :],
                                                                              in_=w_fo_d[:, j0:j1, ch * 512:(ch + 1) * 512]),
                          writes=[wb])
                    wsl.append((wv3, wb, j0, j1))
                for tl in range(8):
                    tb = hf * 8 + tl
                    bank, bb = rotO2.next()
                    for (wv3, wb, j0, j1) in wsl:
                        for j in range(j0, j1):
                            R.op("pe", CALL("matmul",
                                bank[:, :], lhsT=aT.v[:, j, tl * 128:(tl + 1) * 128], rhs=wv3[:, j - j0, :],
                                start=(j == 0), stop=(j == NJ - 1)),
                                reads=[wb] + aT.bufs(j, j + 1, tl * 128, (tl + 1) * 128), writes=[bb])
                    jt, jb = junk_p.next()
                    R.op("act", CALL("activation",
                        out=jt[:, 0:512], in_=bank[:, :], func=AF.Square, scale=1.0 / 32.0, accum_out=ms_t[:, tl, ch:ch + 1]),
                        reads=[bb], writes=[jb, msb])
                    if ch == 0:
                        R.op("act", CALL("activation", out=fst.v[:, tl, :], in_=bank[:, :], func=AF.Copy),
                             reads=[bb], writes=fst.bufs(tl, tl + 1, 0, 512))
                        continue
                    R.op("dve", CALL("tensor_tensor", out=ms_t[:, tl, 0:1], in0=ms_t[:, tl, 0:1], in1=ms_t[:, tl, 1:2],
                                                                 op=ALU.add), reads=[msb], writes=[msb])
                    R.op("act", CALL("activation", out=ms_t[:, tl, 2:3], in_=ms_t[:, tl, 0:1], func=AF.Ln, bias=RMS_EPS,
                                                              scale=1.0), reads=[msb], writes=[msb])
                    R.op("act", CALL("activation", out=ms_t[:, tl, 3:4], in_=ms_t[:, tl, 2:3], func=AF.Exp, scale=-0.5),
                         reads=[msb], writes=[msb])
                    ot, ob = xs_p.next()
                    for c2 in range(2):
                        c0 = c2 * 512
                        src = fst.v[:, tl, :] if c2 == 0 else bank[:, :]
                        srcb = fst.bufs(tl, tl + 1, 0, 512) if c2 == 0 else [bb]
                        tt, ttb = f_p.next()
                        R.op("dve", CALL("scalar_tensor_tensor",
                            out=tt[:, :], in0=src, scalar=ms_t[:, tl, 3:4], in1=g4_t[:, c0:c0 + 512], op0=ALU.mult,
                            op1=ALU.mult), reads=srcb + [msb, cB], writes=[ttb])
                        R.op("dve", CALL("tensor_tensor",
                            out=ot[:, c0:c0 + 512], in0=tt[:, :], in1=hT.v[:, tl, c0:c0 + 512], op=ALU.add),
                            reads=[ttb] + hT.bufs(tl, tl + 1, c0, c0 + 512), writes=[ob])
                    R.dma("sp", CALL("dma_start", out=out_d[s, tb * 128:(tb + 1) * 128, :], in_=ot[:, :]),
                          reads=[ob], writes=[Buf("outsink")])

    fin = Buf("fin")
    tail_deps = [ins for ins in R.streams["sp"] if ins.is_dma]
    fi = R.op("sp", None, reads=[], writes=[fin])
    for d in tail_deps:
        fi.deps.append(d)
        d.needs_inc = True
    R.finalize_and_emit()
    return nc, R


N_CORES = 8
_PROG_CACHE = {}


def make_in_maps(inputs, nseq=2, n_cores=N_CORES):
    f = lambda a: np.ascontiguousarray(np.asarray(a, dtype=np.float32))
    x = f(inputs["x"])
    sink = f(inputs["attn_sink"])[0]
    p = np.arange(128)
    head_of = 2 * np.arange(8)[None, :] + (p[:, None] // 64)
    shared = {
        "meta": f(inputs["meta_tokens"]),
        "w_in": f(inputs["w_in"])[0],
        "w_bs": f(inputs["w_branch_swa"])[0],
        "w_bd": f(inputs["w_branch_diff"])[0],
        "w_out": f(inputs["w_out"])[0],
        "w_fi": f(inputs["w_ffn_in"])[0],
        "w_fo": f(inputs["w_ffn_out"])[0],
        "g1_fm": f(f(inputs["pre_mix_gain"])[0].reshape(8, 128).T),
        "g3_fm": f(f(inputs["pre_ffn_gain"])[0].reshape(8, 128).T),
        "g_post": f(inputs["post_mix_gain"]).reshape(1, D),
        "g_postffn": f(inputs["post_ffn_gain"]).reshape(1, D),
        "bgate_fm": f(f(inputs["b_gate"])[0].reshape(16, 128).T),
        "sink_fm": f(sink[head_of]),
        "lam_p": f(np.stack([f(inputs["lambda_q1"])[0], f(inputs["lambda_k1"])[0],
                             f(inputs["lambda_q2"])[0], f(inputs["lambda_k2"])[0]], 0)),
        "subln": f(f(inputs["diff_subln_gain"])[0].reshape(128, 1)),
    }
    shared.update(host_constants())
    maps = []
    for c in range(n_cores):
        m = dict(shared)
        m["x"] = np.ascontiguousarray(x[c * nseq:(c + 1) * nseq])
        maps.append(m)
    return maps


def kernel(**inputs):
    if "prog" not in _PROG_CACHE:
        _PROG_CACHE["prog"] = build_program(nseq=2)[0]
    nc = _PROG_CACHE["prog"]
    in_maps = make_in_maps(inputs, nseq=2)
    res = run_bass_kernel_spmd(nc, in_maps, core_ids=list(range(N_CORES)))
    out = np.concatenate([np.asarray(r["out"]) for r in res.results], axis=0)
    return out.astype(np.float32, copy=False)
```

```python
import numpy as np
import ml_dtypes
from contextlib import ExitStack
import concourse.bass as bass
import concourse.mybir as mybir
from concourse.bass_utils import run_bass_kernel_spmd

F32 = mybir.dt.float32
BF16 = mybir.dt.bfloat16
AF = mybir.ActivationFunctionType
ALU = mybir.AluOpType
AX = mybir.AxisListType

EPOCH = 16000
N_DMA_SEMS = 14


class Buf:
    __slots__ = ("name", "writer", "readers", "dma_readers")

    def __init__(self, name):
        self.name = name
        self.writer = None
        self.readers = {}
        self.dma_readers = []


class Instr:
    __slots__ = ("eng", "fn", "deps", "is_dma", "seq", "needs_inc", "sem", "val", "dma_slot")

    def __init__(self, eng, fn, is_dma):
        self.eng = eng
        self.fn = fn
        self.deps = []
        self.is_dma = is_dma
        self.seq = -1
        self.needs_inc = False
        self.sem = None
        self.val = 0
        self.dma_slot = -1


ENGS = ("pe", "act", "dve", "pool", "sp")


def CALL(method, *args, **kwargs):
    return (method, args, kwargs)


class Rec:
    def __init__(self, nc):
        self.nc = nc
        self.streams = {e: [] for e in ENGS}
        self.dma_rr = {e: 0 for e in ENGS}
        self.dma_last = {e: [None] * N_DMA_SEMS for e in ENGS}
        self.n_instr = 0

    def _add(self, eng, fn, reads, writes, is_dma):
        ins = Instr(eng, fn, is_dma)
        deps = {}
        for b in reads:
            w = b.writer
            if w is not None:
                deps[id(w)] = w
        for b in writes:
            w = b.writer
            if w is not None:
                deps[id(w)] = w
            for r in b.readers.values():
                deps[id(r)] = r
            for r in b.dma_readers:
                deps[id(r)] = r
        if is_dma:
            slot = self.dma_rr[eng]
            self.dma_rr[eng] = (slot + 1) % N_DMA_SEMS
            prev = self.dma_last[eng][slot]
            if prev is not None:
                deps[id(prev)] = prev
            self.dma_last[eng][slot] = ins
            ins.dma_slot = slot
        for d in deps.values():
            if d is ins:
                continue
            if not is_dma and not d.is_dma and d.eng == eng:
                if eng == "pe":
                    continue
                wrote = False
                for b in reads:
                    if b.writer is d:
                        wrote = True
                for b in writes:
                    if b.writer is d:
                        wrote = True
                if not wrote:
                    continue
            ins.deps.append(d)
            d.needs_inc = True
        for b in reads:
            if is_dma:
                b.dma_readers.append(ins)
            else:
                b.readers[eng] = ins
        for b in writes:
            b.writer = ins
            b.readers = {}
            b.dma_readers = []
        ins.seq = len(self.streams[eng])
        self.streams[eng].append(ins)
        self.n_instr += 1
        return ins

    def op(self, eng, fn, reads=(), writes=()):
        return self._add(eng, fn, reads, writes, False)

    def dma(self, eng, fn, reads=(), writes=()):
        ins = self._add(eng, fn, reads, writes, True)
        ins.needs_inc = True
        return ins

    def finalize_and_emit(self):
        nc = self.nc
        self.sems = {}
        for e in ENGS:
            cnt = 0
            cur = None
            for ins in self.streams[e]:
                if ins.is_dma or not ins.needs_inc:
                    continue
                if cnt % EPOCH == 0:
                    cur = nc.alloc_semaphore(f"s_{e}_{cnt // EPOCH}")
                ins.sem = cur
                ins.val = cnt % EPOCH + 1
                cnt += 1
        for e in ENGS:
            if self.dma_rr[e] == 0 and self.dma_last[e][0] is None:
                continue
            sems = [nc.alloc_semaphore(f"d_{e}_{i}") for i in range(N_DMA_SEMS)]
            counts = [0] * N_DMA_SEMS
            for ins in self.streams[e]:
                if ins.is_dma:
                    counts[ins.dma_slot] += 16
                    ins.sem = sems[ins.dma_slot]
                    ins.val = counts[ins.dma_slot]
        self.n_waits = 0

        def replay(ename, eobj):
            waited_seq = {}
            waited_dma = {}
            for ins in self.streams[ename]:
                waits = []
                for d in ins.deps:
                    if d.is_dma:
                        k = id(d.sem)
                        if waited_dma.get(k, 0) >= d.val:
                            continue
                        waited_dma[k] = d.val
                    else:
                        if waited_seq.get(d.eng, -1) >= d.seq:
                            continue
                        waited_seq[d.eng] = d.seq
                    waits.append((d.sem, d.val))
                best = {}
                for (sm, v) in waits:
                    k = id(sm)
                    if k not in best or best[k][1] < v:
                        best[k] = (sm, v)
                waits = list(best.values())
                if ins.fn is None:
                    for (sm, v) in waits:
                        eobj.wait_ge(sm, v)
                        self.n_waits += 1
                    continue
                for (sm, v) in waits[1:]:
                    eobj.wait_ge(sm, v)
                    self.n_waits += 1
                m, a, kw = ins.fn
                bi = getattr(eobj, m)(*a, **kw)
                if waits:
                    bi._wait_ge(waits[0][0], waits[0][1])
                if ins.is_dma:
                    bi.then_inc(ins.sem, 16)
                elif ins.needs_inc:
                    bi.then_inc(ins.sem, 1)

        with nc.Block() as block:
            @block.tensor
            def _(t):
                replay("pe", t)

            @block.scalar
            def _(a):
                replay("act", a)

            @block.vector
            def _(v):
                replay("dve", v)

            @block.gpsimd
            def _(g):
                replay("pool", g)

            @block.sync
            def _(s):
                replay("sp", s)


D = 1024
SEQ = 2048
NMETA = 16
TOK = SEQ + NMETA
DFF = 2816
NJ = DFF // 128
IN_COLS = 6656
QA0, KA0, VA0, QB0, KB0, VB0, G0 = 0, 1024, 1280, 1536, 2560, 3584, 4608
RMS_EPS = 1e-6
A0 = 0
B0 = A0 + 8 * TOK
C0 = B0 + 8 * SEQ
D0 = C0 + 14336
ARENA = D0 + 8 * SEQ
GRAN = 128
MASKV = -30000.0


def host_constants():
    pos = np.concatenate([np.arange(SEQ) + NMETA, np.arange(NMETA)]).astype(np.float64)
    inv_freq = 10000.0 ** (-(np.arange(0, 64, 2, dtype=np.float64)) / 64.0)
    p = np.arange(128)
    ang = inv_freq[p % 32][:, None] * pos[None, :]
    cosT = np.cos(ang)
    sgn = np.where((p % 64) < 32, 1.0, -1.0)[:, None]
    sinP = np.sin(ang) * sgn
    b = np.arange(128)[:, None]
    a = np.arange(128)[None, :]
    lo = np.where(a <= b, 0.0, MASKV)
    hi = np.where(b <= a, 0.0, MASKV)
    maskb = np.concatenate([np.tile(lo, (1, 4)), np.tile(hi, (1, 4))], axis=1)
    ident = np.eye(128)
    prot = np.zeros((128, 128))
    prot[p ^ 32, p] = 1.0
    ones = np.ones((128, 128))
    onesd = np.full((128, 128), 1.0 / 128.0)
    mats = np.concatenate([ident, prot, ones, onesd], axis=1)
    bf = ml_dtypes.bfloat16
    return {"c_cos": cosT.astype(np.float32).astype(bf), "c_sin": sinP.astype(np.float32).astype(bf),
            "c_mask": maskb.astype(np.float32).astype(bf), "c_mats": mats.astype(np.float32).astype(bf)}


def build_program(nseq=2, stop_after=99, debug=False):
    nc = bass.Bass("TRN2", target_bir_lowering=False)
    R = Rec(nc)

    def din(name, shape, dt=F32):
        return nc.dram_tensor(name, list(shape), dt, kind="ExternalInput").ap()

    x_d = din("x", [nseq, SEQ, D])
    meta_d = din("meta", [NMETA, D])
    w_in_d = din("w_in", [D, IN_COLS]).rearrange("(kc p) n -> p kc n", p=128)
    w_bs_d = din("w_bs", [D, D]).rearrange("(kc p) n -> p kc n", p=128)
    w_bd_d = din("w_bd", [D, D]).rearrange("(kc p) n -> p kc n", p=128)
    w_out_d = din("w_out", [D, D]).rearrange("(kc p) n -> p kc n", p=128)
    w_fi_d = din("w_fi", [D, 2 * DFF]).rearrange("(kc p) n -> p kc n", p=128)
    w_fo_d = din("w_fo", [DFF, D]).rearrange("(j p) n -> p j n", p=128)
    g1_d = din("g1_fm", [128, 8])
    g3_d = din("g3_fm", [128, 8])
    g2_d = din("g_post", [1, D])
    g4_d = din("g_postffn", [1, D])
    bg_d = din("bgate_fm", [128, 16])
    sink_d = din("sink_fm", [128, 8])
    lamp_d = din("lam_p", [4, 64])
    subln_d = din("subln", [128, 1])
    cos_d = din("c_cos", [128, TOK], BF16)
    sin_d = din("c_sin", [128, TOK], BF16)
    mask_d = din("c_mask", [128, 1024], BF16)
    mats_d = din("c_mats", [128, 512], BF16)
    out_d = nc.dram_tensor("out", [nseq, SEQ, D], F32, kind="ExternalOutput").ap()

    arena = nc.alloc_sbuf_tensor("arena", [128, ARENA], BF16)
    agran = [Buf(f"ar{i}") for i in range((ARENA + GRAN - 1) // GRAN)]

    class AT:
        def __init__(self, base, Rr, C, dt=BF16):
            self.base, self.R, self.C, self.dt = base, Rr, C, dt
            self.es = 2 if dt == F32 else 1
            assert base + Rr * C * self.es <= ARENA
            v = arena[:, base:base + Rr * C * self.es]
            if dt == F32:
                v = v.bitcast(F32)
            self.v = v.rearrange("p (r c) -> p r c", r=Rr)

        def bufs(self, r0, r1, c0, c1):
            out = []
            for r in range(r0, r1):
                s = self.base + (r * self.C + c0) * self.es
                e = self.base + (r * self.C + c1) * self.es
                out.extend(agran[s // GRAN:(e - 1) // GRAN + 1])
            return out

    def sb(name, shape, dt):
        return nc.alloc_sbuf_tensor(name, list(shape), dt)

    cos_t = sb("cos_t", [128, TOK], BF16)
    sin_t = sb("sin_t", [128, TOK], BF16)
    mask_t = sb("mask_t", [128, 1024], BF16)
    mats_t = sb("mats_t", [128, 512], BF16)
    ident = mats_t[:, 0:128]
    prot = mats_t[:, 128:256]
    ones = mats_t[:, 256:384]
    onesd = mats_t[:, 384:512]
    g1_t = sb("g1_t", [128, 8], F32)
    g3_t = sb("g3_t", [128, 8], F32)
    g2_t = sb("g2_t", [128, D], F32)
    g4_t = sb("g4_t", [128, D], F32)
    bg_t = sb("bg_t", [128, 16], F32)
    es_t = sb("es_t", [128, 8], F32)
    lamd = sb("lamd", [128, 4], F32)
    nlam = sb("nlam", [128, 1], F32)
    g08 = sb("g08", [128, 1], F32)
    cB = Buf("consts")
    msffn_t = sb("msffn", [128, 8, 4], F32)
    msffn_b = Buf("msffn")

    class TPool:
        def __init__(self, name, n, shape, dt):
            self.t = [sb(f"{name}{i}", shape, dt) for i in range(n)]
            self.b = [Buf(f"{name}{i}") for i in range(n)]
            self.i = 0

        def next(self):
            i = self.i
            self.i = (i + 1) % len(self.t)
            return self.t[i], self.b[i]

    xs_p = TPool("xs", 2, [128, D], F32)
    xn_p = TPool("xn", 2, [128, D], BF16)
    junk_p = TPool("junk", 1, [128, D], BF16)
    st_p = TPool("st", 8, [128, 4], F32)
    e_p = TPool("E", 7, [128, 512], BF16)
    ra_p = e_p
    rb_p = e_p
    f_p = TPool("ft", 5, [128, 512], F32)
    w_p = TPool("wsl", 4, [128, 4096], BF16)
    psum = [nc.alloc_psum_tensor(f"ps{i}", [128, 512], F32) for i in range(8)]
    psb = [Buf(f"ps{i}") for i in range(8)]

    class Rot:
        def __init__(self, idxs):
            self.idxs, self.i = list(idxs), 0

        def next(self):
            k = self.idxs[self.i]
            self.i = (self.i + 1) % len(self.idxs)
            return psum[k], psb[k]

    dbg_out = {}

    def dbg_dump(name, ap, bufs, shape, dt=BF16):
        if not debug:
            return
        d = nc.dram_tensor("dbg_" + name, list(shape), dt, kind="ExternalOutput").ap()
        dbg_out[name] = R.dma("sp", CALL("dma_start", out=d, in_=ap), reads=bufs, writes=[Buf("dbgsink")])

    _lt, _lb = f_p.next()
    lamp_t = _lt[:, 0:256].rearrange("p (a b) -> p a b", a=4)
    lamtmp = _lt[:, 256:384].rearrange("p (a b) -> p a b", a=2)
    def cload(dst, src):
        R.dma("sp", CALL("dma_start", out=dst, in_=src), writes=[cB, _lb])

    cload(cos_t[:], cos_d)
    cload(sin_t[:], sin_d)
    cload(mask_t[:], mask_d)
    cload(mats_t[:], mats_d)
    cload(g1_t[:], g1_d)
    cload(g3_t[:], g3_d)
    cload(g2_t[:], g2_d.partition_broadcast(128))
    cload(g4_t[:], g4_d.partition_broadcast(128))
    cload(bg_t[:], bg_d)
    cload(es_t[:], sink_d)
    for i in range(4):
        cload(lamp_t[:, i, :], lamp_d[i:i + 1, :].partition_broadcast(128))
    cload(g08[:], subln_d)
    R.op("act", CALL("activation", out=es_t[:], in_=es_t[:], func=AF.Exp), reads=[cB], writes=[cB])
    R.op("dve", CALL("tensor_tensor", out=lamtmp[:, 0, :], in0=lamp_t[:, 0, :], in1=lamp_t[:, 1, :], op=ALU.mult),
         reads=[cB, _lb], writes=[cB, _lb])
    R.op("dve", CALL("tensor_tensor", out=lamtmp[:, 1, :], in0=lamp_t[:, 2, :], in1=lamp_t[:, 3, :], op=ALU.mult),
         reads=[cB, _lb], writes=[cB, _lb])
    R.op("dve", CALL("reduce_sum", out=lamd[:, 0:2], in_=lamtmp, axis=AX.X), reads=[cB, _lb], writes=[cB, _lb])
    R.op("act", CALL("activation", out=lamd[:, 2:4], in_=lamd[:, 0:2], func=AF.Exp), reads=[cB], writes=[cB])
    R.op("dve", CALL("tensor_tensor", out=nlam[:], in0=lamd[:, 3:4], in1=lamd[:, 2:3], op=ALU.subtract),
         reads=[cB], writes=[cB])
    R.op("dve", CALL("tensor_scalar", out=nlam[:], in0=nlam[:], scalar1=-0.2, scalar2=None, op0=ALU.add),
         reads=[cB], writes=[cB])
    R.op("dve", CALL("tensor_scalar", out=g08[:], in0=g08[:], scalar1=0.8, scalar2=None, op0=ALU.mult),
         reads=[cB], writes=[cB])

    def rstd_from_ms(ms_ap, st_t, st_b, extra_reads=()):
        R.op("act", CALL("activation", out=st_t[:, 1:2], in_=ms_ap, func=AF.Ln, bias=RMS_EPS, scale=1.0),
             reads=[st_b] + list(extra_reads), writes=[st_b])
        R.op("act", CALL("activation", out=st_t[:, 2:3], in_=st_t[:, 1:2], func=AF.Exp, scale=-0.5),
             reads=[st_b], writes=[st_b])
        return st_t[:, 2:3]

    def load_w(src3, c0, ncols, wt, wb, dst0=0, kc_n=8, dst_view=None):
        v = dst_view if dst_view is not None else wt[:].rearrange("p (k c) -> p k c", k=8)
        R.dma("pool", CALL("dma_start", out=v[:, 0:kc_n, dst0:dst0 + ncols], in_=src3[:, 0:kc_n, c0:c0 + ncols]),
              writes=[wb])

    def norm_transpose(src_tile, src_b, rows, gain_t, dst, c0, rot):
        jt, jb = junk_p.next()
        st, stb = st_p.next()
        R.op("act", CALL("activation", out=jt[0:rows, :], in_=src_tile[0:rows, :], func=AF.Square, scale=1.0 / 32.0,
                                           accum_out=st[0:rows, 0:1]),
             reads=src_b, writes=[jb, stb])
        R.op("act", CALL("activation", out=st[0:rows, 1:2], in_=st[0:rows, 0:1], func=AF.Ln, bias=RMS_EPS, scale=1.0),
             reads=[stb], writes=[stb])
        R.op("act", CALL("activation", out=st[0:rows, 2:3], in_=st[0:rows, 1:2], func=AF.Exp, scale=-0.5),
             reads=[stb], writes=[stb])
        xt, xb = xn_p.next()
        R.op("dve", CALL("tensor_scalar", out=xt[0:rows, :], in0=src_tile[0:rows, :], scalar1=st[0:rows, 2:3],
                                              scalar2=None, op0=ALU.mult),
             reads=list(src_b) + [stb], writes=[xb])
        bank, bb = rot.next()
        pbf = bank[:].bitcast(BF16)
        for k in range(8):
            R.op("pe", CALL("transpose", pbf[:, k * 128:k * 128 + rows], xt[0:rows, k * 128:(k + 1) * 128],
                                                  ident[0:rows, 0:rows]),
                 reads=[xb, cB], writes=[bb])
        pv = pbf.rearrange("p (k t) -> p k t", k=8)
        gb = gain_t[:, 0:8].unsqueeze(2).to_broadcast([128, 8, rows])
        R.op("dve", CALL("tensor_tensor", out=dst.v[:, 0:8, c0:c0 + rows], in0=pv[:, :, 0:rows], in1=gb, op=ALU.mult),
             reads=[bb, cB], writes=dst.bufs(0, 8, c0, c0 + rows))

    def rope_evac(bank, bb, n, tok0, dsts, scale, rot2):
        at, ab = ra_p.next()
        bt, btb = rb_p.next()
        R.op("dve", CALL("tensor_tensor", out=at[:, 0:n], in0=bank[:, 0:n], in1=cos_t[:, tok0:tok0 + n], op=ALU.mult),
             reads=[bb, cB], writes=[ab])
        R.op("dve", CALL("tensor_tensor", out=bt[:, 0:n], in0=bank[:, 0:n], in1=sin_t[:, tok0:tok0 + n], op=ALU.mult),
             reads=[bb, cB], writes=[btb])

        def stage2():
            b2, b2b = rot2.next()
            R.op("pe", CALL("matmul", b2[:, 0:n], lhsT=ident, rhs=at[:, 0:n], start=True, stop=False),
                 reads=[ab, cB], writes=[b2b])
            R.op("pe", CALL("matmul", b2[:, 0:n], lhsT=prot, rhs=bt[:, 0:n], start=False, stop=True),
                 reads=[btb, cB], writes=[b2b])
            for (p0, p1, dst_ap, dst_bufs) in dsts:
                R.op("act", CALL("activation", out=dst_ap, in_=b2[p0:p1, 0:n], func=AF.Copy, scale=scale),
                     reads=[b2b], writes=dst_bufs)
        return stage2

    TT5 = [(0, 512), (512, 512), (1024, 512), (1536, 512), (2048, 16)]

    def proj_fm(wt, wb, wcol, uT, n_tiles, rot, consume):
        wv = wt[:].rearrange("p (k c) -> p k c", k=8)
        for (tok0, n) in TT5[:n_tiles]:
            bank, bb = rot.next()
            for k in range(8):
                R.op("pe", CALL("matmul",
                    bank[:, 0:n], lhsT=wv[:, k, wcol:wcol + 128], rhs=uT.v[:, k, tok0:tok0 + n],
                    start=(k == 0), stop=(k == 7)),
                    reads=[wb] + uT.bufs(k, k + 1, tok0, tok0 + n), writes=[bb])
            consume(bank, bb, tok0, n)

    pending = []
    xloaded = set()

    def flush_pending(keep=0):
        while len(pending) > keep:
            pending.pop(0)()

    for s in range(nseq):
        uT = AT(A0, 8, TOK)
        qaT = AT(B0, 8, SEQ)
        mT = AT(B0, 8, SEQ)
        kaT = AT(C0, 4, TOK)
        va = AT(C0 + 4 * TOK, 17, 256)
        oT = AT(D0, 8, SEQ)

        rot1 = Rot([0, 1])
        xst = AT(B0, 8, D, F32)

        def emit_xload(sq_, tb):
            if (sq_, tb) in xloaded:
                return
            xloaded.add((sq_, tb))
            rows = 128 if tb < 16 else NMETA
            src = x_d[sq_, tb * 128:(tb + 1) * 128, :] if tb < 16 else meta_d
            R.dma("sp", CALL("dma_start", out=xst.v[0:rows, tb % 8, :], in_=src), writes=xst.bufs(tb % 8, tb % 8 + 1, 0, D))

        for tb in range(8):
            emit_xload(s, tb)
        for tb in range(17):
            rows = 128 if tb < 16 else NMETA
            norm_transpose(xst.v[:, tb % 8, :], xst.bufs(tb % 8, tb % 8 + 1, 0, D), rows, g1_t, uT, tb * 128, rot1)
            if tb + 8 < 17:
                emit_xload(s, tb + 8)
        if s == 0:
            dbg_dump("uT", uT.v[:, :, :], uT.bufs(0, 8, 0, TOK), [128, 8, TOK])
        if stop_after <= 1:
            continue

        rotP = Rot([2, 3, 4, 5])
        rotR = Rot([6, 7])
        wq0, wq0b = w_p.next()
        load_w(w_in_d, QA0, 512, wq0, wq0b)
        wq1, wq1b = w_p.next()
        load_w(w_in_d, QA0 + 512, 512, wq1, wq1b)
        wk, wkb = w_p.next()
        wk5 = wk[:].rearrange("p (k g d c) -> p k g d c", k=8, g=4, d=2)
        for kc in range(8):
            for dd in range(2):
                R.dma("pool", CALL("dma_start",
                    out=wk5[:, kc, :, dd, :],
                    in_=w_in_d[:, kc, KA0:KA0 + 256].rearrange("p (g c) -> p g c", g=4)), writes=[wkb])
        wv_, wvb = w_p.next()
        load_w(w_in_d, VA0, 256, wv_, wvb)

        def consume_rope(dst, r, scale):
            def f(bank, bb, tok0, n):
                pending.append(rope_evac(bank, bb, n, tok0, [(0, 128, dst.v[:, r, tok0:tok0 + n], dst.bufs(r, r + 1, tok0, tok0 + n))],
                                         scale, rotR))
                flush_pending(keep=1)
            return f

        def consume_rope_qz(qz):
            def f(bank, bb, tok0, n):
                dsts = [(0, 64, qz.v[0:64, 0, tok0:tok0 + n], qz.bufs(0, 1, tok0, tok0 + n)),
                        (64, 128, qz.v[64:128, 1, tok0:tok0 + n], qz.bufs(1, 2, tok0, tok0 + n))]
                pending.append(rope_evac(bank, bb, n, tok0, dsts, 0.125, rotR))
                flush_pending(keep=1)
            return f

        for c in range(8):
            wt, wb = (wq0, wq0b) if c < 4 else (wq1, wq1b)
            proj_fm(wt, wb, (c % 4) * 128, uT, 4, rotP, consume_rope(qaT, c, 0.125))
        for g in range(4):
            proj_fm(wk, wkb, g * 128, uT, 5, rotP, consume_rope(kaT, g, 1.0))
        flush_pending()
        wvv = wv_[:].rearrange("p (k c) -> p k c", k=8)
        for tb in range(17):
            rows = 128 if tb < 16 else NMETA
            bank, bb = rotP.next()
            for k in range(8):
                R.op("pe", CALL("matmul",
                    bank[0:rows, 0:256], lhsT=uT.v[:, k, tb * 128:tb * 128 + rows], rhs=wvv[:, k, 0:256],
                    start=(k == 0), stop=(k == 7)),
                    reads=[wvb] + uT.bufs(k, k + 1, tb * 128, tb * 128 + rows), writes=[bb])
            R.op("act", CALL("activation", out=va.v[0:rows, tb, :], in_=bank[0:rows, 0:256],
                                                                          func=AF.Copy),
                 reads=[bb], writes=va.bufs(tb, tb + 1, 0, 256))
        if s == 0:
            dbg_dump("qaT", qaT.v[:, :, :], qaT.bufs(0, 8, 0, SEQ), [128, 8, SEQ])
            dbg_dump("kaT", kaT.v[:, :, :], kaT.bufs(0, 4, 0, TOK), [128, 4, TOK])
            dbg_dump("va", va.v[:, 0:16, :], va.bufs(0, 16, 0, 256), [128, 16, 256])
        if stop_after <= 2:
            continue

        rotS = Rot([0, 1, 2, 3, 4, 5])
        rotO = Rot([6, 7])
        items = []
        for g in range(4):
            for i in range(16):
                kbs = []
                if i > 0:
                    kbs.append((i - 1, 0))
                kbs.append((i, None))
                if i < 15:
                    kbs.append((i + 1, 1))
                kbs.append((16, None))
                for n_, (kb, mk) in enumerate(kbs):
                    items.append((g, i, kb, mk, n_ == 0, n_ == len(kbs) - 1))
        state = {}

        def swa_a(it):
            g, i, kb, mk, first, last = it
            nk = 128 if kb < 16 else NMETA
            k0 = kb * 128
            bE, bEb = rotS.next()
            bO, bOb = rotS.next()
            q0, q1 = i * 128, (i + 1) * 128
            R.op("pe", CALL("matmul", bE[0:nk, 0:256], lhsT=kaT.v[0:64, g, k0:k0 + nk],
                                          rhs=qaT.v[0:64, 2 * g:2 * g + 2, q0:q1], start=True, stop=(mk is None)),
                 reads=kaT.bufs(g, g + 1, k0, k0 + nk) + qaT.bufs(2 * g, 2 * g + 2, q0, q1), writes=[bEb])
            R.op("pe", CALL("matmul", bO[0:nk, 0:256], lhsT=kaT.v[64:128, g, k0:k0 + nk],
                                          rhs=qaT.v[64:128, 2 * g:2 * g + 2, q0:q1], start=True, stop=(mk is None)),
                 reads=kaT.bufs(g, g + 1, k0, k0 + nk) + qaT.bufs(2 * g, 2 * g + 2, q0, q1), writes=[bOb])
            if mk is not None:
                R.op("pe", CALL("matmul", bE[:, 0:256], lhsT=ident, rhs=mask_t[:, mk * 512:mk * 512 + 256],
                                              start=False, stop=True), reads=[cB], writes=[bEb])
                R.op("pe", CALL("matmul", bO[:, 0:256], lhsT=ident, rhs=mask_t[:, mk * 512:mk * 512 + 256],
                                              start=False, stop=True), reads=[cB], writes=[bOb])
            et, eb = e_p.next()
            R.op("act", CALL("activation", out=et[0:nk, 0:256], in_=bE[0:nk, 0:256], func=AF.Exp), reads=[bEb], writes=[eb])
            R.op("act", CALL("activation", out=et[0:nk, 256:512], in_=bO[0:nk, 0:256], func=AF.Exp), reads=[bOb], writes=[eb])
            state[it] = (et, eb, nk)

        import os
        SWA_DBG = int(os.environ.get("SWA_DBG", "9"))

        def swa_b(it):
            g, i, kb, mk, first, last = it
            et, eb, nk = state.pop(it)
            if SWA_DBG <= 1:
                return
            if first:
                state["acc"] = rotO.next()
            acc, accb = state["acc"]
            vrd = va.bufs(kb, kb + 1, g * 64, (g + 1) * 64)
            lv = va.v[0:nk, kb, g * 64:(g + 1) * 64]
            R.op("pe", CALL("matmul", acc[0:64, 0:256], lhsT=lv, rhs=et[0:nk, 0:256], start=first, stop=False),
                 reads=vrd + [eb], writes=[accb])
            R.op("pe", CALL("matmul", acc[64:128, 0:256], lhsT=lv, rhs=et[0:nk, 256:512], start=first, stop=False),
                 reads=vrd + [eb], writes=[accb])
            R.op("pe", CALL("matmul", acc[0:64, 256:512], lhsT=ones[0:nk, 0:64], rhs=et[0:nk, 0:256], start=False,
                                          stop=last), reads=[cB, eb], writes=[accb])
            R.op("pe", CALL("matmul", acc[64:128, 256:512], lhsT=ones[0:nk, 0:64], rhs=et[0:nk, 256:512], start=False,
                                          stop=last), reads=[cB, eb], writes=[accb])
            if last and SWA_DBG > 2:
                dt_, db_ = f_p.next()
                esb = es_t[:, 2 * g:2 * g + 2].unsqueeze(2).to_broadcast([128, 2, 128])
                d3 = dt_[:, 0:256].rearrange("p (j q) -> p j q", j=2)
                R.op("dve", CALL("tensor_tensor", out=d3, in0=acc[:, 256:512].rearrange("p (j q) -> p j q", j=2),
                                                      in1=esb, op=ALU.add), reads=[accb, cB], writes=[db_])
                R.op("dve", CALL("reciprocal", out=dt_[:, 256:512], in_=dt_[:, 0:256]), reads=[db_], writes=[db_])
                q0, q1 = i * 128, (i + 1) * 128
                R.op("dve", CALL("tensor_tensor", out=oT.v[:, 2 * g:2 * g + 2, q0:q1],
                                                      in0=acc[:, 0:256].rearrange("p (j q) -> p j q", j=2),
                                                      in1=dt_[:, 256:512].rearrange("p (j q) -> p j q", j=2), op=ALU.mult),
                     reads=[accb, db_], writes=oT.bufs(2 * g, 2 * g + 2, q0, q1))

        DEPTH = 2
        for n_ in range(len(items) + DEPTH):
            if n_ < len(items):
                swa_a(items[n_])
            if n_ - DEPTH >= 0:
                swa_b(items[n_ - DEPTH])
        if s == 0:
            dbg_dump("oswaT", oT.v[:, :, :], oT.bufs(0, 8, 0, SEQ), [128, 8, SEQ])
        if stop_after <= 3:
            continue

        def merge_phase(w_br_d, gcol0, bcol0, accumulate):
            rotM = Rot([0, 1, 2, 3, 4, 5, 6, 7])
            slots = []
            for half in range(2):
                wa, wab = w_p.next()
                load_w(w_br_d, half * 512, 512, wa, wab)
                wg, wgb = w_p.next()
                load_w(w_in_d, gcol0 + half * 512, 512, wg, wgb)
                slots.append((wa, wab, wg, wgb))
            for m in range(8):
                wa, wab, wg, wgb = slots[m // 4]
                wav = wa[:].rearrange("p (k c) -> p k c", k=8)
                wgv = wg[:].rearrange("p (k c) -> p k c", k=8)
                mc = (m % 4) * 128
                for t in range(4):
                    t0 = t * 512
                    bp, bpb = rotM.next()
                    bg, bgb = rotM.next()
                    for k in range(8):
                        R.op("pe", CALL("matmul", bp[:, :], lhsT=wav[:, k, mc:mc + 128],
                                                                    rhs=oT.v[:, k, t0:t0 + 512], start=(k == 0), stop=(k == 7)),
                             reads=[wab] + oT.bufs(k, k + 1, t0, t0 + 512), writes=[bpb])
                    for k in range(8):
                        R.op("pe", CALL("matmul", bg[:, :], lhsT=wgv[:, k, mc:mc + 128],
                                                                    rhs=uT.v[:, k, t0:t0 + 512], start=(k == 0), stop=(k == 7)),
                             reads=[wgb] + uT.bufs(k, k + 1, t0, t0 + 512), writes=[bgb])
                    gt_, gtb = f_p.next()
                    R.op("act", CALL("activation", out=gt_[:, :], in_=bg[:, :], func=AF.Sigmoid,
                                                                      bias=bg_t[:, bcol0 + m:bcol0 + m + 1], scale=1.0),
                         reads=[bgb, cB], writes=[gtb])
                    mb = mT.bufs(m, m + 1, t0, t0 + 512)
                    if not accumulate:
                        R.op("dve", CALL("tensor_tensor", out=mT.v[:, m, t0:t0 + 512], in0=bp[:, :],
                                                                             in1=gt_[:, :], op=ALU.mult),
                             reads=[bpb, gtb], writes=mb)
                    else:
                        R.op("dve", CALL("tensor_tensor", out=gt_[:, :], in0=bp[:, :], in1=gt_[:, :],
                                                                             op=ALU.mult), reads=[bpb, gtb], writes=[gtb])
                        R.op("dve", CALL("tensor_tensor", out=mT.v[:, m, t0:t0 + 512], in0=mT.v[:, m, t0:t0 + 512],
                                                                      in1=gt_[:, :], op=ALU.add), reads=[gtb] + mb, writes=mb)

        merge_phase(w_bs_d, G0, 0, False)
        if s == 0:
            dbg_dump("m1T", mT.v[:, :, :], mT.bufs(0, 8, 0, SEQ), [128, 8, SEQ])
        if stop_after <= 4:
            continue

        rotP = Rot([0, 1, 2])
        rotR = Rot([3])
        for h in range(8):
            base = C0
            qz = AT(base, 2, SEQ)
            kb_ = AT(base + 2 * SEQ, 1, TOK)
            vb = AT(base + 2 * SEQ + TOK, 17, 256)
            hv = (h % 2) * 128
            if h == 0:
                R.op("dve", CALL("memset", qz.v[64:128, 0, :], 0.0), writes=qz.bufs(0, 1, 0, SEQ))
                R.op("dve", CALL("memset", qz.v[0:64, 1, :], 0.0), writes=qz.bufs(1, 2, 0, SEQ))
            wt, wb = w_p.next()
            wv3 = wt[:].rearrange("p (k c) -> p k c", k=8)
            load_w(w_in_d, QB0 + h * 128, 128, wt, wb, dst0=0)
            load_w(w_in_d, KB0 + h * 128, 128, wt, wb, dst0=128)
            if h % 2 == 0:
                load_w(w_in_d, VB0 + h * 128, 256, wt, wb, dst0=256)
            rotP = Rot([0, 1, 2])
            rotR = Rot([7])
            proj_fm(wt, wb, 0, uT, 4, rotP, consume_rope_qz(qz))
            proj_fm(wt, wb, 128, uT, 5, rotP, consume_rope(kb_, 0, 1.0))
            flush_pending()
            for tb in (range(17) if h % 2 == 0 else ()):
                rows = 128 if tb < 16 else NMETA
                bank, bb = rotP.next()
                for k in range(8):
                    R.op("pe", CALL("matmul",
                        bank[0:rows, 0:256], lhsT=uT.v[:, k, tb * 128:tb * 128 + rows], rhs=wv3[:, k, 256:512],
                        start=(k == 0), stop=(k == 7)),
                        reads=[wb] + uT.bufs(k, k + 1, tb * 128, tb * 128 + rows), writes=[bb])
                R.op("dve", CALL("tensor_copy", out=vb.v[0:rows, tb, :], in_=bank[0:rows, 0:256]),
                     reads=[bb], writes=vb.bufs(tb, tb + 1, 0, 256))
            if s == 0 and h == 0:
                dbg_dump("qz0", qz.v[:, :, :], qz.bufs(0, 2, 0, SEQ), [128, 2, SEQ])
                dbg_dump("kb0", kb_.v[:, :, :], kb_.bufs(0, 1, 0, TOK), [128, 1, TOK])
                dbg_dump("vb0", vb.v[:, 0:16, 0:128], vb.bufs(0, 16, 0, 128), [128, 16, 128])
            rotS = Rot([0, 1, 2, 3])
            accs = {0: (4, 5), 1: (6, 7)}
            deferred = []
            ditems = [(qt, c, kbi) for qt in range(4) for c in range(2) for kbi in range(17)]
            dstate = {}

            def dif_a(it):
                qt, c, kbi = it
                nk = 128 if kbi < 16 else NMETA
                k0 = kbi * 128
                bank, bb = rotS.next()
                R.op("pe", CALL("matmul", bank[0:nk, :], lhsT=kb_.v[:, 0, k0:k0 + nk],
                                              rhs=qz.v[:, c, qt * 512:(qt + 1) * 512], start=True, stop=True),
                     reads=kb_.bufs(0, 1, k0, k0 + nk) + qz.bufs(c, c + 1, qt * 512, (qt + 1) * 512), writes=[bb])
                et, eb = e_p.next()
                R.op("act", CALL("activation", out=et[0:nk, :], in_=bank[0:nk, :], func=AF.Exp), reads=[bb], writes=[eb])
                dstate[it] = (et, eb, nk)

            def dif_b(it):
                qt, c, kbi = it
                et, eb, nk = dstate.pop(it)
                first, last = kbi == 0, kbi == 16
                oi, si = accs[c]
                R.op("pe", CALL("matmul", psum[oi][:, :], lhsT=vb.v[0:nk, kbi, hv:hv + 128], rhs=et[0:nk, :], start=first, stop=last),
                     reads=vb.bufs(kbi, kbi + 1, hv, hv + 128) + [eb], writes=[psb[oi]])
                R.op("pe", CALL("matmul", psum[si][:, :], lhsT=ones[0:nk, :], rhs=et[0:nk, :], start=first, stop=last),
                     reads=[cB, eb], writes=[psb[si]])
                if not last:
                    return
                rt, rb = f_p.next()
                R.op("dve", CALL("reciprocal", out=rt[:, :], in_=psum[si][:, :]), reads=[psb[si]], writes=[rb])
                R.op("dve", CALL("tensor_tensor", out=rt[:, :], in0=psum[oi][:, :], in1=rt[:, :], op=ALU.mult),
                     reads=[psb[oi], rb], writes=[rb])
                if c == 0:
                    dstate["t0"] = (rt, rb)
                    return
                t0t, t0b = dstate.pop("t0")
                R.op("dve", CALL("scalar_tensor_tensor", out=rt[:, :], in0=rt[:, :], scalar=nlam[:, 0:1], in1=t0t[:, :],
                                                             op0=ALU.mult, op1=ALU.add), reads=[rb, t0b, cB], writes=[rb])
                sq, sqb = e_p.next()
                R.op("dve", CALL("tensor_tensor", out=sq[:, :], in0=rt[:, :], in1=rt[:, :], op=ALU.mult), reads=[rb], writes=[sqb])

                def tail(qt=qt, rt=rt, rb=rb, t0t=t0t, t0b=t0b, sq=sq, sqb=sqb):
                    ssk, ssb = rotS.next()
                    R.op("pe", CALL("matmul", ssk[:, :], lhsT=onesd, rhs=sq[:, :], start=True, stop=True),
                         reads=[cB, sqb], writes=[ssb])
                    R.op("act", CALL("activation", out=t0t[:, :], in_=ssk[:, :], func=AF.Ln, bias=RMS_EPS, scale=1.0),
                         reads=[ssb], writes=[t0b])
                    R.op("act", CALL("activation", out=t0t[:, :], in_=t0t[:, :], func=AF.Exp, scale=-0.5),
                         reads=[t0b], writes=[t0b])
                    R.op("dve", CALL("scalar_tensor_tensor", out=oT.v[:, h, qt * 512:(qt + 1) * 512], in0=rt[:, :],
                                                                 scalar=g08[:, 0:1], in1=t0t[:, :], op0=ALU.mult, op1=ALU.mult),
                         reads=[rb, t0b, cB], writes=oT.bufs(h, h + 1, qt * 512, (qt + 1) * 512))
                deferred.append([6, tail])

            DD = 3
            for n_ in range(len(ditems) + DD):
                if n_ < len(ditems):
                    dif_a(ditems[n_])
                if n_ - DD >= 0:
                    dif_b(ditems[n_ - DD])
                for dfr in deferred:
                    dfr[0] -= 1
                while deferred and deferred[0][0] <= 0:
                    deferred.pop(0)[1]()
            while deferred:
                deferred.pop(0)[1]()
        if s == 0:
            dbg_dump("odiffT", oT.v[:, :, :], oT.bufs(0, 8, 0, SEQ), [128, 8, SEQ])
        if stop_after <= 5:
            continue
        merge_phase(w_bd_d, G0 + 1024, 8, True)
        if s == 0:
            dbg_dump("mT", mT.v[:, :, :], mT.bufs(0, 8, 0, SEQ), [128, 8, SEQ])
        if stop_after <= 6:
            continue

        for hf in range(2):
            hT = AT(A0, 8, D, F32)
            u2T = AT(C0, 8, 1024)
            aT = AT(C0 + 8192, NJ, 1024)
            fst = AT(C0, 8, 512, F32)
            rotY = Rot([0, 1, 2, 3])
            rotT = Rot([4, 5])
            wo0, wo0b = w_p.next()
            load_w(w_out_d, 0, 512, wo0, wo0b)
            wo1, wo1b = w_p.next()
            load_w(w_out_d, 512, 512, wo1, wo1b)
            wov = [wo0[:].rearrange("p (k c) -> p k c", k=8), wo1[:].rearrange("p (k c) -> p k c", k=8)]
            wob = [wo0b, wo1b]
            nt_q = []
            for tl in range(8):
                tb = hf * 8 + tl
                q0 = tb * 128
                ybank = []
                for ch in range(2):
                    bank, bb = rotY.next()
                    for k in range(8):
                        R.op("pe", CALL("matmul", bank[:, :], lhsT=mT.v[:, k, q0:q0 + 128],
                                                                             rhs=wov[ch][:, k, :], start=(k == 0), stop=(k == 7)),
                             reads=[wob[ch]] + mT.bufs(k, k + 1, q0, q0 + 128), writes=[bb])
                    ybank.append((bank, bb))
                if len(nt_q) >= 2:
                    nt_q.pop(0)()
                st, stb = st_p.next()
                for ch in range(2):
                    jt, jb = junk_p.next()
                    R.op("act", CALL("activation", out=jt[:, 0:512], in_=ybank[ch][0][:, :], func=AF.Square,
                                                                    scale=1.0 / 32.0, accum_out=st[:, ch:ch + 1]),
                         reads=[ybank[ch][1]], writes=[jb, stb])
                R.op("dve", CALL("tensor_tensor", out=st[:, 0:1], in0=st[:, 0:1], in1=st[:, 1:2], op=ALU.add),
                     reads=[stb], writes=[stb])
                R.op("act", CALL("activation", out=st[:, 1:2], in_=st[:, 0:1], func=AF.Ln, bias=RMS_EPS, scale=1.0),
                     reads=[stb], writes=[stb])
                R.op("act", CALL("activation", out=st[:, 2:3], in_=st[:, 1:2], func=AF.Exp, scale=-0.5),
                     reads=[stb], writes=[stb])
                xt, xb = xs_p.next()
                R.dma("sp", CALL("dma_start", out=xt[:, :], in_=x_d[s, q0:q0 + 128, :]), writes=[xb])
                for ch in range(2):
                    c0 = ch * 512
                    tt, ttb = f_p.next()
                    R.op("dve", CALL("scalar_tensor_tensor",
                        out=tt[:, :], in0=ybank[ch][0][:, :], scalar=st[:, 2:3], in1=g2_t[:, c0:c0 + 512],
                        op0=ALU.mult, op1=ALU.mult), reads=[ybank[ch][1], stb, cB], writes=[ttb])
                    R.op("dve", CALL("tensor_tensor", out=hT.v[:, tl, c0:c0 + 512], in0=tt[:, :],
                                                                       in1=xt[:, c0:c0 + 512], op=ALU.add),
                         reads=[ttb, xb], writes=hT.bufs(tl, tl + 1, c0, c0 + 512))
                nt_q.append(lambda tl=tl: norm_transpose(hT.v[:, tl, :], hT.bufs(tl, tl + 1, 0, D), 128, g3_t, u2T, tl * 128, rotT))
            while nt_q:
                nt_q.pop(0)()
            if s == 0 and hf == 0:
                dbg_dump("h0", hT.v[:, :, :], hT.bufs(0, 8, 0, D), [128, 8, D], F32)
                dbg_dump("u2T0", u2T.v[:, :, :], u2T.bufs(0, 8, 0, 1024), [128, 8, 1024])
            if stop_after <= 7:
                continue
            if hf == 1 and s + 1 < nseq:
                for tb in range(8):
                    emit_xload(s + 1, tb)
            rotF = Rot([0, 1, 2, 3, 4, 5, 6, 7])
            wcur = None
            for j in range(NJ):
                if j % 2 == 0:
                    wcur = w_p.next()
                    for jj in range(2):
                        if j + jj < NJ:
                            load_w(w_fi_d, (j + jj) * 128, 128, wcur[0], wcur[1], dst0=jj * 256)
                            load_w(w_fi_d, DFF + (j + jj) * 128, 128, wcur[0], wcur[1], dst0=jj * 256 + 128)
                wt, wb = wcur
                wv3 = wt[:].rearrange("p (k c) -> p k c", k=8)
                cg = (j % 2) * 256
                for t in range(2):
                    t0 = t * 512
                    bgk, bgb = rotF.next()
                    buk, bub = rotF.next()
                    for k in range(8):
                        R.op("pe", CALL("matmul", bgk[:, :], lhsT=wv3[:, k, cg:cg + 128],
                                                                      rhs=u2T.v[:, k, t0:t0 + 512], start=(k == 0), stop=(k == 7)),
                             reads=[wb] + u2T.bufs(k, k + 1, t0, t0 + 512), writes=[bgb])
                    for k in range(8):
                        R.op("pe", CALL("matmul", buk[:, :], lhsT=wv3[:, k, cg + 128:cg + 256],
                                                                      rhs=u2T.v[:, k, t0:t0 + 512], start=(k == 0), stop=(k == 7)),
                             reads=[wb] + u2T.bufs(k, k + 1, t0, t0 + 512), writes=[bub])
                    sg, sgb = f_p.next()
                    R.op("act", CALL("activation", out=sg[:, :], in_=bgk[:, :], func=AF.Silu),
                         reads=[bgb], writes=[sgb])
                    R.op("dve", CALL("tensor_tensor", out=aT.v[:, j, t0:t0 + 512], in0=buk[:, :],
                                                                                     in1=sg[:, :], op=ALU.mult),
                         reads=[bub, sgb], writes=aT.bufs(j, j + 1, t0, t0 + 512))
            if s == 0 and hf == 0:
                dbg_dump("aT0", aT.v[:, :, :], aT.bufs(0, NJ, 0, 1024), [128, NJ, 1024])
            if stop_after <= 8:
                continue
            rotO2 = Rot([0, 1, 2, 3])
            ms_t, msb = msffn_t, msffn_b
            for ch in range(2):
                wsl = []
                for (j0, j1) in ((0, 8), (8, 16), (16, 22)):
                    wt, wb = w_p.next()
                    wv3 = wt[:].rearrange("p (k c) -> p k c", k=8)
                    R.dma("pool", CALL("dma_start", out=wv3[:, 0:j1 - j0, :],
                                                                              in_=w_fo_d[:, j0:j1, ch * 512:(ch + 1) * 512]),
                          writes=[wb])
                    wsl.append((wv3, wb, j0, j1))
                for tl in range(8):
                    tb = hf * 8 + tl
                    bank, bb = rotO2.next()
                    for (wv3, wb, j0, j1) in wsl:
                        for j in range(j0, j1):
                            R.op("pe", CALL("matmul",
                                bank[:, :], lhsT=aT.v[:, j, tl * 128:(tl + 1) * 128], rhs=wv3[:, j - j0, :],
                                start=(j == 0), stop=(j == NJ - 1)),
                                reads=[wb] + aT.bufs(j, j + 1, tl * 128, (tl + 1) * 128), writes=[bb])
                    jt, jb = junk_p.next()
                    R.op("act", CALL("activation",
                        out=jt[:, 0:512], in_=bank[:, :], func=AF.Square, scale=1.0 / 32.0, accum_out=ms_t[:, tl, ch:ch + 1]),
                        reads=[bb], writes=[jb, msb])
                    if ch == 0:
                        R.op("act", CALL("activation", out=fst.v[:, tl, :], in_=bank[:, :], func=AF.Copy),
                             reads=[bb], writes=fst.bufs(tl, tl + 1, 0, 512))
                        continue
                    R.op("dve", CALL("tensor_tensor", out=ms_t[:, tl, 0:1], in0=ms_t[:, tl, 0:1], in1=ms_t[:, tl, 1:2],
                                                                 op=ALU.add), reads=[msb], writes=[msb])
                    R.op("act", CALL("activation", out=ms_t[:, tl, 2:3], in_=ms_t[:, tl, 0:1], func=AF.Ln, bias=RMS_EPS,
                                                              scale=1.0), reads=[msb], writes=[msb])
                    R.op("act", CALL("activation", out=ms_t[:, tl, 3:4], in_=ms_t[:, tl, 2:3], func=AF.Exp, scale=-0.5),
                         reads=[msb], writes=[msb])
                    ot, ob = xs_p.next()
                    for c2 in range(2):
                        c0 = c2 * 512
                        src = fst.v[:, tl, :] if c2 == 0 else bank[:, :]
                        srcb = fst.bufs(tl, tl + 1, 0, 512) if c2 == 0 else [bb]
                        tt, ttb = f_p.next()
                        R.op("dve", CALL("scalar_tensor_tensor",
                            out=tt[:, :], in0=src, scalar=ms_t[:, tl, 3:4], in1=g4_t[:, c0:c0 + 512], op0=ALU.mult,
                            op1=ALU.mult), reads=srcb + [msb, cB], writes=[ttb])
                        R.op("dve", CALL("tensor_tensor",
                            out=ot[:, c0:c0 + 512], in0=tt[:, :], in1=hT.v[:, tl, c0:c0 + 512], op=ALU.add),
                            reads=[ttb] + hT.bufs(tl, tl + 1, c0, c0 + 512), writes=[ob])
                    R.dma("sp", CALL("dma_start", out=out_d[s, tb * 128:(tb + 1) * 128, :], in_=ot[:, :]),
                          reads=[ob], writes=[Buf("outsink")])

    fin = Buf("fin")
    tail_deps = [ins for ins in R.streams["sp"] if ins.is_dma]
    fi = R.op("sp", None, reads=[], writes=[fin])
    for d in tail_deps:
        fi.deps.append(d)
        d.needs_inc = True
    R.finalize_and_emit()
    return nc, R


N_CORES = 8
_PROG_CACHE = {}


def make_in_maps(inputs, nseq=2, n_cores=N_CORES):
    f = lambda a: np.ascontiguousarray(np.asarray(a, dtype=np.float32))
    x = f(inputs["x"])
    sink = f(inputs["attn_sink"])[0]
    p = np.arange(128)
    head_of = 2 * np.arange(8)[None, :] + (p[:, None] // 64)
    shared = {
        "meta": f(inputs["meta_tokens"]),
        "w_in": f(inputs["w_in"])[0],
        "w_bs": f(inputs["w_branch_swa"])[0],
        "w_bd": f(inputs["w_branch_diff"])[0],
        "w_out": f(inputs["w_out"])[0],
        "w_fi": f(inputs["w_ffn_in"])[0],
        "w_fo": f(inputs["w_ffn_out"])[0],
        "g1_fm": f(f(inputs["pre_mix_gain"])[0].reshape(8, 128).T),
        "g3_fm": f(f(inputs["pre_ffn_gain"])[0].reshape(8, 128).T),
        "g_post": f(inputs["post_mix_gain"]).reshape(1, D),
        "g_postffn": f(inputs["post_ffn_gain"]).reshape(1, D),
        "bgate_fm": f(f(inputs["b_gate"])[0].reshape(16, 128).T),
        "sink_fm": f(sink[head_of]),
        "lam_p": f(np.stack([f(inputs["lambda_q1"])[0], f(inputs["lambda_k1"])[0],
                             f(inputs["lambda_q2"])[0], f(inputs["lambda_k2"])[0]], 0)),
        "subln": f(f(inputs["diff_subln_gain"])[0].reshape(128, 1)),
    }
    shared.update(host_constants())
    maps = []
    for c in range(n_cores):
        m = dict(shared)
        m["x"] = np.ascontiguousarray(x[c * nseq:(c + 1) * nseq])
        maps.append(m)
    return maps


def kernel(**inputs):
    if "prog" not in _PROG_CACHE:
        _PROG_CACHE["prog"] = build_program(nseq=2)[0]
    nc = _PROG_CACHE["prog"]
    in_maps = make_in_maps(inputs, nseq=2)
    res = run_bass_kernel_spmd(nc, in_maps, core_ids=list(range(N_CORES)))
    out = np.concatenate([np.asarray(r["out"]) for r in res.results], axis=0)
    return out.astype(np.float32, copy=False)
```

```python
import numpy as np
import ml_dtypes
from contextlib import ExitStack
import concourse.bass as bass
import concourse.mybir as mybir
from concourse.bass_utils import run_bass_kernel_spmd

F32 = mybir.dt.float32
BF16 = mybir.dt.bfloat16
AF = mybir.ActivationFunctionType
ALU = mybir.AluOpType
AX = mybir.AxisListType

EPOCH = 16000
N_DMA_SEMS = 14


class Buf:
    __slots__ = ("name", "writer", "readers", "dma_readers")

    def __init__(self, name):
        self.name = name
        self.writer = None
        self.readers = {}
        self.dma_readers = []


class Instr:
    __slots__ = ("eng", "fn", "deps", "is_dma", "seq", "needs_inc", "sem", "val", "dma_slot")

    def __init__(self, eng, fn, is_dma):
        self.eng = eng
        self.fn = fn
        self.deps = []
        self.is_dma = is_dma
        self.seq = -1
        self.needs_inc = False
        self.sem = None
        self.val = 0
        self.dma_slot = -1


ENGS = ("pe", "act", "dve", "pool", "sp")


def CALL(method, *args, **kwargs):
    return (method, args, kwargs)


class Rec:
    def __init__(self, nc):
        self.nc = nc
        self.streams = {e: [] for e in ENGS}
        self.dma_rr = {e: 0 for e in ENGS}
        self.dma_last = {e: [None] * N_DMA_SEMS for e in ENGS}
        self.n_instr = 0

    def _add(self, eng, fn, reads, writes, is_dma):
        ins = Instr(eng, fn, is_dma)
        deps = {}
        for b in reads:
            w = b.writer
            if w is not None:
                deps[id(w)] = w
        for b in writes:
            w = b.writer
            if w is not None:
                deps[id(w)] = w
            for r in b.readers.values():
                deps[id(r)] = r
            for r in b.dma_readers:
                deps[id(r)] = r
        if is_dma:
            slot = self.dma_rr[eng]
            self.dma_rr[eng] = (slot + 1) % N_DMA_SEMS
            prev = self.dma_last[eng][slot]
            if prev is not None:
                deps[id(prev)] = prev
            self.dma_last[eng][slot] = ins
            ins.dma_slot = slot
        for d in deps.values():
            if d is ins:
                continue
            if not is_dma and not d.is_dma and d.eng == eng:
                if eng == "pe":
                    continue
                wrote = False
                for b in reads:
                    if b.writer is d:
                        wrote = True
                for b in writes:
                    if b.writer is d:
                        wrote = True
                if not wrote:
                    continue
            ins.deps.append(d)
            d.needs_inc = True
        for b in reads:
            if is_dma:
                b.dma_readers.append(ins)
            else:
                b.readers[eng] = ins
        for b in writes:
            b.writer = ins
            b.readers = {}
            b.dma_readers = []
        ins.seq = len(self.streams[eng])
        self.streams[eng].append(ins)
        self.n_instr += 1
        return ins

    def op(self, eng, fn, reads=(), writes=()):
        return self._add(eng, fn, reads, writes, False)

    def dma(self, eng, fn, reads=(), writes=()):
        ins = self._add(eng, fn, reads, writes, True)
        ins.needs_inc = True
        return ins

    def finalize_and_emit(self):
        nc = self.nc
        self.sems = {}
        for e in ENGS:
            cnt = 0
            cur = None
            for ins in self.streams[e]:
                if ins.is_dma or not ins.needs_inc:
                    continue
                if cnt % EPOCH == 0:
                    cur = nc.alloc_semaphore(f"s_{e}_{cnt // EPOCH}")
                ins.sem = cur
                ins.val = cnt % EPOCH + 1
                cnt += 1
        for e in ENGS:
            if self.dma_rr[e] == 0 and self.dma_last[e][0] is None:
                continue
            sems = [nc.alloc_semaphore(f"d_{e}_{i}") for i in range(N_DMA_SEMS)]
            counts = [0] * N_DMA_SEMS
            for ins in self.streams[e]:
                if ins.is_dma:
                    counts[ins.dma_slot] += 16
                    ins.sem = sems[ins.dma_slot]
                    ins.val = counts[ins.dma_slot]
        self.n_waits = 0

        def replay(ename, eobj):
            waited_seq = {}
            waited_dma = {}
            for ins in self.streams[ename]:
                waits = []
                for d in ins.deps:
                    if d.is_dma:
                        k = id(d.sem)
                        if waited_dma.get(k, 0) >= d.val:
                            continue
                        waited_dma[k] = d.val
                    else:
                        if waited_seq.get(d.eng, -1) >= d.seq:
                            continue
                        waited_seq[d.eng] = d.seq
                    waits.append((d.sem, d.val))
                best = {}
                for (sm, v) in waits:
                    k = id(sm)
                    if k not in best or best[k][1] < v:
                        best[k] = (sm, v)
                waits = list(best.values())
                if ins.fn is None:
                    for (sm, v) in waits:
                        eobj.wait_ge(sm, v)
                        self.n_waits += 1
                    continue
                for (sm, v) in waits[1:]:
                    eobj.wait_ge(sm, v)
                    self.n_waits += 1
                m, a, kw = ins.fn
                bi = getattr(eobj, m)(*a, **kw)
                if waits:
                    bi._wait_ge(waits[0][0], waits[0][1])
                if ins.is_dma:
                    bi.then_inc(ins.sem, 16)
                elif ins.needs_inc:
                    bi.then_inc(ins.sem, 1)

        with nc.Block() as block:
            @block.tensor
            def _(t):
                replay("pe", t)

            @block.scalar
            def _(a):
                replay("act", a)

            @block.vector
            def _(v):
                replay("dve", v)

            @block.gpsimd
            def _(g):
                replay("pool", g)

            @block.sync
            def _(s):
                replay("sp", s)


D = 1024
SEQ = 2048
NMETA = 16
TOK = SEQ + NMETA
DFF = 2816
NJ = DFF // 128
IN_COLS = 6656
QA0, KA0, VA0, QB0, KB0, VB0, G0 = 0, 1024, 1280, 1536, 2560, 3584, 4608
RMS_EPS = 1e-6
A0 = 0
B0 = A0 + 8 * TOK
C0 = B0 + 8 * SEQ
D0 = C0 + 14336
ARENA = D0 + 8 * SEQ
GRAN = 128
MASKV = -30000.0


def host_constants():
    pos = np.concatenate([np.arange(SEQ) + NMETA, np.arange(NMETA)]).astype(np.float64)
    inv_freq = 10000.0 ** (-(np.arange(0, 64, 2, dtype=np.float64)) / 64.0)
    p = np.arange(128)
    ang = inv_freq[p % 32][:, None] * pos[None, :]
    cosT = np.cos(ang)
    sgn = np.where((p % 64) < 32, 1.0, -1.0)[:, None]
    sinP = np.sin(ang) * sgn
    b = np.arange(128)[:, None]
    a = np.arange(128)[None, :]
    lo = np.where(a <= b, 0.0, MASKV)
    hi = np.where(b <= a, 0.0, MASKV)
    maskb = np.concatenate([np.tile(lo, (1, 4)), np.tile(hi, (1, 4))], axis=1)
    ident = np.eye(128)
    prot = np.zeros((128, 128))
    prot[p ^ 32, p] = 1.0
    ones = np.ones((128, 128))
    onesd = np.full((128, 128), 1.0 / 128.0)
    mats = np.concatenate([ident, prot, ones, onesd], axis=1)
    bf = ml_dtypes.bfloat16
    return {"c_cos": cosT.astype(np.float32).astype(bf), "c_sin": sinP.astype(np.float32).astype(bf),
            "c_mask": maskb.astype(np.float32).astype(bf), "c_mats": mats.astype(np.float32).astype(bf)}


def build_program(nseq=2, stop_after=99, debug=False):
    nc = bass.Bass("TRN2", target_bir_lowering=False)
    R = Rec(nc)

    def din(name, shape, dt=F32):
        return nc.dram_tensor(name, list(shape), dt, kind="ExternalInput").ap()

    x_d = din("x", [nseq, SEQ, D])
    meta_d = din("meta", [NMETA, D])
    w_in_d = din("w_in", [D, IN_COLS]).rearrange("(kc p) n -> p kc n", p=128)
    w_bs_d = din("w_bs", [D, D]).rearrange("(kc p) n -> p kc n", p=128)
    w_bd_d = din("w_bd", [D, D]).rearrange("(kc p) n -> p kc n", p=128)
    w_out_d = din("w_out", [D, D]).rearrange("(kc p) n -> p kc n", p=128)
    w_fi_d = din("w_fi", [D, 2 * DFF]).rearrange("(kc p) n -> p kc n", p=128)
    w_fo_d = din("w_fo", [DFF, D]).rearrange("(j p) n -> p j n", p=128)
    g1_d = din("g1_fm", [128, 8])
    g3_d = din("g3_fm", [128, 8])
    g2_d = din("g_post", [1, D])
    g4_d = din("g_postffn", [1, D])
    bg_d = din("bgate_fm", [128, 16])
    sink_d = din("sink_fm", [128, 8])
    lamp_d = din("lam_p", [4, 64])
    subln_d = din("subln", [128, 1])
    cos_d = din("c_cos", [128, TOK], BF16)
    sin_d = din("c_sin", [128, TOK], BF16)
    mask_d = din("c_mask", [128, 1024], BF16)
    mats_d = din("c_mats", [128, 512], BF16)
    out_d = nc.dram_tensor("out", [nseq, SEQ, D], F32, kind="ExternalOutput").ap()

    arena = nc.alloc_sbuf_tensor("arena", [128, ARENA], BF16)
    agran = [Buf(f"ar{i}") for i in range((ARENA + GRAN - 1) // GRAN)]

    class AT:
        def __init__(self, base, Rr, C, dt=BF16):
            self.base, self.R, self.C, self.dt = base, Rr, C, dt
            self.es = 2 if dt == F32 else 1
            assert base + Rr * C * self.es <= ARENA
            v = arena[:, base:base + Rr * C * self.es]
            if dt == F32:
                v = v.bitcast(F32)
            self.v = v.rearrange("p (r c) -> p r c", r=Rr)

        def bufs(self, r0, r1, c0, c1):
            out = []
            for r in range(r0, r1):
                s = self.base + (r * self.C + c0) * self.es
                e = self.base + (r * self.C + c1) * self.es
                out.extend(agran[s // GRAN:(e - 1) // GRAN + 1])
            return out

    def sb(name, shape, dt):
        return nc.alloc_sbuf_tensor(name, list(shape), dt)

    cos_t = sb("cos_t", [128, TOK], BF16)
    sin_t = sb("sin_t", [128, TOK], BF16)
    mask_t = sb("mask_t", [128, 1024], BF16)
    mats_t = sb("mats_t", [128, 512], BF16)
    ident = mats_t[:, 0:128]
    prot = mats_t[:, 128:256]
    ones = mats_t[:, 256:384]
    onesd = mats_t[:, 384:512]
    g1_t = sb("g1_t", [128, 8], F32)
    g3_t = sb("g3_t", [128, 8], F32)
    g2_t = sb("g2_t", [128, D], F32)
    g4_t = sb("g4_t", [128, D], F32)
    bg_t = sb("bg_t", [128, 16], F32)
    es_t = sb("es_t", [128, 8], F32)
    lamd = sb("lamd", [128, 4], F32)
    nlam = sb("nlam", [128, 1], F32)
    g08 = sb("g08", [128, 1], F32)
    cB = Buf("consts")
    msffn_t = sb("msffn", [128, 8, 4], F32)
    msffn_b = Buf("msffn")

    class TPool:
        def __init__(self, name, n, shape, dt):
            self.t = [sb(f"{name}{i}", shape, dt) for i in range(n)]
            self.b = [Buf(f"{name}{i}") for i in range(n)]
            self.i = 0

        def next(self):
            i = self.i
            self.i = (i + 1) % len(self.t)
            return self.t[i], self.b[i]

    xs_p = TPool("xs", 2, [128, D], F32)
    xn_p = TPool("xn", 2, [128, D], BF16)
    junk_p = TPool("junk", 1, [128, D], BF16)
    st_p = TPool("st", 8, [128, 4], F32)
    e_p = TPool("E", 7, [128, 512], BF16)
    ra_p = e_p
    rb_p = e_p
    f_p = TPool("ft", 5, [128, 512], F32)
    w_p = TPool("wsl", 4, [128, 4096], BF16)
    psum = [nc.alloc_psum_tensor(f"ps{i}", [128, 512], F32) for i in range(8)]
    psb = [Buf(f"ps{i}") for i in range(8)]

    class Rot:
        def __init__(self, idxs):
            self.idxs, self.i = list(idxs), 0

        def next(self):
            k = self.idxs[self.i]
            self.i = (self.i + 1) % len(self.idxs)
            return psum[k], psb[k]

    dbg_out = {}

    def dbg_dump(name, ap, bufs, shape, dt=BF16):
        if not debug:
            return
        d = nc.dram_tensor("dbg_" + name, list(shape), dt, kind="ExternalOutput").ap()
        dbg_out[name] = R.dma("sp", CALL("dma_start", out=d, in_=ap), reads=bufs, writes=[Buf("dbgsink")])

    _lt, _lb = f_p.next()
    lamp_t = _lt[:, 0:256].rearrange("p (a b) -> p a b", a=4)
    lamtmp = _lt[:, 256:384].rearrange("p (a b) -> p a b", a=2)
    const_bufs = []

    def cload(dst, src):
        b_ = Buf("c%d" % len(const_bufs))
        const_bufs.append(b_)
        R.dma("sp", CALL("dma_start", out=dst, in_=src), writes=[b_])

    cload(cos_t[:], cos_d)
    cload(sin_t[:], sin_d)
    cload(mask_t[:], mask_d)
    cload(mats_t[:], mats_d)
    cload(g1_t[:], g1_d)
    cload(g3_t[:], g3_d)
    cload(g2_t[:], g2_d.partition_broadcast(128))
    cload(g4_t[:], g4_d.partition_broadcast(128))
    cload(bg_t[:], bg_d)
    cload(es_t[:], sink_d)
    for i in range(4):
        cload(lamp_t[:, i, :], lamp_d[i:i + 1, :].partition_broadcast(128))
    cload(g08[:], subln_d)
    R.op("act", CALL("activation", out=es_t[:], in_=es_t[:], func=AF.Exp), reads=const_bufs + [cB], writes=[cB])
    R.op("dve", CALL("tensor_tensor", out=lamtmp[:, 0, :], in0=lamp_t[:, 0, :], in1=lamp_t[:, 1, :], op=ALU.mult),
         reads=const_bufs + [_lb], writes=[cB, _lb])
    R.op("dve", CALL("tensor_tensor", out=lamtmp[:, 1, :], in0=lamp_t[:, 2, :], in1=lamp_t[:, 3, :], op=ALU.mult),
         reads=const_bufs + [_lb], writes=[cB, _lb])
    R.op("dve", CALL("reduce_sum", out=lamd[:, 0:2], in_=lamtmp, axis=AX.X), reads=const_bufs + [_lb], writes=[cB, _lb])
    R.op("act", CALL("activation", out=lamd[:, 2:4], in_=lamd[:, 0:2], func=AF.Exp), reads=const_bufs + [cB], writes=[cB])
    R.op("dve", CALL("tensor_tensor", out=nlam[:], in0=lamd[:, 3:4], in1=lamd[:, 2:3], op=ALU.subtract),
         reads=const_bufs + [cB], writes=[cB])
    R.op("dve", CALL("tensor_scalar", out=nlam[:], in0=nlam[:], scalar1=-0.2, scalar2=None, op0=ALU.add),
         reads=const_bufs + [cB], writes=[cB])
    R.op("dve", CALL("tensor_scalar", out=g08[:], in0=g08[:], scalar1=0.8, scalar2=None, op0=ALU.mult),
         reads=const_bufs + [cB], writes=[cB])

    def rstd_from_ms(ms_ap, st_t, st_b, extra_reads=()):
        R.op("act", CALL("activation", out=st_t[:, 1:2], in_=ms_ap, func=AF.Ln, bias=RMS_EPS, scale=1.0),
             reads=[st_b] + list(extra_reads), writes=[st_b])
        R.op("act", CALL("activation", out=st_t[:, 2:3], in_=st_t[:, 1:2], func=AF.Exp, scale=-0.5),
             reads=[st_b], writes=[st_b])
        return st_t[:, 2:3]

    def load_w(src3, c0, ncols, wt, wb, dst0=0, kc_n=8, dst_view=None):
        v = dst_view if dst_view is not None else wt[:].rearrange("p (k c) -> p k c", k=8)
        R.dma("pool", CALL("dma_start", out=v[:, 0:kc_n, dst0:dst0 + ncols], in_=src3[:, 0:kc_n, c0:c0 + ncols]),
              writes=[wb])

    def norm_transpose(src_tile, src_b, rows, gain_t, dst, c0, rot):
        jt, jb = junk_p.next()
        st, stb = st_p.next()
        R.op("act", CALL("activation", out=jt[0:rows, :], in_=src_tile[0:rows, :], func=AF.Square, scale=1.0 / 32.0,
                                           accum_out=st[0:rows, 0:1]),
             reads=src_b, writes=[jb, stb])
        R.op("act", CALL("activation", out=st[0:rows, 1:2], in_=st[0:rows, 0:1], func=AF.Ln, bias=RMS_EPS, scale=1.0),
             reads=[stb], writes=[stb])
        R.op("act", CALL("activation", out=st[0:rows, 2:3], in_=st[0:rows, 1:2], func=AF.Exp, scale=-0.5),
             reads=[stb], writes=[stb])
        xt, xb = xn_p.next()
        R.op("dve", CALL("tensor_scalar", out=xt[0:rows, :], in0=src_tile[0:rows, :], scalar1=st[0:rows, 2:3],
                                              scalar2=None, op0=ALU.mult),
             reads=list(src_b) + [stb], writes=[xb])
        bank, bb = rot.next()
        pbf = bank[:].bitcast(BF16)
        for k in range(8):
            R.op("pe", CALL("transpose", pbf[:, k * 128:k * 128 + rows], xt[0:rows, k * 128:(k + 1) * 128],
                                                  ident[0:rows, 0:rows]),
                 reads=[xb, cB], writes=[bb])
        pv = pbf.rearrange("p (k t) -> p k t", k=8)
        gb = gain_t[:, 0:8].unsqueeze(2).to_broadcast([128, 8, rows])
        R.op("dve", CALL("tensor_tensor", out=dst.v[:, 0:8, c0:c0 + rows], in0=pv[:, :, 0:rows], in1=gb, op=ALU.mult),
             reads=[bb, cB], writes=dst.bufs(0, 8, c0, c0 + rows))

    def rope_evac(bank, bb, n, tok0, dsts, scale, rot2):
        at, ab = ra_p.next()
        bt, btb = rb_p.next()
        R.op("dve", CALL("tensor_tensor", out=at[:, 0:n], in0=bank[:, 0:n], in1=cos_t[:, tok0:tok0 + n], op=ALU.mult),
             reads=[bb, cB], writes=[ab])
        R.op("dve", CALL("tensor_tensor", out=bt[:, 0:n], in0=bank[:, 0:n], in1=sin_t[:, tok0:tok0 + n], op=ALU.mult),
             reads=[bb, cB], writes=[btb])

        def stage2():
            b2, b2b = rot2.next()
            R.op("pe", CALL("matmul", b2[:, 0:n], lhsT=ident, rhs=at[:, 0:n], start=True, stop=False),
                 reads=[ab, cB], writes=[b2b])
            R.op("pe", CALL("matmul", b2[:, 0:n], lhsT=prot, rhs=bt[:, 0:n], start=False, stop=True),
                 reads=[btb, cB], writes=[b2b])
            for (p0, p1, dst_ap, dst_bufs) in dsts:
                R.op("act", CALL("activation", out=dst_ap, in_=b2[p0:p1, 0:n], func=AF.Copy, scale=scale),
                     reads=[b2b], writes=dst_bufs)
        return stage2

    TT5 = [(0, 512), (512, 512), (1024, 512), (1536, 512), (2048, 16)]

    def proj_fm(wt, wb, wcol, uT, n_tiles, rot, consume):
        wv = wt[:].rearrange("p (k c) -> p k c", k=8)
        for (tok0, n) in TT5[:n_tiles]:
            bank, bb = rot.next()
            for k in range(8):
                R.op("pe", CALL("matmul",
                    bank[:, 0:n], lhsT=wv[:, k, wcol:wcol + 128], rhs=uT.v[:, k, tok0:tok0 + n],
                    start=(k == 0), stop=(k == 7)),
                    reads=[wb] + uT.bufs(k, k + 1, tok0, tok0 + n), writes=[bb])
            consume(bank, bb, tok0, n)

    pending = []
    xloaded = set()

    def flush_pending(keep=0):
        while len(pending) > keep:
            pending.pop(0)()

    for s in range(nseq):
        uT = AT(A0, 8, TOK)
        qaT = AT(B0, 8, SEQ)
        mT = AT(B0, 8, SEQ)
        kaT = AT(C0, 4, TOK)
        va = AT(C0 + 4 * TOK, 17, 256)
        oT = AT(D0, 8, SEQ)

        rot1 = Rot([0, 1])
        xst = AT(B0, 8, D, F32)

        def emit_xload(sq_, tb):
            if (sq_, tb) in xloaded:
                return
            xloaded.add((sq_, tb))
            rows = 128 if tb < 16 else NMETA
            src = x_d[sq_, tb * 128:(tb + 1) * 128, :] if tb < 16 else meta_d
            R.dma("sp", CALL("dma_start", out=xst.v[0:rows, tb % 8, :], in_=src), writes=xst.bufs(tb % 8, tb % 8 + 1, 0, D))

        for tb in range(8):
            emit_xload(s, tb)
        for tb in range(17):
            rows = 128 if tb < 16 else NMETA
            norm_transpose(xst.v[:, tb % 8, :], xst.bufs(tb % 8, tb % 8 + 1, 0, D), rows, g1_t, uT, tb * 128, rot1)
            if tb + 8 < 17:
                emit_xload(s, tb + 8)
        if s == 0:
            dbg_dump("uT", uT.v[:, :, :], uT.bufs(0, 8, 0, TOK), [128, 8, TOK])
        if stop_after <= 1:
            continue

        rotP = Rot([2, 3, 4, 5])
        rotR = Rot([6, 7])
        wq0, wq0b = w_p.next()
        load_w(w_in_d, QA0, 512, wq0, wq0b)
        wq1, wq1b = w_p.next()
        load_w(w_in_d, QA0 + 512, 512, wq1, wq1b)
        wk, wkb = w_p.next()
        wk5 = wk[:].rearrange("p (k g d c) -> p k g d c", k=8, g=4, d=2)
        for kc in range(8):
            for dd in range(2):
                R.dma("pool", CALL("dma_start",
                    out=wk5[:, kc, :, dd, :],
                    in_=w_in_d[:, kc, KA0:KA0 + 256].rearrange("p (g c) -> p g c", g=4)), writes=[wkb])
        wv_, wvb = w_p.next()
        load_w(w_in_d, VA0, 256, wv_, wvb)

        def consume_rope(dst, r, scale):
            def f(bank, bb, tok0, n):
                pending.append(rope_evac(bank, bb, n, tok0, [(0, 128, dst.v[:, r, tok0:tok0 + n], dst.bufs(r, r + 1, tok0, tok0 + n))],
                                         scale, rotR))
                flush_pending(keep=1)
            return f

        def consume_rope_qz(qz):
            def f(bank, bb, tok0, n):
                dsts = [(0, 64, qz.v[0:64, 0, tok0:tok0 + n], qz.bufs(0, 1, tok0, tok0 + n)),
                        (64, 128, qz.v[64:128, 1, tok0:tok0 + n], qz.bufs(1, 2, tok0, tok0 + n))]
                pending.append(rope_evac(bank, bb, n, tok0, dsts, 0.125, rotR))
                flush_pending(keep=1)
            return f

        for c in range(8):
            wt, wb = (wq0, wq0b) if c < 4 else (wq1, wq1b)
            proj_fm(wt, wb, (c % 4) * 128, uT, 4, rotP, consume_rope(qaT, c, 0.125))
        for g in range(4):
            proj_fm(wk, wkb, g * 128, uT, 5, rotP, consume_rope(kaT, g, 1.0))
        flush_pending()
        wvv = wv_[:].rearrange("p (k c) -> p k c", k=8)
        for tb in range(17):
            rows = 128 if tb < 16 else NMETA
            bank, bb = rotP.next()
            for k in range(8):
                R.op("pe", CALL("matmul",
                    bank[0:rows, 0:256], lhsT=uT.v[:, k, tb * 128:tb * 128 + rows], rhs=wvv[:, k, 0:256],
                    start=(k == 0), stop=(k == 7)),
                    reads=[wvb] + uT.bufs(k, k + 1, tb * 128, tb * 128 + rows), writes=[bb])
            R.op("act", CALL("activation", out=va.v[0:rows, tb, :], in_=bank[0:rows, 0:256],
                                                                          func=AF.Copy),
                 reads=[bb], writes=va.bufs(tb, tb + 1, 0, 256))
        if s == 0:
            dbg_dump("qaT", qaT.v[:, :, :], qaT.bufs(0, 8, 0, SEQ), [128, 8, SEQ])
            dbg_dump("kaT", kaT.v[:, :, :], kaT.bufs(0, 4, 0, TOK), [128, 4, TOK])
            dbg_dump("va", va.v[:, 0:16, :], va.bufs(0, 16, 0, 256), [128, 16, 256])
        if stop_after <= 2:
            continue

        rotS = Rot([0, 1, 2, 3, 4, 5])
        rotO = Rot([6, 7])
        items = []
        for g in range(4):
            for i in range(16):
                kbs = []
                if i > 0:
                    kbs.append((i - 1, 0))
                kbs.append((i, None))
                if i < 15:
                    kbs.append((i + 1, 1))
                kbs.append((16, None))
                for n_, (kb, mk) in enumerate(kbs):
                    items.append((g, i, kb, mk, n_ == 0, n_ == len(kbs) - 1))
        state = {}

        def swa_a(it):
            g, i, kb, mk, first, last = it
            nk = 128 if kb < 16 else NMETA
            k0 = kb * 128
            bE, bEb = rotS.next()
            bO, bOb = rotS.next()
            q0, q1 = i * 128, (i + 1) * 128
            R.op("pe", CALL("matmul", bE[0:nk, 0:256], lhsT=kaT.v[0:64, g, k0:k0 + nk],
                                          rhs=qaT.v[0:64, 2 * g:2 * g + 2, q0:q1], start=True, stop=(mk is None)),
                 reads=kaT.bufs(g, g + 1, k0, k0 + nk) + qaT.bufs(2 * g, 2 * g + 2, q0, q1), writes=[bEb])
            R.op("pe", CALL("matmul", bO[0:nk, 0:256], lhsT=kaT.v[64:128, g, k0:k0 + nk],
                                          rhs=qaT.v[64:128, 2 * g:2 * g + 2, q0:q1], start=True, stop=(mk is None)),
                 reads=kaT.bufs(g, g + 1, k0, k0 + nk) + qaT.bufs(2 * g, 2 * g + 2, q0, q1), writes=[bOb])
            if mk is not None:
                R.op("pe", CALL("matmul", bE[:, 0:256], lhsT=ident, rhs=mask_t[:, mk * 512:mk * 512 + 256],
                                              start=False, stop=True), reads=[cB], writes=[bEb])
                R.op("pe", CALL("matmul", bO[:, 0:256], lhsT=ident, rhs=mask_t[:, mk * 512:mk * 512 + 256],
                                              start=False, stop=True), reads=[cB], writes=[bOb])
            et, eb = e_p.next()
            R.op("act", CALL("activation", out=et[0:nk, 0:256], in_=bE[0:nk, 0:256], func=AF.Exp), reads=[bEb], writes=[eb])
            R.op("act", CALL("activation", out=et[0:nk, 256:512], in_=bO[0:nk, 0:256], func=AF.Exp), reads=[bOb], writes=[eb])
            state[it] = (et, eb, nk)

        import os
        SWA_DBG = int(os.environ.get("SWA_DBG", "9"))

        def swa_b(it):
            g, i, kb, mk, first, last = it
            et, eb, nk = state.pop(it)
            if SWA_DBG <= 1:
                return
            if first:
                state["acc"] = rotO.next()
            acc, accb = state["acc"]
            vrd = va.bufs(kb, kb + 1, g * 64, (g + 1) * 64)
            lv = va.v[0:nk, kb, g * 64:(g + 1) * 64]
            R.op("pe", CALL("matmul", acc[0:64, 0:256], lhsT=lv, rhs=et[0:nk, 0:256], start=first, stop=False),
                 reads=vrd + [eb], writes=[accb])
            R.op("pe", CALL("matmul", acc[64:128, 0:256], lhsT=lv, rhs=et[0:nk, 256:512], start=first, stop=False),
                 reads=vrd + [eb], writes=[accb])
            R.op("pe", CALL("matmul", acc[0:64, 256:512], lhsT=ones[0:nk, 0:64], rhs=et[0:nk, 0:256], start=False,
                                          stop=last), reads=[cB, eb], writes=[accb])
            R.op("pe", CALL("matmul", acc[64:128, 256:512], lhsT=ones[0:nk, 0:64], rhs=et[0:nk, 256:512], start=False,
                                          stop=last), reads=[cB, eb], writes=[accb])
            if last and SWA_DBG > 2:
                dt_, db_ = f_p.next()
                esb = es_t[:, 2 * g:2 * g + 2].unsqueeze(2).to_broadcast([128, 2, 128])
                d3 = dt_[:, 0:256].rearrange("p (j q) -> p j q", j=2)
                R.op("dve", CALL("tensor_tensor", out=d3, in0=acc[:, 256:512].rearrange("p (j q) -> p j q", j=2),
                                                      in1=esb, op=ALU.add), reads=[accb, cB], writes=[db_])
                R.op("dve", CALL("reciprocal", out=dt_[:, 256:512], in_=dt_[:, 0:256]), reads=[db_], writes=[db_])
                q0, q1 = i * 128, (i + 1) * 128
                R.op("dve", CALL("tensor_tensor", out=oT.v[:, 2 * g:2 * g + 2, q0:q1],
                                                      in0=acc[:, 0:256].rearrange("p (j q) -> p j q", j=2),
                                                      in1=dt_[:, 256:512].rearrange("p (j q) -> p j q", j=2), op=ALU.mult),
                     reads=[accb, db_], writes=oT.bufs(2 * g, 2 * g + 2, q0, q1))

        DEPTH = 2
        for n_ in range(len(items) + DEPTH):
            if n_ < len(items):
                swa_a(items[n_])
            if n_ - DEPTH >= 0:
                swa_b(items[n_ - DEPTH])
        if s == 0:
            dbg_dump("oswaT", oT.v[:, :, :], oT.bufs(0, 8, 0, SEQ), [128, 8, SEQ])
        if stop_after <= 3:
            continue

        def merge_phase(w_br_d, gcol0, bcol0, accumulate):
            rotM = Rot([0, 1, 2, 3, 4, 5, 6, 7])
            slots = []
            for half in range(2):
                wa, wab = w_p.next()
                load_w(w_br_d, half * 512, 512, wa, wab)
                wg, wgb = w_p.next()
                load_w(w_in_d, gcol0 + half * 512, 512, wg, wgb)
                slots.append((wa, wab, wg, wgb))
            for m in range(8):
                wa, wab, wg, wgb = slots[m // 4]
                wav = wa[:].rearrange("p (k c) -> p k c", k=8)
                wgv = wg[:].rearrange("p (k c) -> p k c", k=8)
                mc = (m % 4) * 128
                for t in range(4):
                    t0 = t * 512
                    bp, bpb = rotM.next()
                    bg, bgb = rotM.next()
                    for k in range(8):
                        R.op("pe", CALL("matmul", bp[:, :], lhsT=wav[:, k, mc:mc + 128],
                                                                    rhs=oT.v[:, k, t0:t0 + 512], start=(k == 0), stop=(k == 7)),
                             reads=[wab] + oT.bufs(k, k + 1, t0, t0 + 512), writes=[bpb])
                    for k in range(8):
                        R.op("pe", CALL("matmul", bg[:, :], lhsT=wgv[:, k, mc:mc + 128],
                                                                    rhs=uT.v[:, k, t0:t0 + 512], start=(k == 0), stop=(k == 7)),
                             reads=[wgb] + uT.bufs(k, k + 1, t0, t0 + 512), writes=[bgb])
                    gt_, gtb = f_p.next()
                    R.op("act", CALL("activation", out=gt_[:, :], in_=bg[:, :], func=AF.Sigmoid,
                                                                      bias=bg_t[:, bcol0 + m:bcol0 + m + 1], scale=1.0),
                         reads=[bgb, cB], writes=[gtb])
                    mb = mT.bufs(m, m + 1, t0, t0 + 512)
                    if not accumulate:
                        R.op("dve", CALL("tensor_tensor", out=mT.v[:, m, t0:t0 + 512], in0=bp[:, :],
                                                                             in1=gt_[:, :], op=ALU.mult),
                             reads=[bpb, gtb], writes=mb)
                    else:
                        R.op("dve", CALL("tensor_tensor", out=gt_[:, :], in0=bp[:, :], in1=gt_[:, :],
                                                                             op=ALU.mult), reads=[bpb, gtb], writes=[gtb])
                        R.op("dve", CALL("tensor_tensor", out=mT.v[:, m, t0:t0 + 512], in0=mT.v[:, m, t0:t0 + 512],
                                                                      in1=gt_[:, :], op=ALU.add), reads=[gtb] + mb, writes=mb)

        merge_phase(w_bs_d, G0, 0, False)
        if s == 0:
            dbg_dump("m1T", mT.v[:, :, :], mT.bufs(0, 8, 0, SEQ), [128, 8, SEQ])
        if stop_after <= 4:
            continue

        rotP = Rot([0, 1, 2])
        rotR = Rot([3])
        for h in range(8):
            base = C0
            qz = AT(base, 2, SEQ)
            kb_ = AT(base + 2 * SEQ, 1, TOK)
            vb = AT(base + 2 * SEQ + TOK, 17, 256)
            hv = (h % 2) * 128
            if h == 0:
                R.op("dve", CALL("memset", qz.v[64:128, 0, :], 0.0), writes=qz.bufs(0, 1, 0, SEQ))
                R.op("dve", CALL("memset", qz.v[0:64, 1, :], 0.0), writes=qz.bufs(1, 2, 0, SEQ))
            wt, wb = w_p.next()
            wv3 = wt[:].rearrange("p (k c) -> p k c", k=8)
            load_w(w_in_d, QB0 + h * 128, 128, wt, wb, dst0=0)
            load_w(w_in_d, KB0 + h * 128, 128, wt, wb, dst0=128)
            if h % 2 == 0:
                load_w(w_in_d, VB0 + h * 128, 256, wt, wb, dst0=256)
            rotP = Rot([0, 1, 2])
            rotR = Rot([7])
            proj_fm(wt, wb, 0, uT, 4, rotP, consume_rope_qz(qz))
            proj_fm(wt, wb, 128, uT, 5, rotP, consume_rope(kb_, 0, 1.0))
            flush_pending()
            for tb in (range(17) if h % 2 == 0 else ()):
                rows = 128 if tb < 16 else NMETA
                bank, bb = rotP.next()
                for k in range(8):
                    R.op("pe", CALL("matmul",
                        bank[0:rows, 0:256], lhsT=uT.v[:, k, tb * 128:tb * 128 + rows], rhs=wv3[:, k, 256:512],
                        start=(k == 0), stop=(k == 7)),
                        reads=[wb] + uT.bufs(k, k + 1, tb * 128, tb * 128 + rows), writes=[bb])
                R.op("dve", CALL("tensor_copy", out=vb.v[0:rows, tb, :], in_=bank[0:rows, 0:256]),
                     reads=[bb], writes=vb.bufs(tb, tb + 1, 0, 256))
            if s == 0 and h == 0:
                dbg_dump("qz0", qz.v[:, :, :], qz.bufs(0, 2, 0, SEQ), [128, 2, SEQ])
                dbg_dump("kb0", kb_.v[:, :, :], kb_.bufs(0, 1, 0, TOK), [128, 1, TOK])
                dbg_dump("vb0", vb.v[:, 0:16, 0:128], vb.bufs(0, 16, 0, 128), [128, 16, 128])
            rotS = Rot([0, 1, 2, 3])
            accs = {0: (4, 5), 1: (6, 7)}
            deferred = []
            ditems = [(qt, c, kbi) for qt in range(4) for c in range(2) for kbi in range(17)]
            dstate = {}

            def dif_a(it):
                qt, c, kbi = it
                nk = 128 if kbi < 16 else NMETA
                k0 = kbi * 128
                bank, bb = rotS.next()
                R.op("pe", CALL("matmul", bank[0:nk, :], lhsT=kb_.v[:, 0, k0:k0 + nk],
                                              rhs=qz.v[:, c, qt * 512:(qt + 1) * 512], start=True, stop=True),
                     reads=kb_.bufs(0, 1, k0, k0 + nk) + qz.bufs(c, c + 1, qt * 512, (qt + 1) * 512), writes=[bb])
                et, eb = e_p.next()
                R.op("act", CALL("activation", out=et[0:nk, :], in_=bank[0:nk, :], func=AF.Exp), reads=[bb], writes=[eb])
                dstate[it] = (et, eb, nk)

            def dif_b(it):
                qt, c, kbi = it
                et, eb, nk = dstate.pop(it)
                first, last = kbi == 0, kbi == 16
                oi, si = accs[c]
                R.op("pe", CALL("matmul", psum[oi][:, :], lhsT=vb.v[0:nk, kbi, hv:hv + 128], rhs=et[0:nk, :], start=first, stop=last),
                     reads=vb.bufs(kbi, kbi + 1, hv, hv + 128) + [eb], writes=[psb[oi]])
                R.op("pe", CALL("matmul", psum[si][:, :], lhsT=ones[0:nk, :], rhs=et[0:nk, :], start=first, stop=last),
                     reads=[cB, eb], writes=[psb[si]])
                if not last:
                    return
                rt, rb = f_p.next()
                R.op("dve", CALL("reciprocal", out=rt[:, :], in_=psum[si][:, :]), reads=[psb[si]], writes=[rb])
                R.op("dve", CALL("tensor_tensor", out=rt[:, :], in0=psum[oi][:, :], in1=rt[:, :], op=ALU.mult),
                     reads=[psb[oi], rb], writes=[rb])
                if c == 0:
                    dstate["t0"] = (rt, rb)
                    return
                t0t, t0b = dstate.pop("t0")
                R.op("dve", CALL("scalar_tensor_tensor", out=rt[:, :], in0=rt[:, :], scalar=nlam[:, 0:1], in1=t0t[:, :],
                                                             op0=ALU.mult, op1=ALU.add), reads=[rb, t0b, cB], writes=[rb])
                sq, sqb = e_p.next()
                R.op("dve", CALL("tensor_tensor", out=sq[:, :], in0=rt[:, :], in1=rt[:, :], op=ALU.mult), reads=[rb], writes=[sqb])

                def tail(qt=qt, rt=rt, rb=rb, t0t=t0t, t0b=t0b, sq=sq, sqb=sqb):
                    ssk, ssb = rotS.next()
                    R.op("pe", CALL("matmul", ssk[:, :], lhsT=onesd, rhs=sq[:, :], start=True, stop=True),
                         reads=[cB, sqb], writes=[ssb])
                    R.op("act", CALL("activation", out=t0t[:, :], in_=ssk[:, :], func=AF.Ln, bias=RMS_EPS, scale=1.0),
                         reads=[ssb], writes=[t0b])
                    R.op("act", CALL("activation", out=t0t[:, :], in_=t0t[:, :], func=AF.Exp, scale=-0.5),
                         reads=[t0b], writes=[t0b])
                    R.op("dve", CALL("scalar_tensor_tensor", out=oT.v[:, h, qt * 512:(qt + 1) * 512], in0=rt[:, :],
                                                                 scalar=g08[:, 0:1], in1=t0t[:, :], op0=ALU.mult, op1=ALU.mult),
                         reads=[rb, t0b, cB], writes=oT.bufs(h, h + 1, qt * 512, (qt + 1) * 512))
                deferred.append([6, tail])

            DD = 3
            for n_ in range(len(ditems) + DD):
                if n_ < len(ditems):
                    dif_a(ditems[n_])
                if n_ - DD >= 0:
                    dif_b(ditems[n_ - DD])
                for dfr in deferred:
                    dfr[0] -= 1
                while deferred and deferred[0][0] <= 0:
                    deferred.pop(0)[1]()
            while deferred:
                deferred.pop(0)[1]()
        if s == 0:
            dbg_dump("odiffT", oT.v[:, :, :], oT.bufs(0, 8, 0, SEQ), [128, 8, SEQ])
        if stop_after <= 5:
            continue
        merge_phase(w_bd_d, G0 + 1024, 8, True)
        if s == 0:
            dbg_dump("mT", mT.v[:, :, :], mT.bufs(0, 8, 0, SEQ), [128, 8, SEQ])
        if stop_after <= 6:
            continue

        for hf in range(2):
            hT = AT(A0, 8, D, F32)
            u2T = AT(C0, 8, 1024)
            aT = AT(C0 + 8192, NJ, 1024)
            fst = AT(C0, 8, 512, F32)
            rotY = Rot([0, 1, 2, 3])
            rotT = Rot([4, 5])
            wo0, wo0b = w_p.next()
            load_w(w_out_d, 0, 512, wo0, wo0b)
            wo1, wo1b = w_p.next()
            load_w(w_out_d, 512, 512, wo1, wo1b)
            wov = [wo0[:].rearrange("p (k c) -> p k c", k=8), wo1[:].rearrange("p (k c) -> p k c", k=8)]
            wob = [wo0b, wo1b]
            nt_q = []
            for tl in range(8):
                tb = hf * 8 + tl
                q0 = tb * 128
                ybank = []
                for ch in range(2):
                    bank, bb = rotY.next()
                    for k in range(8):
                        R.op("pe", CALL("matmul", bank[:, :], lhsT=mT.v[:, k, q0:q0 + 128],
                                                                             rhs=wov[ch][:, k, :], start=(k == 0), stop=(k == 7)),
                             reads=[wob[ch]] + mT.bufs(k, k + 1, q0, q0 + 128), writes=[bb])
                    ybank.append((bank, bb))
                if len(nt_q) >= 2:
                    nt_q.pop(0)()
                st, stb = st_p.next()
                for ch in range(2):
                    jt, jb = junk_p.next()
                    R.op("act", CALL("activation", out=jt[:, 0:512], in_=ybank[ch][0][:, :], func=AF.Square,
                                                                    scale=1.0 / 32.0, accum_out=st[:, ch:ch + 1]),
                         reads=[ybank[ch][1]], writes=[jb, stb])
                R.op("dve", CALL("tensor_tensor", out=st[:, 0:1], in0=st[:, 0:1], in1=st[:, 1:2], op=ALU.add),
                     reads=[stb], writes=[stb])
                R.op("act", CALL("activation", out=st[:, 1:2], in_=st[:, 0:1], func=AF.Ln, bias=RMS_EPS, scale=1.0),
                     reads=[stb], writes=[stb])
                R.op("act", CALL("activation", out=st[:, 2:3], in_=st[:, 1:2], func=AF.Exp, scale=-0.5),
                     reads=[stb], writes=[stb])
                xt, xb = xs_p.next()
                R.dma("sp", CALL("dma_start", out=xt[:, :], in_=x_d[s, q0:q0 + 128, :]), writes=[xb])
                for ch in range(2):
                    c0 = ch * 512
                    tt, ttb = f_p.next()
                    R.op("dve", CALL("scalar_tensor_tensor",
                        out=tt[:, :], in0=ybank[ch][0][:, :], scalar=st[:, 2:3], in1=g2_t[:, c0:c0 + 512],
                        op0=ALU.mult, op1=ALU.mult), reads=[ybank[ch][1], stb, cB], writes=[ttb])
                    R.op("dve", CALL("tensor_tensor", out=hT.v[:, tl, c0:c0 + 512], in0=tt[:, :],
                                                                       in1=xt[:, c0:c0 + 512], op=ALU.add),
                         reads=[ttb, xb], writes=hT.bufs(tl, tl + 1, c0, c0 + 512))
                nt_q.append(lambda tl=tl: norm_transpose(hT.v[:, tl, :], hT.bufs(tl, tl + 1, 0, D), 128, g3_t, u2T, tl * 128, rotT))
            while nt_q:
                nt_q.pop(0)()
            if s == 0 and hf == 0:
                dbg_dump("h0", hT.v[:, :, :], hT.bufs(0, 8, 0, D), [128, 8, D], F32)
                dbg_dump("u2T0", u2T.v[:, :, :], u2T.bufs(0, 8, 0, 1024), [128, 8, 1024])
            if stop_after <= 7:
                continue
            if hf == 1 and s + 1 < nseq:
                for tb in range(8):
                    emit_xload(s + 1, tb)
            rotF = Rot([0, 1, 2, 3, 4, 5, 6, 7])
            wcur = None
            for j in range(NJ):
                if j % 2 == 0:
                    wcur = w_p.next()
                    for jj in range(2):
                        if j + jj < NJ:
                            load_w(w_fi_d, (j + jj) * 128, 128, wcur[0], wcur[1], dst0=jj * 256)
                            load_w(w_fi_d, DFF + (j + jj) * 128, 128, wcur[0], wcur[1], dst0=jj * 256 + 128)
                wt, wb = wcur
                wv3 = wt[:].rearrange("p (k c) -> p k c", k=8)
                cg = (j % 2) * 256
                for t in range(2):
                    t0 = t * 512
                    bgk, bgb = rotF.next()
                    buk, bub = rotF.next()
                    for k in range(8):
                        R.op("pe", CALL("matmul", bgk[:, :], lhsT=wv3[:, k, cg:cg + 128],
                                                                      rhs=u2T.v[:, k, t0:t0 + 512], start=(k == 0), stop=(k == 7)),
                             reads=[wb] + u2T.bufs(k, k + 1, t0, t0 + 512), writes=[bgb])
                    for k in range(8):
                        R.op("pe", CALL("matmul", buk[:, :], lhsT=wv3[:, k, cg + 128:cg + 256],
                                                                      rhs=u2T.v[:, k, t0:t0 + 512], start=(k == 0), stop=(k == 7)),
                             reads=[wb] + u2T.bufs(k, k + 1, t0, t0 + 512), writes=[bub])
                    sg, sgb = f_p.next()
                    R.op("act", CALL("activation", out=sg[:, :], in_=bgk[:, :], func=AF.Silu),
                         reads=[bgb], writes=[sgb])
                    R.op("dve", CALL("tensor_tensor", out=aT.v[:, j, t0:t0 + 512], in0=buk[:, :],
                                                                                     in1=sg[:, :], op=ALU.mult),
                         reads=[bub, sgb], writes=aT.bufs(j, j + 1, t0, t0 + 512))
            if s == 0 and hf == 0:
                dbg_dump("aT0", aT.v[:, :, :], aT.bufs(0, NJ, 0, 1024), [128, NJ, 1024])
            if stop_after <= 8:
                continue
            rotO2 = Rot([0, 1, 2, 3])
            ms_t, msb = msffn_t, msffn_b
            for ch in range(2):
                wsl = []
                for (j0, j1) in ((0, 8), (8, 16), (16, 22)):
                    wt, wb = w_p.next()
                    wv3 = wt[:].rearrange("p (k c) -> p k c", k=8)
                    R.dma("pool", CALL("dma_start", out=wv3[:, 0:j1 - j0, :],
                                                                              in_=w_fo_d[:, j0:j1, ch * 512:(ch + 1) * 512]),
                          writes=[wb])
                    wsl.append((wv3, wb, j0, j1))
                for tl in range(8):
                    tb = hf * 8 + tl
                    bank, bb = rotO2.next()
                    for (wv3, wb, j0, j1) in wsl:
                        for j in range(j0, j1):
                            R.op("pe", CALL("matmul",
                                bank[:, :], lhsT=aT.v[:, j, tl * 128:(tl + 1) * 128], rhs=wv3[:, j - j0, :],
                                start=(j == 0), stop=(j == NJ - 1)),
                                reads=[wb] + aT.bufs(j, j + 1, tl * 128, (tl + 1) * 128), writes=[bb])
                    jt, jb = junk_p.next()
                    R.op("act", CALL("activation",
                        out=jt[:, 0:512], in_=bank[:, :], func=AF.Square, scale=1.0 / 32.0, accum_out=ms_t[:, tl, ch:ch + 1]),
                        reads=[bb], writes=[jb, msb])
                    if ch == 0:
                        R.op("act", CALL("activation", out=fst.v[:, tl, :], in_=bank[:, :], func=AF.Copy),
                             reads=[bb], writes=fst.bufs(tl, tl + 1, 0, 512))
                        continue
                    R.op("dve", CALL("tensor_tensor", out=ms_t[:, tl, 0:1], in0=ms_t[:, tl, 0:1], in1=ms_t[:, tl, 1:2],
                                                                 op=ALU.add), reads=[msb], writes=[msb])
                    R.op("act", CALL("activation", out=ms_t[:, tl, 2:3], in_=ms_t[:, tl, 0:1], func=AF.Ln, bias=RMS_EPS,
                                                              scale=1.0), reads=[msb], writes=[msb])
                    R.op("act", CALL("activation", out=ms_t[:, tl, 3:4], in_=ms_t[:, tl, 2:3], func=AF.Exp, scale=-0.5),
                         reads=[msb], writes=[msb])
                    ot, ob = xs_p.next()
                    for c2 in range(2):
                        c0 = c2 * 512
                        src = fst.v[:, tl, :] if c2 == 0 else bank[:, :]
                        srcb = fst.bufs(tl, tl + 1, 0, 512) if c2 == 0 else [bb]
                        tt, ttb = f_p.next()
                        R.op("dve", CALL("scalar_tensor_tensor",
                            out=tt[:, :], in0=src, scalar=ms_t[:, tl, 3:4], in1=g4_t[:, c0:c0 + 512], op0=ALU.mult,
                            op1=ALU.mult), reads=srcb + [msb, cB], writes=[ttb])
                        R.op("dve", CALL("tensor_tensor",
                            out=ot[:, c0:c0 + 512], in0=tt[:, :], in1=hT.v[:, tl, c0:c0 + 512], op=ALU.add),
                            reads=[ttb] + hT.bufs(tl, tl + 1, c0, c0 + 512), writes=[ob])
                    R.dma("sp", CALL("dma_start", out=out_d[s, tb * 128:(tb + 1) * 128, :], in_=ot[:, :]),
                          reads=[ob], writes=[Buf("outsink")])

    fin = Buf("fin")
    tail_deps = [ins for ins in R.streams["sp"] if ins.is_dma]
    fi = R.op("sp", None, reads=[], writes=[fin])
    for d in tail_deps:
        fi.deps.append(d)
        d.needs_inc = True
    R.finalize_and_emit()
    return nc, R


N_CORES = 8
_PROG_CACHE = {}


def make_in_maps(inputs, nseq=2, n_cores=N_CORES):
    f = lambda a: np.ascontiguousarray(np.asarray(a, dtype=np.float32))
    x = f(inputs["x"])
    sink = f(inputs["attn_sink"])[0]
    p = np.arange(128)
    head_of = 2 * np.arange(8)[None, :] + (p[:, None] // 64)
    shared = {
        "meta": f(inputs["meta_tokens"]),
        "w_in": f(inputs["w_in"])[0],
        "w_bs": f(inputs["w_branch_swa"])[0],
        "w_bd": f(inputs["w_branch_diff"])[0],
        "w_out": f(inputs["w_out"])[0],
        "w_fi": f(inputs["w_ffn_in"])[0],
        "w_fo": f(inputs["w_ffn_out"])[0],
        "g1_fm": f(f(inputs["pre_mix_gain"])[0].reshape(8, 128).T),
        "g3_fm": f(f(inputs["pre_ffn_gain"])[0].reshape(8, 128).T),
        "g_post": f(inputs["post_mix_gain"]).reshape(1, D),
        "g_postffn": f(inputs["post_ffn_gain"]).reshape(1, D),
        "bgate_fm": f(f(inputs["b_gate"])[0].reshape(16, 128).T),
        "sink_fm": f(sink[head_of]),
        "lam_p": f(np.stack([f(inputs["lambda_q1"])[0], f(inputs["lambda_k1"])[0],
                             f(inputs["lambda_q2"])[0], f(inputs["lambda_k2"])[0]], 0)),
        "subln": f(f(inputs["diff_subln_gain"])[0].reshape(128, 1)),
    }
    shared.update(host_constants())
    maps = []
    for c in range(n_cores):
        m = dict(shared)
        m["x"] = np.ascontiguousarray(x[c * nseq:(c + 1) * nseq])
        maps.append(m)
    return maps


def kernel(**inputs):
    if "prog" not in _PROG_CACHE:
        _PROG_CACHE["prog"] = build_program(nseq=2)[0]
    nc = _PROG_CACHE["prog"]
    in_maps = make_in_maps(inputs, nseq=2)
    res = run_bass_kernel_spmd(nc, in_maps, core_ids=list(range(N_CORES)))
    out = np.concatenate([np.asarray(r["out"]) for r in res.results], axis=0)
    return out.astype(np.float32, copy=False)
```

```python
import numpy as np
import ml_dtypes
from contextlib import ExitStack
import concourse.bass as bass
import concourse.mybir as mybir
from concourse.bass_utils import run_bass_kernel_spmd

F32 = mybir.dt.float32
BF16 = mybir.dt.bfloat16
AF = mybir.ActivationFunctionType
ALU = mybir.AluOpType
AX = mybir.AxisListType

EPOCH = 16000
N_DMA_SEMS = 14


class Buf:
    __slots__ = ("name", "writer", "readers", "dma_readers")

    def __init__(self, name):
        self.name = name
        self.writer = None
        self.readers = {}
        self.dma_readers = []


class Instr:
    __slots__ = ("eng", "fn", "deps", "is_dma", "seq", "needs_inc", "sem", "val", "dma_slot")

    def __init__(self, eng, fn, is_dma):
        self.eng = eng
        self.fn = fn
        self.deps = []
        self.is_dma = is_dma
        self.seq = -1
        self.needs_inc = False
        self.sem = None
        self.val = 0
        self.dma_slot = -1


ENGS = ("pe", "act", "dve", "pool", "sp")


def CALL(method, *args, **kwargs):
    return (method, args, kwargs)


class Rec:
    def __init__(self, nc):
        self.nc = nc
        self.streams = {e: [] for e in ENGS}
        self.dma_rr = {e: 0 for e in ENGS}
        self.dma_last = {e: [None] * N_DMA_SEMS for e in ENGS}
        self.n_instr = 0

    def _add(self, eng, fn, reads, writes, is_dma):
        ins = Instr(eng, fn, is_dma)
        deps = {}
        for b in reads:
            w = b.writer
            if w is not None:
                deps[id(w)] = w
        for b in writes:
            w = b.writer
            if w is not None:
                deps[id(w)] = w
            for r in b.readers.values():
                deps[id(r)] = r
            for r in b.dma_readers:
                deps[id(r)] = r
        if is_dma:
            slot = self.dma_rr[eng]
            self.dma_rr[eng] = (slot + 1) % N_DMA_SEMS
            prev = self.dma_last[eng][slot]
            if prev is not None:
                deps[id(prev)] = prev
            self.dma_last[eng][slot] = ins
            ins.dma_slot = slot
        for d in deps.values():
            if d is ins:
                continue
            if not is_dma and not d.is_dma and d.eng == eng:
                if eng == "pe":
                    continue
                wrote = False
                for b in reads:
                    if b.writer is d:
                        wrote = True
                for b in writes:
                    if b.writer is d:
                        wrote = True
                if not wrote:
                    continue
            ins.deps.append(d)
            d.needs_inc = True
        for b in reads:
            if is_dma:
                b.dma_readers.append(ins)
            else:
                b.readers[eng] = ins
        for b in writes:
            b.writer = ins
            b.readers = {}
            b.dma_readers = []
        ins.seq = len(self.streams[eng])
        self.streams[eng].append(ins)
        self.n_instr += 1
        return ins

    def op(self, eng, fn, reads=(), writes=()):
        return self._add(eng, fn, reads, writes, False)

    def dma(self, eng, fn, reads=(), writes=()):
        ins = self._add(eng, fn, reads, writes, True)
        ins.needs_inc = True
        return ins

    def finalize_and_emit(self):
        nc = self.nc
        self.sems = {}
        for e in ENGS:
            cnt = 0
            cur = None
            for ins in self.streams[e]:
                if ins.is_dma or not ins.needs_inc:
                    continue
                if cnt % EPOCH == 0:
                    cur = nc.alloc_semaphore(f"s_{e}_{cnt // EPOCH}")
                ins.sem = cur
                ins.val = cnt % EPOCH + 1
                cnt += 1
        for e in ENGS:
            if self.dma_rr[e] == 0 and self.dma_last[e][0] is None:
                continue
            sems = [nc.alloc_semaphore(f"d_{e}_{i}") for i in range(N_DMA_SEMS)]
            counts = [0] * N_DMA_SEMS
            for ins in self.streams[e]:
                if ins.is_dma:
                    counts[ins.dma_slot] += 16
                    ins.sem = sems[ins.dma_slot]
                    ins.val = counts[ins.dma_slot]
        self.n_waits = 0

        def replay(ename, eobj):
            waited_seq = {}
            waited_dma = {}
            for ins in self.streams[ename]:
                waits = []
                for d in ins.deps:
                    if d.is_dma:
                        k = id(d.sem)
                        if waited_dma.get(k, 0) >= d.val:
                            continue
                        waited_dma[k] = d.val
                    else:
                        if waited_seq.get(d.eng, -1) >= d.seq:
                            continue
                        waited_seq[d.eng] = d.seq
                    waits.append((d.sem, d.val))
                best = {}
                for (sm, v) in waits:
                    k = id(sm)
                    if k not in best or best[k][1] < v:
                        best[k] = (sm, v)
                waits = list(best.values())
                if ins.fn is None:
                    for (sm, v) in waits:
                        eobj.wait_ge(sm, v)
                        self.n_waits += 1
                    continue
                for (sm, v) in waits[1:]:
                    eobj.wait_ge(sm, v)
                    self.n_waits += 1
                m, a, kw = ins.fn
                bi = getattr(eobj, m)(*a, **kw)
                if waits:
                    bi._wait_ge(waits[0][0], waits[0][1])
                if ins.is_dma:
                    bi.then_inc(ins.sem, 16)
                elif ins.needs_inc:
                    bi.then_inc(ins.sem, 1)

        with nc.Block() as block:
            @block.tensor
            def _(t):
                replay("pe", t)

            @block.scalar
            def _(a):
                replay("act", a)

            @block.vector
            def _(v):
                replay("dve", v)

            @block.gpsimd
            def _(g):
                replay("pool", g)

            @block.sync
            def _(s):
                replay("sp", s)


D = 1024
SEQ = 2048
NMETA = 16
TOK = SEQ + NMETA
DFF = 2816
NJ = DFF // 128
IN_COLS = 6656
QA0, KA0, VA0, QB0, KB0, VB0, G0 = 0, 1024, 1280, 1536, 2560, 3584, 4608
RMS_EPS = 1e-6
A0 = 0
B0 = A0 + 8 * TOK
C0 = B0 + 8 * SEQ
D0 = C0 + 14336
ARENA = D0 + 8 * SEQ
GRAN = 128
MASKV = -30000.0


def host_constants():
    pos = np.concatenate([np.arange(SEQ) + NMETA, np.arange(NMETA)]).astype(np.float64)
    inv_freq = 10000.0 ** (-(np.arange(0, 64, 2, dtype=np.float64)) / 64.0)
    p = np.arange(128)
    ang = inv_freq[p % 32][:, None] * pos[None, :]
    cosT = np.cos(ang)
    sgn = np.where((p % 64) < 32, 1.0, -1.0)[:, None]
    sinP = np.sin(ang) * sgn
    b = np.arange(128)[:, None]
    a = np.arange(128)[None, :]
    lo = np.where(a <= b, 0.0, MASKV)
    hi = np.where(b <= a, 0.0, MASKV)
    maskb = np.concatenate([np.tile(lo, (1, 4)), np.tile(hi, (1, 4))], axis=1)
    ident = np.eye(128)
    prot = np.zeros((128, 128))
    prot[p ^ 32, p] = 1.0
    ones = np.ones((128, 128))
    onesd = np.full((128, 128), 1.0 / 128.0)
    mats = np.concatenate([ident, prot, ones, onesd], axis=1)
    bf = ml_dtypes.bfloat16
    return {"c_cos": cosT.astype(np.float32).astype(bf), "c_sin": sinP.astype(np.float32).astype(bf),
            "c_mask": maskb.astype(np.float32).astype(bf), "c_mats": mats.astype(np.float32).astype(bf)}


def build_program(nseq=2, stop_after=99, debug=False):
    nc = bass.Bass("TRN2", target_bir_lowering=False)
    R = Rec(nc)

    def din(name, shape, dt=F32):
        return nc.dram_tensor(name, list(shape), dt, kind="ExternalInput").ap()

    x_d = din("x", [nseq, SEQ, D])
    meta_d = din("meta", [NMETA, D])
    w_in_d = din("w_in", [D, IN_COLS]).rearrange("(kc p) n -> p kc n", p=128)
    w_bs_d = din("w_bs", [D, D]).rearrange("(kc p) n -> p kc n", p=128)
    w_bd_d = din("w_bd", [D, D]).rearrange("(kc p) n -> p kc n", p=128)
    w_out_d = din("w_out", [D, D]).rearrange("(kc p) n -> p kc n", p=128)
    w_fi_d = din("w_fi", [D, 2 * DFF]).rearrange("(kc p) n -> p kc n", p=128)
    w_fo_d = din("w_fo", [DFF, D]).rearrange("(j p) n -> p j n", p=128)
    g1_d = din("g1_fm", [128, 8])
    g3_d = din("g3_fm", [128, 8])
    g2_d = din("g_post", [1, D])
    g4_d = din("g_postffn", [1, D])
    bg_d = din("bgate_fm", [128, 16])
    sink_d = din("sink_fm", [128, 8])
    lamp_d = din("lam_p", [4, 64])
    subln_d = din("subln", [128, 1])
    cos_d = din("c_cos", [128, TOK], BF16)
    sin_d = din("c_sin", [128, TOK], BF16)
    mask_d = din("c_mask", [128, 1024], BF16)
    mats_d = din("c_mats", [128, 512], BF16)
    out_d = nc.dram_tensor("out", [nseq, SEQ, D], F32, kind="ExternalOutput").ap()

    arena = nc.alloc_sbuf_tensor("arena", [128, ARENA], BF16)
    agran = [Buf(f"ar{i}") for i in range((ARENA + GRAN - 1) // GRAN)]

    class AT:
        def __init__(self, base, Rr, C, dt=BF16):
            self.base, self.R, self.C, self.dt = base, Rr, C, dt
            self.es = 2 if dt == F32 else 1
            assert base + Rr * C * self.es <= ARENA
            v = arena[:, base:base + Rr * C * self.es]
            if dt == F32:
                v = v.bitcast(F32)
            self.v = v.rearrange("p (r c) -> p r c", r=Rr)

        def bufs(self, r0, r1, c0, c1):
            out = []
            for r in range(r0, r1):
                s = self.base + (r * self.C + c0) * self.es
                e = self.base + (r * self.C + c1) * self.es
                out.extend(agran[s // GRAN:(e - 1) // GRAN + 1])
            return out

    def sb(name, shape, dt):
        return nc.alloc_sbuf_tensor(name, list(shape), dt)

    cos_t = sb("cos_t", [128, TOK], BF16)
    sin_t = sb("sin_t", [128, TOK], BF16)
    mask_t = sb("mask_t", [128, 1024], BF16)
    mats_t = sb("mats_t", [128, 512], BF16)
    ident = mats_t[:, 0:128]
    prot = mats_t[:, 128:256]
    ones = mats_t[:, 256:384]
    onesd = mats_t[:, 384:512]
    g1_t = sb("g1_t", [128, 8], F32)
    g3_t = sb("g3_t", [128, 8], F32)
    g2_t = sb("g2_t", [128, D], F32)
    g4_t = sb("g4_t", [128, D], F32)
    bg_t = sb("bg_t", [128, 16], F32)
    es_t = sb("es_t", [128, 8], F32)
    lamd = sb("lamd", [128, 4], F32)
    nlam = sb("nlam", [128, 1], F32)
    g08 = sb("g08", [128, 1], F32)
    cB = Buf("consts")
    msffn_t = sb("msffn", [128, 8, 4], F32)
    msffn_b = Buf("msffn")

    class TPool:
        def __init__(self, name, n, shape, dt):
            self.t = [sb(f"{name}{i}", shape, dt) for i in range(n)]
            self.b = [Buf(f"{name}{i}") for i in range(n)]
            self.i = 0

        def next(self):
            i = self.i
            self.i = (i + 1) % len(self.t)
            return self.t[i], self.b[i]

    xs_p = TPool("xs", 2, [128, D], F32)
    xn_p = TPool("xn", 2, [128, D], BF16)
    junk_p = TPool("junk", 1, [128, D], BF16)
    st_p = TPool("st", 8, [128, 4], F32)
    e_p = TPool("E", 7, [128, 512], BF16)
    ra_p = e_p
    rb_p = e_p
    f_p = TPool("ft", 4, [128, 512], F32)
    sq_p = TPool("sq", 2, [128, 512], BF16)
    w_p = TPool("wsl", 4, [128, 4096], BF16)
    psum = [nc.alloc_psum_tensor(f"ps{i}", [128, 512], F32) for i in range(8)]
    psb = [Buf(f"ps{i}") for i in range(8)]

    class Rot:
        def __init__(self, idxs):
            self.idxs, self.i = list(idxs), 0

        def next(self):
            k = self.idxs[self.i]
            self.i = (self.i + 1) % len(self.idxs)
            return psum[k], psb[k]

    dbg_out = {}

    def dbg_dump(name, ap, bufs, shape, dt=BF16):
        if not debug:
            return
        d = nc.dram_tensor("dbg_" + name, list(shape), dt, kind="ExternalOutput").ap()
        dbg_out[name] = R.dma("sp", CALL("dma_start", out=d, in_=ap), reads=bufs, writes=[Buf("dbgsink")])

    _lt, _lb = f_p.next()
    lamp_t = _lt[:, 0:256].rearrange("p (a b) -> p a b", a=4)
    lamtmp = _lt[:, 256:384].rearrange("p (a b) -> p a b", a=2)
    def cload(dst, src):
        R.dma("sp", CALL("dma_start", out=dst, in_=src), writes=[cB, _lb])

    cload(cos_t[:], cos_d)
    cload(sin_t[:], sin_d)
    cload(mask_t[:], mask_d)
    cload(mats_t[:], mats_d)
    cload(g1_t[:], g1_d)
    cload(g3_t[:], g3_d)
    cload(g2_t[:], g2_d.partition_broadcast(128))
    cload(g4_t[:], g4_d.partition_broadcast(128))
    cload(bg_t[:], bg_d)
    cload(es_t[:], sink_d)
    for i in range(4):
        cload(lamp_t[:, i, :], lamp_d[i:i + 1, :].partition_broadcast(128))
    cload(g08[:], subln_d)
    R.op("act", CALL("activation", out=es_t[:], in_=es_t[:], func=AF.Exp), reads=[cB], writes=[cB])
    R.op("dve", CALL("tensor_tensor", out=lamtmp[:, 0, :], in0=lamp_t[:, 0, :], in1=lamp_t[:, 1, :], op=ALU.mult),
         reads=[cB, _lb], writes=[cB, _lb])
    R.op("dve", CALL("tensor_tensor", out=lamtmp[:, 1, :], in0=lamp_t[:, 2, :], in1=lamp_t[:, 3, :], op=ALU.mult),
         reads=[cB, _lb], writes=[cB, _lb])
    R.op("dve", CALL("reduce_sum", out=lamd[:, 0:2], in_=lamtmp, axis=AX.X), reads=[cB, _lb], writes=[cB, _lb])
    R.op("act", CALL("activation", out=lamd[:, 2:4], in_=lamd[:, 0:2], func=AF.Exp), reads=[cB], writes=[cB])
    R.op("dve", CALL("tensor_tensor", out=nlam[:], in0=lamd[:, 3:4], in1=lamd[:, 2:3], op=ALU.subtract),
         reads=[cB], writes=[cB])
    R.op("dve", CALL("tensor_scalar", out=nlam[:], in0=nlam[:], scalar1=-0.2, scalar2=None, op0=ALU.add),
         reads=[cB], writes=[cB])
    R.op("dve", CALL("tensor_scalar", out=g08[:], in0=g08[:], scalar1=0.8, scalar2=None, op0=ALU.mult),
         reads=[cB], writes=[cB])

    def rstd_from_ms(ms_ap, st_t, st_b, extra_reads=()):
        R.op("act", CALL("activation", out=st_t[:, 1:2], in_=ms_ap, func=AF.Ln, bias=RMS_EPS, scale=1.0),
             reads=[st_b] + list(extra_reads), writes=[st_b])
        R.op("act", CALL("activation", out=st_t[:, 2:3], in_=st_t[:, 1:2], func=AF.Exp, scale=-0.5),
             reads=[st_b], writes=[st_b])
        return st_t[:, 2:3]

    def load_w(src3, c0, ncols, wt, wb, dst0=0, kc_n=8, dst_view=None):
        v = dst_view if dst_view is not None else wt[:].rearrange("p (k c) -> p k c", k=8)
        R.dma("pool", CALL("dma_start", out=v[:, 0:kc_n, dst0:dst0 + ncols], in_=src3[:, 0:kc_n, c0:c0 + ncols]),
              writes=[wb])

    def norm_transpose(src_tile, src_b, rows, gain_t, dst, c0, rot):
        jt, jb = junk_p.next()
        st, stb = st_p.next()
        R.op("act", CALL("activation", out=jt[0:rows, :], in_=src_tile[0:rows, :], func=AF.Square, scale=1.0 / 32.0,
                                           accum_out=st[0:rows, 0:1]),
             reads=src_b, writes=[jb, stb])
        R.op("act", CALL("activation", out=st[0:rows, 1:2], in_=st[0:rows, 0:1], func=AF.Ln, bias=RMS_EPS, scale=1.0),
             reads=[stb], writes=[stb])
        R.op("act", CALL("activation", out=st[0:rows, 2:3], in_=st[0:rows, 1:2], func=AF.Exp, scale=-0.5),
             reads=[stb], writes=[stb])
        xt, xb = xn_p.next()
        R.op("dve", CALL("tensor_scalar", out=xt[0:rows, :], in0=src_tile[0:rows, :], scalar1=st[0:rows, 2:3],
                                              scalar2=None, op0=ALU.mult),
             reads=list(src_b) + [stb], writes=[xb])
        bank, bb = rot.next()
        pbf = bank[:].bitcast(BF16)
        for k in range(8):
            R.op("pe", CALL("transpose", pbf[:, k * 128:k * 128 + rows], xt[0:rows, k * 128:(k + 1) * 128],
                                                  ident[0:rows, 0:rows]),
                 reads=[xb, cB], writes=[bb])
        pv = pbf.rearrange("p (k t) -> p k t", k=8)
        gb = gain_t[:, 0:8].unsqueeze(2).to_broadcast([128, 8, rows])
        R.op("dve", CALL("tensor_tensor", out=dst.v[:, 0:8, c0:c0 + rows], in0=pv[:, :, 0:rows], in1=gb, op=ALU.mult),
             reads=[bb, cB], writes=dst.bufs(0, 8, c0, c0 + rows))

    def rope_evac(bank, bb, n, tok0, dsts, scale, rot2):
        at, ab = ra_p.next()
        bt, btb = rb_p.next()
        R.op("dve", CALL("tensor_tensor", out=at[:, 0:n], in0=bank[:, 0:n], in1=cos_t[:, tok0:tok0 + n], op=ALU.mult),
             reads=[bb, cB], writes=[ab])
        R.op("dve", CALL("tensor_tensor", out=bt[:, 0:n], in0=bank[:, 0:n], in1=sin_t[:, tok0:tok0 + n], op=ALU.mult),
             reads=[bb, cB], writes=[btb])

        def stage2():
            b2, b2b = rot2.next()
            R.op("pe", CALL("matmul", b2[:, 0:n], lhsT=ident, rhs=at[:, 0:n], start=True, stop=False),
                 reads=[ab, cB], writes=[b2b])
            R.op("pe", CALL("matmul", b2[:, 0:n], lhsT=prot, rhs=bt[:, 0:n], start=False, stop=True),
                 reads=[btb, cB], writes=[b2b])
            for (p0, p1, dst_ap, dst_bufs) in dsts:
                R.op("act", CALL("activation", out=dst_ap, in_=b2[p0:p1, 0:n], func=AF.Copy, scale=scale),
                     reads=[b2b], writes=dst_bufs)
        return stage2

    TT5 = [(0, 512), (512, 512), (1024, 512), (1536, 512), (2048, 16)]

    def proj_fm(wt, wb, wcol, uT, n_tiles, rot, consume):
        wv = wt[:].rearrange("p (k c) -> p k c", k=8)
        for (tok0, n) in TT5[:n_tiles]:
            bank, bb = rot.next()
            for k in range(8):
                R.op("pe", CALL("matmul",
                    bank[:, 0:n], lhsT=wv[:, k, wcol:wcol + 128], rhs=uT.v[:, k, tok0:tok0 + n],
                    start=(k == 0), stop=(k == 7)),
                    reads=[wb] + uT.bufs(k, k + 1, tok0, tok0 + n), writes=[bb])
            consume(bank, bb, tok0, n)

    pending = []

    def flush_pending(keep=0):
        while len(pending) > keep:
            pending.pop(0)()

    for s in range(nseq):
        uT = AT(A0, 8, TOK)
        qaT = AT(B0, 8, SEQ)
        mT = AT(B0, 8, SEQ)
        kaT = AT(C0, 4, TOK)
        va = AT(C0 + 4 * TOK, 17, 256)
        oT = AT(D0, 8, SEQ)

        rot1 = Rot([0, 1])
        for tb in range(17):
            rows = 128 if tb < 16 else NMETA
            xt, xb = xs_p.next()
            src = x_d[s, tb * 128:(tb + 1) * 128, :] if tb < 16 else meta_d
            R.dma("sp", CALL("dma_start", out=xt[0:rows, :], in_=src), writes=[xb])
            norm_transpose(xt, [xb], rows, g1_t, uT, tb * 128, rot1)
        if s == 0:
            dbg_dump("uT", uT.v[:, :, :], uT.bufs(0, 8, 0, TOK), [128, 8, TOK])
        if stop_after <= 1:
            continue

        rotP = Rot([2, 3, 4, 5])
        rotR = Rot([6, 7])
        wq0, wq0b = w_p.next()
        load_w(w_in_d, QA0, 512, wq0, wq0b)
        wq1, wq1b = w_p.next()
        load_w(w_in_d, QA0 + 512, 512, wq1, wq1b)
        wk, wkb = w_p.next()
        wk5 = wk[:].rearrange("p (k g d c) -> p k g d c", k=8, g=4, d=2)
        for kc in range(8):
            for dd in range(2):
                R.dma("pool", CALL("dma_start",
                    out=wk5[:, kc, :, dd, :],
                    in_=w_in_d[:, kc, KA0:KA0 + 256].rearrange("p (g c) -> p g c", g=4)), writes=[wkb])
        wv_, wvb = w_p.next()
        load_w(w_in_d, VA0, 256, wv_, wvb)

        def consume_rope(dst, r, scale):
            def f(bank, bb, tok0, n):
                pending.append(rope_evac(bank, bb, n, tok0, [(0, 128, dst.v[:, r, tok0:tok0 + n], dst.bufs(r, r + 1, tok0, tok0 + n))],
                                         scale, rotR))
                flush_pending(keep=1)
            return f

        def consume_rope_qz(qz):
            def f(bank, bb, tok0, n):
                dsts = [(0, 64, qz.v[0:64, 0, tok0:tok0 + n], qz.bufs(0, 1, tok0, tok0 + n)),
                        (64, 128, qz.v[64:128, 1, tok0:tok0 + n], qz.bufs(1, 2, tok0, tok0 + n))]
                pending.append(rope_evac(bank, bb, n, tok0, dsts, 0.125, rotR))
                flush_pending(keep=1)
            return f

        for c in range(8):
            wt, wb = (wq0, wq0b) if c < 4 else (wq1, wq1b)
            proj_fm(wt, wb, (c % 4) * 128, uT, 4, rotP, consume_rope(qaT, c, 0.125))
        for g in range(4):
            proj_fm(wk, wkb, g * 128, uT, 5, rotP, consume_rope(kaT, g, 1.0))
        flush_pending()
        wvv = wv_[:].rearrange("p (k c) -> p k c", k=8)
        for tb in range(17):
            rows = 128 if tb < 16 else NMETA
            bank, bb = rotP.next()
            for k in range(8):
                R.op("pe", CALL("matmul",
                    bank[0:rows, 0:256], lhsT=uT.v[:, k, tb * 128:tb * 128 + rows], rhs=wvv[:, k, 0:256],
                    start=(k == 0), stop=(k == 7)),
                    reads=[wvb] + uT.bufs(k, k + 1, tb * 128, tb * 128 + rows), writes=[bb])
            R.op("act", CALL("activation", out=va.v[0:rows, tb, :], in_=bank[0:rows, 0:256],
                                                                          func=AF.Copy),
                 reads=[bb], writes=va.bufs(tb, tb + 1, 0, 256))
        if s == 0:
            dbg_dump("qaT", qaT.v[:, :, :], qaT.bufs(0, 8, 0, SEQ), [128, 8, SEQ])
            dbg_dump("kaT", kaT.v[:, :, :], kaT.bufs(0, 4, 0, TOK), [128, 4, TOK])
            dbg_dump("va", va.v[:, 0:16, :], va.bufs(0, 16, 0, 256), [128, 16, 256])
        if stop_after <= 2:
            continue

        rotS = Rot([0, 1, 2, 3, 4, 5])
        rotO = Rot([6, 7])
        items = []
        for g in range(4):
            for i in range(16):
                kbs = []
                if i > 0:
                    kbs.append((i - 1, 0))
                kbs.append((i, None))
                if i < 15:
                    kbs.append((i + 1, 1))
                kbs.append((16, None))
                for n_, (kb, mk) in enumerate(kbs):
                    items.append((g, i, kb, mk, n_ == 0, n_ == len(kbs) - 1))
        state = {}

        def swa_a(it):
            g, i, kb, mk, first, last = it
            nk = 128 if kb < 16 else NMETA
            k0 = kb * 128
            bE, bEb = rotS.next()
            bO, bOb = rotS.next()
            q0, q1 = i * 128, (i + 1) * 128
            R.op("pe", CALL("matmul", bE[0:nk, 0:256], lhsT=kaT.v[0:64, g, k0:k0 + nk],
                                          rhs=qaT.v[0:64, 2 * g:2 * g + 2, q0:q1], start=True, stop=(mk is None)),
                 reads=kaT.bufs(g, g + 1, k0, k0 + nk) + qaT.bufs(2 * g, 2 * g + 2, q0, q1), writes=[bEb])
            R.op("pe", CALL("matmul", bO[0:nk, 0:256], lhsT=kaT.v[64:128, g, k0:k0 + nk],
                                          rhs=qaT.v[64:128, 2 * g:2 * g + 2, q0:q1], start=True, stop=(mk is None)),
                 reads=kaT.bufs(g, g + 1, k0, k0 + nk) + qaT.bufs(2 * g, 2 * g + 2, q0, q1), writes=[bOb])
            if mk is not None:
                R.op("pe", CALL("matmul", bE[:, 0:256], lhsT=ident, rhs=mask_t[:, mk * 512:mk * 512 + 256],
                                              start=False, stop=True), reads=[cB], writes=[bEb])
                R.op("pe", CALL("matmul", bO[:, 0:256], lhsT=ident, rhs=mask_t[:, mk * 512:mk * 512 + 256],
                                              start=False, stop=True), reads=[cB], writes=[bOb])
            et, eb = e_p.next()
            R.op("act", CALL("activation", out=et[0:nk, 0:256], in_=bE[0:nk, 0:256], func=AF.Exp), reads=[bEb], writes=[eb])
            R.op("act", CALL("activation", out=et[0:nk, 256:512], in_=bO[0:nk, 0:256], func=AF.Exp), reads=[bOb], writes=[eb])
            state[it] = (et, eb, nk)

        import os
        SWA_DBG = int(os.environ.get("SWA_DBG", "9"))

        def swa_b(it):
            g, i, kb, mk, first, last = it
            et, eb, nk = state.pop(it)
            if SWA_DBG <= 1:
                return
            if first:
                state["acc"] = rotO.next()
            acc, accb = state["acc"]
            vrd = va.bufs(kb, kb + 1, g * 64, (g + 1) * 64)
            lv = va.v[0:nk, kb, g * 64:(g + 1) * 64]
            R.op("pe", CALL("matmul", acc[0:64, 0:256], lhsT=lv, rhs=et[0:nk, 0:256], start=first, stop=False),
                 reads=vrd + [eb], writes=[accb])
            R.op("pe", CALL("matmul", acc[64:128, 0:256], lhsT=lv, rhs=et[0:nk, 256:512], start=first, stop=False),
                 reads=vrd + [eb], writes=[accb])
            R.op("pe", CALL("matmul", acc[0:64, 256:512], lhsT=ones[0:nk, 0:64], rhs=et[0:nk, 0:256], start=False,
                                          stop=last), reads=[cB, eb], writes=[accb])
            R.op("pe", CALL("matmul", acc[64:128, 256:512], lhsT=ones[0:nk, 0:64], rhs=et[0:nk, 256:512], start=False,
                                          stop=last), reads=[cB, eb], writes=[accb])
            if last and SWA_DBG > 2:
                dt_, db_ = f_p.next()
                esb = es_t[:, 2 * g:2 * g + 2].unsqueeze(2).to_broadcast([128, 2, 128])
                d3 = dt_[:, 0:256].rearrange("p (j q) -> p j q", j=2)
                R.op("dve", CALL("tensor_tensor", out=d3, in0=acc[:, 256:512].rearrange("p (j q) -> p j q", j=2),
                                                      in1=esb, op=ALU.add), reads=[accb, cB], writes=[db_])
                R.op("dve", CALL("reciprocal", out=dt_[:, 256:512], in_=dt_[:, 0:256]), reads=[db_], writes=[db_])
                q0, q1 = i * 128, (i + 1) * 128
                R.op("dve", CALL("tensor_tensor", out=oT.v[:, 2 * g:2 * g + 2, q0:q1],
                                                      in0=acc[:, 0:256].rearrange("p (j q) -> p j q", j=2),
                                                      in1=dt_[:, 256:512].rearrange("p (j q) -> p j q", j=2), op=ALU.mult),
                     reads=[accb, db_], writes=oT.bufs(2 * g, 2 * g + 2, q0, q1))

        DEPTH = 2
        for n_ in range(len(items) + DEPTH):
            if n_ < len(items):
                swa_a(items[n_])
            if n_ - DEPTH >= 0:
                swa_b(items[n_ - DEPTH])
        if s == 0:
            dbg_dump("oswaT", oT.v[:, :, :], oT.bufs(0, 8, 0, SEQ), [128, 8, SEQ])
        if stop_after <= 3:
            continue

        def merge_phase(w_br_d, gcol0, bcol0, accumulate):
            rotM = Rot([0, 1, 2, 3, 4, 5, 6, 7])
            slots = []
            for half in range(2):
                wa, wab = w_p.next()
                load_w(w_br_d, half * 512, 512, wa, wab)
                wg, wgb = w_p.next()
                load_w(w_in_d, gcol0 + half * 512, 512, wg, wgb)
                slots.append((wa, wab, wg, wgb))
            for m in range(8):
                wa, wab, wg, wgb = slots[m // 4]
                wav = wa[:].rearrange("p (k c) -> p k c", k=8)
                wgv = wg[:].rearrange("p (k c) -> p k c", k=8)
                mc = (m % 4) * 128
                for t in range(4):
                    t0 = t * 512
                    bp, bpb = rotM.next()
                    bg, bgb = rotM.next()
                    for k in range(8):
                        R.op("pe", CALL("matmul", bp[:, :], lhsT=wav[:, k, mc:mc + 128],
                                                                    rhs=oT.v[:, k, t0:t0 + 512], start=(k == 0), stop=(k == 7)),
                             reads=[wab] + oT.bufs(k, k + 1, t0, t0 + 512), writes=[bpb])
                    for k in range(8):
                        R.op("pe", CALL("matmul", bg[:, :], lhsT=wgv[:, k, mc:mc + 128],
                                                                    rhs=uT.v[:, k, t0:t0 + 512], start=(k == 0), stop=(k == 7)),
                             reads=[wgb] + uT.bufs(k, k + 1, t0, t0 + 512), writes=[bgb])
                    gt_, gtb = f_p.next()
                    R.op("act", CALL("activation", out=gt_[:, :], in_=bg[:, :], func=AF.Sigmoid,
                                                                      bias=bg_t[:, bcol0 + m:bcol0 + m + 1], scale=1.0),
                         reads=[bgb, cB], writes=[gtb])
                    mb = mT.bufs(m, m + 1, t0, t0 + 512)
                    if not accumulate:
                        R.op("dve", CALL("tensor_tensor", out=mT.v[:, m, t0:t0 + 512], in0=bp[:, :],
                                                                             in1=gt_[:, :], op=ALU.mult),
                             reads=[bpb, gtb], writes=mb)
                    else:
                        R.op("dve", CALL("tensor_tensor", out=gt_[:, :], in0=bp[:, :], in1=gt_[:, :],
                                                                             op=ALU.mult), reads=[bpb, gtb], writes=[gtb])
                        R.op("dve", CALL("tensor_tensor", out=mT.v[:, m, t0:t0 + 512], in0=mT.v[:, m, t0:t0 + 512],
                                                                      in1=gt_[:, :], op=ALU.add), reads=[gtb] + mb, writes=mb)

        merge_phase(w_bs_d, G0, 0, False)
        if s == 0:
            dbg_dump("m1T", mT.v[:, :, :], mT.bufs(0, 8, 0, SEQ), [128, 8, SEQ])
        if stop_after <= 4:
            continue

        rotP = Rot([0, 1, 2])
        rotR = Rot([3])
        deferred = []
        for h in range(8):
            base = C0
            qz = AT(base, 2, SEQ)
            kb_ = AT(base + 2 * SEQ, 1, TOK)
            vb = AT(base + 2 * SEQ + TOK, 17, 256)
            hv = (h % 2) * 128
            if h == 0:
                R.op("dve", CALL("memset", qz.v[64:128, 0, :], 0.0), writes=qz.bufs(0, 1, 0, SEQ))
                R.op("dve", CALL("memset", qz.v[0:64, 1, :], 0.0), writes=qz.bufs(1, 2, 0, SEQ))
            wt, wb = w_p.next()
            wv3 = wt[:].rearrange("p (k c) -> p k c", k=8)
            load_w(w_in_d, QB0 + h * 128, 128, wt, wb, dst0=0)
            load_w(w_in_d, KB0 + h * 128, 128, wt, wb, dst0=128)
            if h % 2 == 0:
                load_w(w_in_d, VB0 + h * 128, 256, wt, wb, dst0=256)
            rotP = Rot([0, 1, 2])
            rotR = Rot([7])
            proj_fm(wt, wb, 0, uT, 4, rotP, consume_rope_qz(qz))
            proj_fm(wt, wb, 128, uT, 5, rotP, consume_rope(kb_, 0, 1.0))
            flush_pending()
            while deferred:
                deferred.pop(0)[1]()
            for tb in (range(17) if h % 2 == 0 else ()):
                rows = 128 if tb < 16 else NMETA
                bank, bb = rotP.next()
                for k in range(8):
                    R.op("pe", CALL("matmul",
                        bank[0:rows, 0:256], lhsT=uT.v[:, k, tb * 128:tb * 128 + rows], rhs=wv3[:, k, 256:512],
                        start=(k == 0), stop=(k == 7)),
                        reads=[wb] + uT.bufs(k, k + 1, tb * 128, tb * 128 + rows), writes=[bb])
                R.op("dve", CALL("tensor_copy", out=vb.v[0:rows, tb, :], in_=bank[0:rows, 0:256]),
                     reads=[bb], writes=vb.bufs(tb, tb + 1, 0, 256))
            if s == 0 and h == 0:
                dbg_dump("qz0", qz.v[:, :, :], qz.bufs(0, 2, 0, SEQ), [128, 2, SEQ])
                dbg_dump("kb0", kb_.v[:, :, :], kb_.bufs(0, 1, 0, TOK), [128, 1, TOK])
                dbg_dump("vb0", vb.v[:, 0:16, 0:128], vb.bufs(0, 16, 0, 128), [128, 16, 128])
            rotS = Rot([0, 1, 2, 3])
            accs = {0: (4, 5), 1: (6, 7)}
            ditems = [(qt, c, kbi) for qt in range(4) for c in range(2) for kbi in range(17)]
            dstate = {}

            def dif_a(it):
                qt, c, kbi = it
                nk = 128 if kbi < 16 else NMETA
                k0 = kbi * 128
                bank, bb = rotS.next()
                R.op("pe", CALL("matmul", bank[0:nk, :], lhsT=kb_.v[:, 0, k0:k0 + nk],
                                              rhs=qz.v[:, c, qt * 512:(qt + 1) * 512], start=True, stop=True),
                     reads=kb_.bufs(0, 1, k0, k0 + nk) + qz.bufs(c, c + 1, qt * 512, (qt + 1) * 512), writes=[bb])
                et, eb = e_p.next()
                R.op("act", CALL("activation", out=et[0:nk, :], in_=bank[0:nk, :], func=AF.Exp), reads=[bb], writes=[eb])
                dstate[it] = (et, eb, nk)

            def dif_b(it):
                qt, c, kbi = it
                et, eb, nk = dstate.pop(it)
                first, last = kbi == 0, kbi == 16
                oi, si = accs[c]
                R.op("pe", CALL("matmul", psum[oi][:, :], lhsT=vb.v[0:nk, kbi, hv:hv + 128], rhs=et[0:nk, :], start=first, stop=last),
                     reads=vb.bufs(kbi, kbi + 1, hv, hv + 128) + [eb], writes=[psb[oi]])
                R.op("pe", CALL("matmul", psum[si][:, :], lhsT=ones[0:nk, :], rhs=et[0:nk, :], start=first, stop=last),
                     reads=[cB, eb], writes=[psb[si]])
                if not last:
                    return
                rt, rb = f_p.next()
                R.op("dve", CALL("reciprocal", out=rt[:, :], in_=psum[si][:, :]), reads=[psb[si]], writes=[rb])
                R.op("dve", CALL("tensor_tensor", out=rt[:, :], in0=psum[oi][:, :], in1=rt[:, :], op=ALU.mult),
                     reads=[psb[oi], rb], writes=[rb])
                if c == 0:
                    dstate["t0"] = (rt, rb)
                    return
                t0t, t0b = dstate.pop("t0")
                R.op("dve", CALL("scalar_tensor_tensor", out=rt[:, :], in0=rt[:, :], scalar=nlam[:, 0:1], in1=t0t[:, :],
                                                             op0=ALU.mult, op1=ALU.add), reads=[rb, t0b, cB], writes=[rb])
                sq, sqb = sq_p.next()
                R.op("dve", CALL("tensor_tensor", out=sq[:, :], in0=rt[:, :], in1=rt[:, :], op=ALU.mult), reads=[rb], writes=[sqb])

                def tail(qt=qt, rt=rt, rb=rb, t0t=t0t, t0b=t0b, sq=sq, sqb=sqb, h=h):
                    ssk, ssb = rotS.next()
                    R.op("pe", CALL("matmul", ssk[:, :], lhsT=onesd, rhs=sq[:, :], start=True, stop=True),
                         reads=[cB, sqb], writes=[ssb])
                    R.op("act", CALL("activation", out=t0t[:, :], in_=ssk[:, :], func=AF.Ln, bias=RMS_EPS, scale=1.0),
                         reads=[ssb], writes=[t0b])
                    R.op("act", CALL("activation", out=t0t[:, :], in_=t0t[:, :], func=AF.Exp, scale=-0.5),
                         reads=[t0b], writes=[t0b])
                    R.op("dve", CALL("scalar_tensor_tensor", out=oT.v[:, h, qt * 512:(qt + 1) * 512], in0=rt[:, :],
                                                                 scalar=g08[:, 0:1], in1=t0t[:, :], op0=ALU.mult, op1=ALU.mult),
                         reads=[rb, t0b, cB], writes=oT.bufs(h, h + 1, qt * 512, (qt + 1) * 512))
                deferred.append([10, tail])

            DD = 3
            for n_ in range(len(ditems) + DD):
                if n_ < len(ditems):
                    dif_a(ditems[n_])
                if n_ - DD >= 0:
                    dif_b(ditems[n_ - DD])
                for dfr in deferred:
                    dfr[0] -= 1
                while deferred and deferred[0][0] <= 0:
                    deferred.pop(0)[1]()
        while deferred:
            deferred.pop(0)[1]()
        if s == 0:
            dbg_dump("odiffT", oT.v[:, :, :], oT.bufs(0, 8, 0, SEQ), [128, 8, SEQ])
        if stop_after <= 5:
            continue
        merge_phase(w_bd_d, G0 + 1024, 8, True)
        if s == 0:
            dbg_dump("mT", mT.v[:, :, :], mT.bufs(0, 8, 0, SEQ), [128, 8, SEQ])
        if stop_after <= 6:
            continue

        for hf in range(2):
            hT = AT(A0, 8, D, F32)
            u2T = AT(C0, 8, 1024)
            aT = AT(C0 + 8192, NJ, 1024)
            fst = AT(C0, 8, 512, F32)
            rotY = Rot([0, 1, 2, 3])
            rotT = Rot([4, 5])
            wo0, wo0b = w_p.next()
            load_w(w_out_d, 0, 512, wo0, wo0b)
            wo1, wo1b = w_p.next()
            load_w(w_out_d, 512, 512, wo1, wo1b)
            wov = [wo0[:].rearrange("p (k c) -> p k c", k=8), wo1[:].rearrange("p (k c) -> p k c", k=8)]
            wob = [wo0b, wo1b]
            nt_q = []
            for tl in range(8):
                tb = hf * 8 + tl
                q0 = tb * 128
                ybank = []
                for ch in range(2):
                    bank, bb = rotY.next()
                    for k in range(8):
                        R.op("pe", CALL("matmul", bank[:, :], lhsT=mT.v[:, k, q0:q0 + 128],
                                                                             rhs=wov[ch][:, k, :], start=(k == 0), stop=(k == 7)),
                             reads=[wob[ch]] + mT.bufs(k, k + 1, q0, q0 + 128), writes=[bb])
                    ybank.append((bank, bb))
                if len(nt_q) >= 2:
                    nt_q.pop(0)()
                st, stb = st_p.next()
                for ch in range(2):
                    jt, jb = junk_p.next()
                    R.op("act", CALL("activation", out=jt[:, 0:512], in_=ybank[ch][0][:, :], func=AF.Square,
                                                                    scale=1.0 / 32.0, accum_out=st[:, ch:ch + 1]),
                         reads=[ybank[ch][1]], writes=[jb, stb])
                R.op("dve", CALL("tensor_tensor", out=st[:, 0:1], in0=st[:, 0:1], in1=st[:, 1:2], op=ALU.add),
                     reads=[stb], writes=[stb])
                R.op("act", CALL("activation", out=st[:, 1:2], in_=st[:, 0:1], func=AF.Ln, bias=RMS_EPS, scale=1.0),
                     reads=[stb], writes=[stb])
                R.op("act", CALL("activation", out=st[:, 2:3], in_=st[:, 1:2], func=AF.Exp, scale=-0.5),
                     reads=[stb], writes=[stb])
                xt, xb = xs_p.next()
                R.dma("sp", CALL("dma_start", out=xt[:, :], in_=x_d[s, q0:q0 + 128, :]), writes=[xb])
                for ch in range(2):
                    c0 = ch * 512
                    tt, ttb = f_p.next()
                    R.op("dve", CALL("scalar_tensor_tensor",
                        out=tt[:, :], in0=ybank[ch][0][:, :], scalar=st[:, 2:3], in1=g2_t[:, c0:c0 + 512],
                        op0=ALU.mult, op1=ALU.mult), reads=[ybank[ch][1], stb, cB], writes=[ttb])
                    R.op("dve", CALL("tensor_tensor", out=hT.v[:, tl, c0:c0 + 512], in0=tt[:, :],
                                                                       in1=xt[:, c0:c0 + 512], op=ALU.add),
                         reads=[ttb, xb], writes=hT.bufs(tl, tl + 1, c0, c0 + 512))
                nt_q.append(lambda tl=tl: norm_transpose(hT.v[:, tl, :], hT.bufs(tl, tl + 1, 0, D), 128, g3_t, u2T, tl * 128, rotT))
            while nt_q:
                nt_q.pop(0)()
            if s == 0 and hf == 0:
                dbg_dump("h0", hT.v[:, :, :], hT.bufs(0, 8, 0, D), [128, 8, D], F32)
                dbg_dump("u2T0", u2T.v[:, :, :], u2T.bufs(0, 8, 0, 1024), [128, 8, 1024])
            if stop_after <= 7:
                continue
            rotF = Rot([0, 1, 2, 3, 4, 5, 6, 7])
            wcur = None
            for j in range(NJ):
                if j % 2 == 0:
                    wcur = w_p.next()
                    for jj in range(2):
                        if j + jj < NJ:
                            load_w(w_fi_d, (j + jj) * 128, 128, wcur[0], wcur[1], dst0=jj * 256)
                            load_w(w_fi_d, DFF + (j + jj) * 128, 128, wcur[0], wcur[1], dst0=jj * 256 + 128)
                wt, wb = wcur
                wv3 = wt[:].rearrange("p (k c) -> p k c", k=8)
                cg = (j % 2) * 256
                for t in range(2):
                    t0 = t * 512
                    bgk, bgb = rotF.next()
                    buk, bub = rotF.next()
                    for k in range(8):
                        R.op("pe", CALL("matmul", bgk[:, :], lhsT=wv3[:, k, cg:cg + 128],
                                                                      rhs=u2T.v[:, k, t0:t0 + 512], start=(k == 0), stop=(k == 7)),
                             reads=[wb] + u2T.bufs(k, k + 1, t0, t0 + 512), writes=[bgb])
                    for k in range(8):
                        R.op("pe", CALL("matmul", buk[:, :], lhsT=wv3[:, k, cg + 128:cg + 256],
                                                                      rhs=u2T.v[:, k, t0:t0 + 512], start=(k == 0), stop=(k == 7)),
                             reads=[wb] + u2T.bufs(k, k + 1, t0, t0 + 512), writes=[bub])
                    sg, sgb = f_p.next()
                    R.op("act", CALL("activation", out=sg[:, :], in_=bgk[:, :], func=AF.Silu),
                         reads=[bgb], writes=[sgb])
                    R.op("dve", CALL("tensor_tensor", out=aT.v[:, j, t0:t0 + 512], in0=buk[:, :],
                                                                                     in1=sg[:, :], op=ALU.mult),
                         reads=[bub, sgb], writes=aT.bufs(j, j + 1, t0, t0 + 512))
            if s == 0 and hf == 0:
                dbg_dump("aT0", aT.v[:, :, :], aT.bufs(0, NJ, 0, 1024), [128, NJ, 1024])
            if stop_after <= 8:
                continue
            rotO2 = Rot([0, 1, 2, 3])
            ms_t, msb = msffn_t, msffn_b
            for ch in range(2):
                wsl = []
                for (j0, j1) in ((0, 8), (8, 16), (16, 22)):
                    wt, wb = w_p.next()
                    wv3 = wt[:].rearrange("p (k c) -> p k c", k=8)
                    R.dma("pool", CALL("dma_start", out=wv3[:, 0:j1 - j0, :],
                                                                              in_=w_fo_d[:, j0:j1, ch * 512:(ch + 1) * 512]),
                          writes=[wb])
                    wsl.append((wv3, wb, j0, j1))
                for tl in range(8):
                    tb = hf * 8 + tl
                    bank, bb = rotO2.next()
                    for (wv3, wb, j0, j1) in wsl:
                        for j in range(j0, j1):
                            R.op("pe", CALL("matmul",
                                bank[:, :], lhsT=aT.v[:, j, tl * 128:(tl + 1) * 128], rhs=wv3[:, j - j0, :],
                                start=(j == 0), stop=(j == NJ - 1)),
                                reads=[wb] + aT.bufs(j, j + 1, tl * 128, (tl + 1) * 128), writes=[bb])
                    jt, jb = junk_p.next()
                    R.op("act", CALL("activation",
                        out=jt[:, 0:512], in_=bank[:, :], func=AF.Square, scale=1.0 / 32.0, accum_out=ms_t[:, tl, ch:ch + 1]),
                        reads=[bb], writes=[jb, msb])
                    if ch == 0:
                        R.op("act", CALL("activation", out=fst.v[:, tl, :], in_=bank[:, :], func=AF.Copy),
                             reads=[bb], writes=fst.bufs(tl, tl + 1, 0, 512))
                        continue
                    R.op("dve", CALL("tensor_tensor", out=ms_t[:, tl, 0:1], in0=ms_t[:, tl, 0:1], in1=ms_t[:, tl, 1:2],
                                                                 op=ALU.add), reads=[msb], writes=[msb])
                    R.op("act", CALL("activation", out=ms_t[:, tl, 2:3], in_=ms_t[:, tl, 0:1], func=AF.Ln, bias=RMS_EPS,
                                                              scale=1.0), reads=[msb], writes=[msb])
                    R.op("act", CALL("activation", out=ms_t[:, tl, 3:4], in_=ms_t[:, tl, 2:3], func=AF.Exp, scale=-0.5),
                         reads=[msb], writes=[msb])
                    ot, ob = xs_p.next()
                    for c2 in range(2):
                        c0 = c2 * 512
                        src = fst.v[:, tl, :] if c2 == 0 else bank[:, :]
                        srcb = fst.bufs(tl, tl + 1, 0, 512) if c2 == 0 else [bb]
                        tt, ttb = f_p.next()
                        R.op("dve", CALL("scalar_tensor_tensor",
                            out=tt[:, :], in0=src, scalar=ms_t[:, tl, 3:4], in1=g4_t[:, c0:c0 + 512], op0=ALU.mult,
                            op1=ALU.mult), reads=srcb + [msb, cB], writes=[ttb])
                        R.op("dve", CALL("tensor_tensor",
                            out=ot[:, c0:c0 + 512], in0=tt[:, :], in1=hT.v[:, tl, c0:c0 + 512], op=ALU.add),
                            reads=[ttb] + hT.bufs(tl, tl + 1, c0, c0 + 512), writes=[ob])
                    R.dma("sp", CALL("dma_start", out=out_d[s, tb * 128:(tb + 1) * 128, :], in_=ot[:, :]),
                          reads=[ob], writes=[Buf("outsink")])

    fin = Buf("fin")
    tail_deps = [ins for ins in R.streams["sp"] if ins.is_dma]
    fi = R.op("sp", None, reads=[], writes=[fin])
    for d in tail_deps:
        fi.deps.append(d)
        d.needs_inc = True
    R.finalize_and_emit()
    return nc, R


N_CORES = 8
_PROG_CACHE = {}


def make_in_maps(inputs, nseq=2, n_cores=N_CORES):
    f = lambda a: np.ascontiguousarray(np.asarray(a, dtype=np.float32))
    x = f(inputs["x"])
    sink = f(inputs["attn_sink"])[0]
    p = np.arange(128)
    head_of = 2 * np.arange(8)[None, :] + (p[:, None] // 64)
    shared = {
        "meta": f(inputs["meta_tokens"]),
        "w_in": f(inputs["w_in"])[0],
        "w_bs": f(inputs["w_branch_swa"])[0],
        "w_bd": f(inputs["w_branch_diff"])[0],
        "w_out": f(inputs["w_out"])[0],
        "w_fi": f(inputs["w_ffn_in"])[0],
        "w_fo": f(inputs["w_ffn_out"])[0],
        "g1_fm": f(f(inputs["pre_mix_gain"])[0].reshape(8, 128).T),
        "g3_fm": f(f(inputs["pre_ffn_gain"])[0].reshape(8, 128).T),
        "g_post": f(inputs["post_mix_gain"]).reshape(1, D),
        "g_postffn": f(inputs["post_ffn_gain"]).reshape(1, D),
        "bgate_fm": f(f(inputs["b_gate"])[0].reshape(16, 128).T),
        "sink_fm": f(sink[head_of]),
        "lam_p": f(np.stack([f(inputs["lambda_q1"])[0], f(inputs["lambda_k1"])[0],
                             f(inputs["lambda_q2"])[0], f(inputs["lambda_k2"])[0]], 0)),
        "subln": f(f(inputs["diff_subln_gain"])[0].reshape(128, 1)),
    }
    shared.update(host_constants())
    maps = []
    for c in range(n_cores):
        m = dict(shared)
        m["x"] = np.ascontiguousarray(x[c * nseq:(c + 1) * nseq])
        maps.append(m)
    return maps


def kernel(**inputs):
    if "prog" not in _PROG_CACHE:
        _PROG_CACHE["prog"] = build_program(nseq=2)[0]
    nc = _PROG_CACHE["prog"]
    in_maps = make_in_maps(inputs, nseq=2)
    res = run_bass_kernel_spmd(nc, in_maps, core_ids=list(range(N_CORES)))
    out = np.concatenate([np.asarray(r["out"]) for r in res.results], axis=0)
    return out.astype(np.float32, copy=False)
```
